# Optimizing a Trainium2 kernel written in Bass

```python
import math
import jax, jax.numpy as jnp
from jax import lax
import numpy as np

D_MODEL = 1024
BATCH = 8
SEQ = 2048
DEPTH = 1
DEC_BATCH = 32
DEC_SEQ = 8
PAST_LEN = 16384
PAGE_SIZE = 128

H_A = 4
DK_A = 64
DV_A = 128
GATE_RANK = 16
GATE_TAU = 16.0
GLA_CHUNK = 64
H_B = 4
Q_LORA = 384
KV_LORA = 256
QK_NOPE = 128
QK_ROPE = 64
V_B = 128
ROPE_BASE = 10000.0
Q_BLOCK = 128
MIX_WIDTH = H_A * DV_A + H_B * V_B
D_FF = 2816
CONV_W = 3
EPS = 1e-6
COLS = (H_A * DK_A, H_A * DK_A, H_A * DV_A, H_A * DV_A, GATE_RANK, Q_LORA, KV_LORA, QK_ROPE)
IN_COLS = sum(COLS)

kernel_name = "hymba_gla_mla_convffn_step"


def rmsnorm(x, g):
    xf = x.astype(jnp.float32)
    y = xf * lax.rsqrt(jnp.mean(xf * xf, axis=-1, keepdims=True) + EPS)
    return (y * g.astype(jnp.float32)).astype(x.dtype)


def rope(x, pos):
    r = x.shape[-1]
    inv = ROPE_BASE ** (-jnp.arange(0, r, 2, dtype=jnp.float32) / r)
    ang = pos.astype(jnp.float32)[:, None] * inv[None, :]
    cos = jnp.cos(ang)[None, :, None, :]
    sin = jnp.sin(ang)[None, :, None, :]
    xf = x.astype(jnp.float32)
    x1, x2 = xf[..., : r // 2], xf[..., r // 2:]
    out = jnp.concatenate([x1 * cos - x2 * sin, x1 * sin + x2 * cos], axis=-1)
    return out.astype(x.dtype)


def gla_chunked(q, k, v, log_a, s0):
    B, T, H, DK = q.shape
    DV = v.shape[-1]
    C = math.gcd(T, GLA_CHUNK)
    N = T // C

    def to_chunks(a):
        return a.astype(jnp.float32).reshape(B, N, C, H, a.shape[-1]).transpose(1, 0, 2, 3, 4)

    tri = jnp.tril(jnp.ones((C, C), dtype=bool))[None, :, :, None, None]

    def step(S, inp):
        qc, kc, vc, gc = inp
        b = jnp.cumsum(gc, axis=1)
        o_inter = jnp.einsum('bchk,bhkv->bchv', qc * jnp.exp(b), S)
        diff = jnp.where(tri, b[:, :, None] - b[:, None, :], -jnp.inf)
        att = jnp.einsum('bthk,bshk,btshk->bhts', qc, kc, jnp.exp(diff))
        o_intra = jnp.einsum('bhts,bshv->bthv', att, vc)
        bl = b[:, -1]
        S_new = S * jnp.exp(bl)[..., None] + jnp.einsum('bshk,bshv->bhkv', kc * jnp.exp(bl[:, None] - b), vc)
        return S_new, o_inter + o_intra

    S_fin, o = lax.scan(step, s0.astype(jnp.float32), (to_chunks(q), to_chunks(k), to_chunks(v), to_chunks(log_a)))
    o = o.transpose(1, 0, 2, 3, 4).reshape(B, T, H, DV).astype(v.dtype)
    return o, S_fin


def mla_attend(q_lat, q_rope, kv, kr, q_pos, k_pos):
    B, T, H, L = q_lat.shape
    R = q_rope.shape[-1]
    scale = (QK_NOPE + QK_ROPE) ** -0.5

    def block(args):
        ql, qr, qp = args
        s = jnp.einsum('bthl,bsl->bhts', ql, kv) + jnp.einsum('bthr,bsr->bhts', qr, kr)
        s = s.astype(jnp.float32) * scale
        s = jnp.where(k_pos[None, :] <= qp[:, None], s, -jnp.inf)
        p = jax.nn.softmax(s, axis=-1).astype(kv.dtype)
        return jnp.einsum('bhts,bsl->bthl', p, kv)

    blk = Q_BLOCK if T % Q_BLOCK == 0 else T
    nb = T // blk
    if nb == 1:
        return block((q_lat, q_rope, q_pos))
    qlb = q_lat.reshape(B, nb, blk, H, L).transpose(1, 0, 2, 3, 4)
    qrb = q_rope.reshape(B, nb, blk, H, R).transpose(1, 0, 2, 3, 4)
    qpb = q_pos.reshape(nb, blk)
    out = lax.map(block, (qlb, qrb, qpb))
    return out.transpose(1, 0, 2, 3, 4).reshape(B, T, H, L)


def trunk_layer(x, c, pos, gla_s0, conv_buf, past_lat, past_kr, lw):
    B, T, _ = x.shape
    mod = jax.nn.silu(c) @ lw['w_ada'] + lw['b_ada']
    sh1, sc1, gt1, sh2, sc2, gt2 = [m[:, None, :] for m in jnp.split(mod, 6, axis=-1)]

    h = rmsnorm(x, lw['g_pre_mix']) * (1.0 + sc1) + sh1
    z = h @ lw['w_in']
    offs = np.cumsum(COLS)[:-1].tolist()
    q_a, k_a, v_a, r_a, gr_a, cq, ckv, kr = jnp.split(z, offs, axis=-1)

    q_a = q_a.reshape(B, T, H_A, DK_A) * (DK_A ** -0.5)
    k_a = k_a.reshape(B, T, H_A, DK_A)
    v_a = v_a.reshape(B, T, H_A, DV_A)
    log_a = jax.nn.log_sigmoid((gr_a @ lw['w_gla_a2'] + lw['b_gla_a']).astype(jnp.float32)) / GATE_TAU
    log_a = log_a.reshape(B, T, H_A, DK_A)
    o_a, s_a = gla_chunked(q_a, k_a, v_a, log_a, gla_s0)
    o_a = rmsnorm(o_a, lw['g_gla_out']) * jax.nn.silu(r_a.reshape(B, T, H_A, DV_A))
    o_a = o_a.reshape(B, T, H_A * DV_A)

    cq = rmsnorm(cq, lw['g_mla_q'])
    qb = (cq @ lw['w_mla_uq']).reshape(B, T, H_B, QK_NOPE + QK_ROPE)
    q_nope, q_rope = qb[..., :QK_NOPE], rope(qb[..., QK_NOPE:], pos)
    ckv = rmsnorm(ckv, lw['g_mla_kv'])
    kr = rope(kr[:, :, None, :], pos)[:, :, 0, :]
    q_lat = jnp.einsum('bthn,lhn->bthl', q_nope, lw['w_mla_uk'])
    if past_lat is None:
        kv_all, kr_all, k_pos = ckv, kr, pos
    else:
        kv_all = jnp.concatenate([past_lat, ckv], axis=1)
        kr_all = jnp.concatenate([past_kr, kr], axis=1)
        k_pos = jnp.concatenate([jnp.arange(past_lat.shape[1], dtype=jnp.int32), pos])
    o_lat = mla_attend(q_lat, q_rope, kv_all, kr_all, pos, k_pos)
    o_b = jnp.einsum('bthl,lhv->bthv', o_lat, lw['w_mla_uv'])
    o_b = rmsnorm(o_b, lw['g_mla_out']).reshape(B, T, H_B * V_B)

    mix = jnp.concatenate([o_a, o_b], axis=-1) @ lw['w_o']
    x = x + gt1 * rmsnorm(mix, lw['g_post_mix'])

    h2 = rmsnorm(x, lw['g_pre_ffn']) * (1.0 + sc2) + sh2
    u = h2 @ lw['w_ffn_up']
    if conv_buf is None:
        conv_buf = jnp.zeros((B, CONV_W - 1, u.shape[-1]), dtype=u.dtype)
    padded = jnp.concatenate([conv_buf.astype(u.dtype), u], axis=1)
    w_c = lw['w_ffn_conv']
    uc = sum(w_c[j] * padded[:, j:j + T] for j in range(CONV_W)) + lw['b_ffn_conv']
    a_up, g_up = uc[..., :D_FF], uc[..., D_FF:]
    y = (a_up * jax.nn.gelu(g_up)) @ lw['w_ffn_down']
    x = x + gt2 * rmsnorm(y, lw['g_post_ffn'])
    new_conv = padded[:, -(CONV_W - 1):]
    return x, ckv, kr, s_a, new_conv


def setup_inputs(seed: int = 0) -> dict:
    key = jax.random.key(seed)
    ks = jax.random.split(key, 40)
    f32 = jnp.float32
    n_pages = PAST_LEN // PAGE_SIZE
    n_used = DEC_BATCH * n_pages
    n_phys = n_used + n_used // 4

    def nrm(k, shape, scale):
        return jax.random.normal(k, shape, f32) * scale

    def gain(k, shape):
        return 1.0 + 0.1 * jax.random.normal(k, shape, f32)

    page_table = jax.random.permutation(ks[0], n_phys)[:n_used].reshape(DEC_BATCH, n_pages).astype(jnp.int32)
    return {
        'x_prompt': nrm(ks[1], (BATCH, SEQ, D_MODEL), 1.0),
        'x_sample': nrm(ks[2], (DEC_BATCH, DEC_SEQ, D_MODEL), 1.0),
        'c_prompt': nrm(ks[3], (BATCH, D_MODEL), 1.0),
        'c_sample': nrm(ks[4], (DEC_BATCH, D_MODEL), 1.0),
        'cache_latent': nrm(ks[5], (DEPTH, n_phys, PAGE_SIZE, KV_LORA), 1.0),
        'cache_krope': nrm(ks[6], (DEPTH, n_phys, PAGE_SIZE, QK_ROPE), 1.0),
        'state_gla': nrm(ks[7], (DEPTH, DEC_BATCH, H_A, DK_A, DV_A), 0.5),
        'state_conv': nrm(ks[8], (DEPTH, DEC_BATCH, CONV_W - 1, 2 * D_FF), 1.0),
        'page_table': page_table,
        'w_ada': nrm(ks[9], (DEPTH, D_MODEL, 6 * D_MODEL), 0.5 * D_MODEL ** -0.5),
        'b_ada': nrm(ks[10], (DEPTH, 6 * D_MODEL), 0.01),
        'g_pre_mix': gain(ks[11], (DEPTH, D_MODEL)),
        'g_post_mix': gain(ks[12], (DEPTH, D_MODEL)),
        'g_pre_ffn': gain(ks[13], (DEPTH, D_MODEL)),
        'g_post_ffn': gain(ks[14], (DEPTH, D_MODEL)),
        'w_in': nrm(ks[15], (DEPTH, D_MODEL, IN_COLS), D_MODEL ** -0.5),
        'w_gla_a2': nrm(ks[16], (DEPTH, GATE_RANK, H_A * DK_A), GATE_RANK ** -0.5),
        'b_gla_a': nrm(ks[17], (DEPTH, H_A * DK_A), 0.01),
        'g_gla_out': gain(ks[18], (DEPTH, DV_A)),
        'g_mla_q': gain(ks[19], (DEPTH, Q_LORA)),
        'g_mla_kv': gain(ks[20], (DEPTH, KV_LORA)),
        'w_mla_uq': nrm(ks[21], (DEPTH, Q_LORA, H_B * (QK_NOPE + QK_ROPE)), Q_LORA ** -0.5),
        'w_mla_uk': nrm(ks[22], (DEPTH, KV_LORA, H_B, QK_NOPE), KV_LORA ** -0.5),
        'w_mla_uv': nrm(ks[23], (DEPTH, KV_LORA, H_B, V_B), KV_LORA ** -0.5),
        'g_mla_out': gain(ks[24], (DEPTH, V_B)),
        'w_o': nrm(ks[25], (DEPTH, MIX_WIDTH, D_MODEL), MIX_WIDTH ** -0.5),
        'w_ffn_up': nrm(ks[26], (DEPTH, D_MODEL, 2 * D_FF), D_MODEL ** -0.5),
        'w_ffn_conv': nrm(ks[27], (DEPTH, CONV_W, 2 * D_FF), CONV_W ** -0.5),
        'b_ffn_conv': nrm(ks[28], (DEPTH, 2 * D_FF), 0.01),
        'w_ffn_down': nrm(ks[29], (DEPTH, D_FF, D_MODEL), D_FF ** -0.5),
    }


def reference(x_prompt, x_sample, c_prompt, c_sample, cache_latent, cache_krope, state_gla, state_conv,
              page_table, w_ada, b_ada, g_pre_mix, g_post_mix, g_pre_ffn, g_post_ffn, w_in, w_gla_a2,
              b_gla_a, g_gla_out, g_mla_q, g_mla_kv, w_mla_uq, w_mla_uk, w_mla_uv, g_mla_out, w_o,
              w_ffn_up, w_ffn_conv, b_ffn_conv, w_ffn_down):
    B, T = x_prompt.shape[0], x_prompt.shape[1]
    Bd, Td = x_sample.shape[0], x_sample.shape[1]
    n_pages = page_table.shape[1]
    past_len = n_pages * cache_latent.shape[2]
    pos_p = jnp.arange(T, dtype=jnp.int32)
    pos_s = past_len + jnp.arange(Td, dtype=jnp.int32)

    hp, hs = x_prompt, x_sample
    lat_p, kr_p, gla_p, conv_p = [], [], [], []
    lat_s, kr_s, gla_s, conv_s = [], [], [], []
    for l in range(DEPTH):
        lw = dict(w_ada=w_ada[l], b_ada=b_ada[l], g_pre_mix=g_pre_mix[l], g_post_mix=g_post_mix[l],
                  g_pre_ffn=g_pre_ffn[l], g_post_ffn=g_post_ffn[l], w_in=w_in[l], w_gla_a2=w_gla_a2[l],
                  b_gla_a=b_gla_a[l], g_gla_out=g_gla_out[l], g_mla_q=g_mla_q[l], g_mla_kv=g_mla_kv[l],
                  w_mla_uq=w_mla_uq[l], w_mla_uk=w_mla_uk[l], w_mla_uv=w_mla_uv[l], g_mla_out=g_mla_out[l],
                  w_o=w_o[l], w_ffn_up=w_ffn_up[l], w_ffn_conv=w_ffn_conv[l], b_ffn_conv=b_ffn_conv[l],
                  w_ffn_down=w_ffn_down[l])
        s0 = jnp.zeros((B, H_A, DK_A, DV_A), dtype=jnp.float32)
        hp, a1, a2, a3, a4 = trunk_layer(hp, c_prompt, pos_p, s0, None, None, None, lw)
        lat_p.append(a1); kr_p.append(a2); gla_p.append(a3); conv_p.append(a4)
        past_lat = cache_latent[l][page_table].reshape(Bd, past_len, KV_LORA)
        past_kr = cache_krope[l][page_table].reshape(Bd, past_len, QK_ROPE)
        hs, b1, b2, b3, b4 = trunk_layer(hs, c_sample, pos_s, state_gla[l], state_conv[l], past_lat, past_kr, lw)
        lat_s.append(b1); kr_s.append(b2); gla_s.append(b3); conv_s.append(b4)

    return (hp, hs, jnp.stack(lat_p), jnp.stack(kr_p), jnp.stack(gla_p), jnp.stack(conv_p),
            jnp.stack(lat_s), jnp.stack(kr_s), jnp.stack(gla_s), jnp.stack(conv_s))
```

```python
import numpy as np
from contextlib import ExitStack
import concourse.bass as bass
import concourse.mybir as mybir
from concourse.bass_utils import run_bass_kernel_spmd

F32 = mybir.dt.float32
BF16 = mybir.dt.bfloat16
I32 = mybir.dt.int32
AF = mybir.ActivationFunctionType
ALU = mybir.AluOpType
AX = mybir.AxisListType

D = 1024
T = 2048
NB = 16
NS = 4
TS = 8
NTS = 32
DFF = 2816
INC = 2256
NPHYS = 5120
PAST = 16384
NU = 16
SCALE = 192.0 ** -0.5
EPS = 1e-6
C_Q, C_K, C_V, C_R, C_G, C_CQ, C_KV, C_KR = 0, 256, 512, 1024, 1536, 1552, 1936, 2192
NEG = -30000.0


class Buf:
    __slots__ = ("name", "w", "r")

    def __init__(self, name):
        self.name = name
        self.w = None
        self.r = {}


class MBuf:
    def __init__(self, name):
        self.name = name
        self.parts = []

    def new(self):
        b = Buf("%s_%d" % (self.name, len(self.parts)))
        self.parts.append(b)
        return b


def _expand(bufs):
    out = []
    for b in bufs:
        if isinstance(b, MBuf):
            out.extend(b.parts)
        else:
            out.append(b)
    return out


class Trk:
    def __init__(self, nc, es, n_dsem=56):
        self.nc = nc
        self.eng = dict(pe=nc.tensor, act=nc.scalar, dve=nc.vector, pool=nc.gpsimd, sp=nc.sync)
        self.sem = {k: es.enter_context(nc.semaphore("sem_" + k)) for k in self.eng}
        self.cnt = {k: 0 for k in self.eng}
        self.seen = {k: {} for k in self.eng}
        self.dsem = [es.enter_context(nc.semaphore("dsem%d" % i)) for i in range(n_dsem)]
        self.dval = [0] * n_dsem
        self.snaps = {k: [] for k in self.eng}
        self.dsnaps = {}
        self.sched = None
        self.dnext = {"sp": 0, "pool": n_dsem // 2}
        self.drange = {"sp": (0, n_dsem // 2), "pool": (n_dsem // 2, n_dsem)}

    def _semobj(self, k):
        return self.sem[k] if isinstance(k, str) else self.dsem[k]

    def _waits(self, e, reads, writes):
        need = {}

        def add(kv):
            if kv is None:
                return
            k, v = kv
            if need.get(k, 0) < v:
                need[k] = v

        for b in reads:
            add(b.w)
        for b in writes:
            add(b.w)
            for k, v in b.r.items():
                add((k, v))
        for k, v in sorted(need.items(), key=lambda kv: -kv[1] if isinstance(kv[0], str) else 0):
            if self.seen[e].get(k, 0) >= v:
                continue
            self.eng[e].wait_ge(self._semobj(k), v)
            self.seen[e][k] = v
            snap = self.snaps[k][v - 1] if isinstance(k, str) else self.dsnaps.get((k, v))
            if snap:
                se = self.seen[e]
                for k2, v2 in snap.items():
                    if se.get(k2, 0) < v2:
                        se[k2] = v2

    def op(self, e, fn, reads=(), writes=()):
        reads = _expand(reads)
        self._waits(e, reads, writes)
        ins = fn(self.eng[e])
        self.cnt[e] += 1
        ins.then_inc(self.sem[e], 1)
        n = self.cnt[e]
        self.snaps[e].append(dict(self.seen[e]))
        for b in reads:
            b.r[e] = n
        for b in writes:
            b.w = (e, n)
            b.r = {}
        if self.sched is not None:
            self.sched.switch()
        return ins

    def dma(self, q, out, in_, reads=(), writes=(), indirect=None):
        reads = _expand(reads)
        i = self.dnext[q]
        lo_, hi_ = self.drange[q]
        self.dnext[q] = lo_ + (i + 1 - lo_) % (hi_ - lo_)
        self._waits(q, reads, writes)
        if self.dval[i] > 0 and self.seen[q].get(i, 0) < self.dval[i]:
            self.eng[q].wait_ge(self.dsem[i], self.dval[i])
            self.seen[q][i] = self.dval[i]
        if indirect is None:
            ins = self.eng[q].dma_start(out=out, in_=in_)
        else:
            ins = self.eng[q].indirect_dma_start(out=out, out_offset=None, in_=in_, in_offset=indirect)
        self.dval[i] += 16
        ins.then_inc(self.dsem[i], 16)
        v = self.dval[i]
        self.dsnaps[(i, v)] = dict(self.seen[q])
        for b in reads:
            b.r[i] = v
        for b in writes:
            b.w = (i, v)
            b.r = {}
        if self.sched is not None:
            self.sched.switch()
        return ins

    def barrier(self):
        for e in self.eng:
            for k in self.eng:
                if k != e and self.cnt[k] > self.seen[e].get(k, 0):
                    self.eng[e].wait_ge(self.sem[k], self.cnt[k])
                    self.seen[e][k] = self.cnt[k]
            for i, v in enumerate(self.dval):
                if v > self.seen[e].get(i, 0):
                    self.eng[e].wait_ge(self.dsem[i], v)
                    self.seen[e][i] = v


class Sched:
    def __init__(self, weights):
        import threading
        self.th = threading
        self.cv = threading.Condition()
        self.weights = list(weights)
        self.cur = 0
        self.left = self.weights[0]
        self.live = []
        self.exc = None
        self.tls = threading.local()

    def run(self, fns):
        self.live = list(range(len(fns)))
        ts = [self.th.Thread(target=self._body, args=(i, fn)) for i, fn in enumerate(fns)]
        for t in ts:
            t.start()
        for t in ts:
            t.join()
        if self.exc is not None:
            raise self.exc

    def _next(self, me):
        k = self.live.index(me) if me in self.live else -1
        return self.live[(k + 1) % len(self.live)]

    def _body(self, idx, fn):
        self.tls.idx = idx
        with self.cv:
            while self.cur != idx and self.exc is None:
                self.cv.wait()
        try:
            if self.exc is None:
                fn()
        except BaseException as e:
            if self.exc is None:
                self.exc = e
        finally:
            with self.cv:
                nxt = None
                if idx in self.live:
                    if len(self.live) > 1:
                        nxt = self._next(idx)
                    self.live.remove(idx)
                if nxt is not None:
                    self.cur = nxt
                    self.left = self.weights[nxt]
                self.cv.notify_all()

    def switch(self):
        me = self.tls.idx
        with self.cv:
            if self.exc is not None:
                raise RuntimeError("aborted")
            self.left -= 1
            if self.left > 0 or len(self.live) <= 1:
                if self.left <= 0:
                    self.left = self.weights[me]
                return
            nxt = self._next(me)
            self.cur = nxt
            self.left = self.weights[nxt]
            self.cv.notify_all()
            while self.cur != me and self.exc is None:
                self.cv.wait()
            if self.exc is not None:
                raise RuntimeError("aborted")


class Pool_:
    def __init__(self, tiles, name):
        self.tiles = tiles
        self.bufs = [Buf("%s%d" % (name, i)) for i in range(len(tiles))]
        self.i = 0

    def get(self):
        t, b = self.tiles[self.i], self.bufs[self.i]
        self.i = (self.i + 1) % len(self.tiles)
        return t, b


class _Stop(Exception):
    pass


def build_nc(nphys=NPHYS, dbg=False, stop=None):
    try:
        return _build_nc(nphys, dbg, stop)
    except _Stop as e:
        return e.args[0]


def _build_nc(nphys, dbg, stop):
    nc = bass.Bass("TRN2", target_bir_lowering=False)

    def din(name, shape, dt=F32):
        return nc.dram_tensor(name, list(shape), dt, kind="ExternalInput").ap()

    def dout(name, shape, dt=F32):
        return nc.dram_tensor(name, list(shape), dt, kind="ExternalOutput").ap()

    x_p = din("x_p", [T, D])
    x_s = din("x_s", [NTS, D])
    c_all = din("c_all", [5, D])
    lat_rows = din("lat_rows", [nphys * 16, 8 * 256])
    kr_rows = din("kr_rows", [nphys * 16, 8 * 64])
    ptb = din("ptb", [128, NS * NU], I32)
    lo16 = din("lo16", [128, 1])
    sgla = din("sgla", [NS, 2, 128, 128])
    sconv = din("sconv", [8, 2 * DFF])
    w_ada = din("w_ada", [D, 6 * D])
    b_ada = din("b_ada", [1, 6 * D])
    g_post_mix = din("g_post_mix", [1, D])
    g_post_ffn = din("g_post_ffn", [1, D])
    w_in = din("w_in", [D, INC])
    w_a2 = din("w_a2", [16, 256])
    b_a = din("b_a", [128, 2])
    g_gla_out = din("g_gla_out", [1, 128])
    g_mla_q = din("g_mla_q", [1, 384])
    g_mla_kv = din("g_mla_kv", [1, 256])
    w_uq = din("w_uq", [384, 768])
    w_uk = din("w_uk", [256, 512])
    w_uv = din("w_uv", [256, 512])
    g_mla_out = din("g_mla_out", [1, 128])
    w_o = din("w_o", [D, D])
    w_up = din("w_up", [D, 2 * DFF])
    wconv = din("wconv", [128, 44, 4])
    w_down = din("w_down", [DFF, D])
    ident_f = din("ident_f", [128, 128])
    cos_p = din("cos_p", [T, 128])
    sin_p = din("sin_p", [T, 128])
    cos_s = din("cos_s", [NTS, 128])
    sin_s = din("sin_s", [NTS, 128])
    gmask_p = din("gmask_p", [128, 128])
    gmask_s = din("gmask_s", [NTS, NTS])
    amask_p = din("amask_p", [128, 128])
    amask_s = din("amask_s", [NTS, NS * NTS])
    sel_p = din("sel_p", [5, 128])
    sel_s = din("sel_s", [5, NTS])
    seqsel = din("seqsel", [NTS, NS])
    gvt = din("gvt", [128, 4, 8])

    y_p = dout("y_p", [T, D])
    y_s = dout("y_s", [NTS, D])
    lat_p = dout("lat_p", [T, 256])
    kr_p = dout("kr_p", [T, 64])
    gla_p = dout("gla_p", [2, 128, 128])
    conv_p = dout("conv_p", [2, 2 * DFF])
    lat_s = dout("lat_s", [NTS, 256])
    kr_s = dout("kr_s", [NTS, 64])
    gla_s = dout("gla_s", [NS, 2, 128, 128])
    conv_s = dout("conv_s", [8, 2 * DFF])
    x1s = nc.dram_tensor("x1s", [T + NTS, D], F32, kind="Internal").ap()
    dbg_mixc = dout("dbg_mixc", [NTS, D], BF16) if dbg else None

    with ExitStack() as es:
        tk = Trk(nc, es)
        cnt = [0]

        def sb(stack, shape, dt=F32, name=None):
            cnt[0] += 1
            return stack.enter_context(nc.sbuf_tensor(name or ("t%d" % cnt[0]), list(shape), dt))

        def ps(stack, shape, dt=F32, name=None):
            cnt[0] += 1
            return stack.enter_context(nc.psum_tensor(name or ("p%d" % cnt[0]), list(shape), dt))

        op = tk.op
        dma = tk.dma

        MODP = sb(es, [128, 2, D], BF16)
        MODS = sb(es, [NTS, 2, D], BF16)
        ABT = sb(es, [128, 4, 8, 5], F32)
        bMODP, bMODS = Buf("modp"), Buf("mods")
        bABT = Buf("abt")
        IDF = sb(es, [128, 128], F32)
        IDB = sb(es, [128, 128], BF16)
        bID = Buf("id")
        EPSC = sb(es, [128, 1], F32)
        ONEC = sb(es, [128, 1], F32)
        bCONST = Buf("const")
        dma("sp", IDF[:], ident_f[:], writes=[bID])
        op("dve", lambda e: e.tensor_copy(out=IDB[:], in_=IDF[:]), reads=[bID], writes=[bID])
        op("pool", lambda e: e.memset(EPSC[:], EPS), writes=[bCONST])
        op("pool", lambda e: e.memset(ONEC[:], 1.0), writes=[bCONST])

        def rstd_from_ss(n, ss, bss, out, bout, inv_n):
            op("act", lambda e: e.activation(out=out, in_=ss, func=AF.Ln, bias=EPSC[0:n, :], scale=inv_n),
               reads=[bss, bCONST], writes=[bout])
            op("act", lambda e: e.activation(out=out, in_=out, func=AF.Exp, scale=-0.5),
               reads=[bout], writes=[bout])

        with ExitStack() as e1:
            WIN = sb(e1, [128, 8, INC], BF16)
            WUQ = sb(e1, [128, 3, 4, 192], BF16)
            WUKT = sb(e1, [128, 4, 256], BF16)
            WUV = sb(e1, [128, 2, 512], BF16)
            WO = sb(e1, [128, 8, D], BF16)
            WA2 = sb(e1, [16, 256], BF16)
            BAN = sb(e1, [128, 2], F32)
            GGLA = sb(e1, [128, 4, 128], F32)
            GQ = sb(e1, [128, 384], F32)
            GKV = sb(e1, [128, 256], F32)
            GMO = sb(e1, [128, 128], F32)
            GMSKP = sb(e1, [128, 128], F32)
            GMSKS = sb(e1, [NTS, NTS], F32)
            AMP = sb(e1, [128, 128], BF16)
            AMS = sb(e1, [NTS, NS * NTS], BF16)
            SEQSEL = sb(e1, [NTS, NS], F32)
            bW = MBuf("w1")
            for k in range(8):
                for h2 in range(2):
                    dma("pool", WIN[:, k, h2 * 1128:(h2 + 1) * 1128],
                        w_in[k * 128:(k + 1) * 128, h2 * 1128:(h2 + 1) * 1128], writes=[bW.new()])
            for c in range(3):
                dma("pool", WUQ[:, c, :, :].rearrange("p h n -> p (h n)"), w_uq[c * 128:(c + 1) * 128, :], writes=[bW.new()])
            for c in range(2):
                dma("pool", WUV[:, c, :], w_uv[c * 128:(c + 1) * 128, :], writes=[bW.new()])
            for k in range(8):
                dma("pool", WO[:, k, :], w_o[k * 128:(k + 1) * 128, :], writes=[bW.new()])
            dma("pool", WA2[:], w_a2[:], writes=[bW.new()])
            dma("pool", AMP[:], amask_p[:], writes=[bW.new()])
            dma("pool", AMS[:], amask_s[:], writes=[bW.new()])
            dma("sp", BAN[:], b_a[:], writes=[bW.new()])
            for h in range(4):
                dma("sp", GGLA[:, h, :], g_gla_out.partition_broadcast(128).rearrange("p a f -> p (a f)"), writes=[bW.new()])
            dma("sp", GQ[:], g_mla_q.partition_broadcast(128).rearrange("p a f -> p (a f)"), writes=[bW.new()])
            dma("sp", GKV[:], g_mla_kv.partition_broadcast(128).rearrange("p a f -> p (a f)"), writes=[bW.new()])
            dma("sp", GMO[:], g_mla_out.partition_broadcast(128).rearrange("p a f -> p (a f)"), writes=[bW.new()])
            dma("sp", GMSKP[:], gmask_p[:], writes=[bW.new()])
            dma("sp", GMSKS[:], gmask_s[:], writes=[bW.new()])
            dma("sp", SEQSEL[:], seqsel[:], writes=[bW.new()])
            op("dve", lambda e: e.tensor_scalar(out=BAN[:], in0=BAN[:], scalar1=-1.0, scalar2=None, op0=ALU.mult),
               reads=[bW], writes=[bW.new()])

            with ExitStack() as ep:
                PF = Pool_([ps(ep, [128, 512], F32) for _ in range(3)], "pf")
                PB = Pool_([ps(ep, [128, 1024], BF16) for _ in range(2)], "pb")
                PFS = Pool_([ps(ep, [128, 512], F32) for _ in range(2)], "pfs")
                PBS = Pool_([ps(ep, [128, 1024], BF16) for _ in range(1)], "pbs")

                with ExitStack() as e0:
                    WKF = sb(e0, [128, 2, 512], F32)
                    CALL = sb(e0, [5, D], F32)
                    CT = sb(e0, [128, 8, 5], BF16)
                    BADA = sb(e0, [5, 6 * D], F32)
                    MOD = sb(e0, [5, 6 * D], F32)
                    GT = sb(e0, [128, 2, D], F32)
                    GTT = sb(e0, [128, 4, 8], F32)
                    MODT = sb(e0, [128, 48, 5], F32)
                    SELP = sb(e0, [5, 128], F32)
                    SELS = sb(e0, [5, NTS], F32)
                    WA = [sb(e0, [128, 8, 512], BF16) for _ in range(4)]
                    bWA = [Buf("wa%d" % i) for i in range(4)]
                    bS = MBuf("setup")
                    bMOD = Buf("mod")
                    for c in range(2):
                        dma("sp", WKF[:, c, :], w_uk[c * 128:(c + 1) * 128, :], writes=[bS.new()])
                    for h in range(4):
                        pt_, bpt = PF.get()
                        for c in range(2):
                            op("pe", lambda e, c=c, h=h: e.transpose(out=pt_[:, c * 128:(c + 1) * 128],
                                                                    in_=WKF[:, c, h * 128:(h + 1) * 128], identity=IDF[:]),
                               reads=[bS, bID], writes=[bpt])
                        op("act", lambda e, h=h: e.copy(out=WUKT[:, h, :], in_=pt_[:, 0:256]), reads=[bpt], writes=[bW.new()])
                    dma("sp", CALL[:], c_all[:], writes=[bS.new()])
                    dma("sp", BADA[:], b_ada.partition_broadcast(5).rearrange("p a f -> p (a f)"), writes=[bS.new()])
                    dma("sp", SELP[:], sel_p[:], writes=[bS.new()])
                    dma("sp", SELS[:], sel_s[:], writes=[bS.new()])
                    for i, g in enumerate([g_post_mix, g_post_ffn]):
                        dma("sp", GT[:, i, :], g.partition_broadcast(128).rearrange("p a f -> p (a f)"), writes=[bS.new()])
                    dma("sp", GTT[:], gvt[:], writes=[bS.new()])
                    op("act", lambda e: e.activation(out=CALL[:], in_=CALL[:], func=AF.Silu), reads=[bS], writes=[bS.new()])
                    pt_, bpt = PF.get()
                    for k in range(8):
                        op("pe", lambda e, k=k: e.transpose(out=pt_[:, k * 5:(k + 1) * 5], in_=CALL[:, k * 128:(k + 1) * 128],
                                                            identity=IDF[0:5, 0:5]), reads=[bS, bID], writes=[bpt])
                    op("dve", lambda e: e.tensor_copy(out=CT[:].rearrange("p k c -> p (k c)"), in_=pt_[:, 0:40]),
                       reads=[bpt], writes=[bS.new()])
                    for cc in range(12):
                        wa, bwa = WA[cc % 4], bWA[cc % 4]
                        dma("pool", wa[:], w_ada[:, cc * 512:(cc + 1) * 512].rearrange("(k p) c -> p k c", p=128),
                            writes=[bwa])
                        pm, bpm = PF.get()

                        def mm(e, wa=wa, pm=pm):
                            for k in range(8):
                                r = e.matmul(pm[0:5, :], lhsT=CT[:, k, :], rhs=wa[:, k, :], start=(k == 0), stop=(k == 7))
                            return r
                        op("pe", mm, reads=[bS, bwa], writes=[bpm])
                        op("dve", lambda e, pm=pm, cc=cc: e.tensor_tensor(out=MOD[:, cc * 512:(cc + 1) * 512], in0=pm[0:5, :],
                                                                          in1=BADA[:, cc * 512:(cc + 1) * 512], op=ALU.add),
                           reads=[bpm, bS], writes=[bMOD])
                    for (SEL, n, MT, bMT) in ((SELP, 128, MODP, bMODP), (SELS, NTS, MODS, bMODS)):
                        for mi, part in enumerate((2, 5)):
                            for hf in range(2):
                                pm, bpm = PF.get()
                                op("pe", lambda e, pm=pm, SEL=SEL, n=n, part=part, hf=hf: e.matmul(
                                    pm[0:n, :], lhsT=SEL[:, 0:n], rhs=MOD[:, part * D + hf * 512: part * D + (hf + 1) * 512],
                                    start=True, stop=True), reads=[bS, bMOD], writes=[bpm])
                                cs = slice(hf * 512, (hf + 1) * 512)
                                op("dve", lambda e, pm=pm, n=n, MT=MT, mi=mi, cs=cs: e.tensor_tensor(
                                    out=MT[0:n, mi, cs], in0=pm[0:n, :], in1=GT[0:n, mi, cs], op=ALU.mult),
                                   reads=[bpm, bS], writes=[bMT])
                    pm, bpm = PF.get()

                    def tmod(e, pm=pm):
                        for j in range(48):
                            r = e.transpose(out=pm[:, j * 5:(j + 1) * 5], in_=MOD[0:5, j * 128:(j + 1) * 128], identity=IDF[0:5, 0:5])
                        return r
                    op("pe", tmod, reads=[bMOD, bID], writes=[bpm])
                    op("act", lambda e, pm=pm: e.copy(out=MODT[:].rearrange("p j c -> p (j c)"), in_=pm[:, 0:240]), reads=[bpm], writes=[bS.new()])
                    for (ai, scp, shp, gi) in ((0, 1, 0, 0), (2, 4, 3, 2)):
                        for k in range(8):
                            op("dve", lambda e, ai=ai, scp=scp, gi=gi, k=k: e.tensor_scalar(
                                out=ABT[:, ai, k, :], in0=MODT[:, scp * 8 + k, :], scalar1=1.0, scalar2=GTT[:, gi, k:k + 1],
                                op0=ALU.add, op1=ALU.mult), reads=[bS], writes=[bABT])
                        op("dve", lambda e, ai=ai, shp=shp: e.tensor_copy(out=ABT[:, ai + 1, :, :], in_=MODT[:, shp * 8:(shp + 1) * 8, :]),
                           reads=[bS], writes=[bABT])
                    tk.barrier()
                    if stop == "setup":
                        raise _Stop(nc)
                XT = [sb(e1, [128, D], F32) for _ in range(2)]
                bXT = [Buf("x0"), Buf("x1")]
                SQ = sb(e1, [128, D], BF16)
                bSQ = Buf("sq")
                HB = sb(e1, [128, D], BF16)
                bHB = Buf("hb")
                HT = sb(e1, [128, 8, 128], BF16)
                bHT = Buf("ht")
                ST = sb(e1, [128, 16], F32)
                bST = Buf("st")
                GRT = sb(e1, [16, 128], BF16)
                bGRT = Buf("grt")
                E1 = sb(e1, [128, 2, 128], F32)
                L1 = sb(e1, [128, 2, 128], F32)
                CL = sb(e1, [128, 2, 128], F32)
                EB = sb(e1, [128, 2, 128], F32)
                ENB = sb(e1, [128, 2, 128], F32)
                ONES = sb(e1, [128, 128], F32)
                bGATE = Buf("gate")
                bEB = Buf("eb")
                QE = sb(e1, [128, 2, 128], BF16)
                KE = sb(e1, [128, 2, 128], BF16)
                bQK = Buf("qk")
                QEZ = [sb(e1, [128, 2, 128], BF16) for _ in range(2)]
                HM = sb(e1, [128, 2], F32)
                KET = sb(e1, [128, 2, 128], BF16)
                bKET = Buf("ket")
                KETM = sb(e1, [NTS, 128], BF16)
                bKETM = Buf("ketm")
                ATT = sb(e1, [128, 4, 128], BF16)
                bATT = Buf("att")
                VA = sb(e1, [128, 512], BF16)
                bVA = Buf("va")
                GSR = sb(e1, [128, 512], F32)
                bGSR = Buf("gsr")
                SST = sb(e1, [128, 2, 128], F32)
                SSTB = sb(e1, [128, 2, 128], BF16)
                bSST = Buf("sst")
                SSS = sb(e1, [128, 2, NS, 128], F32)
                SSSB = sb(e1, [128, 2, NS, 128], BF16)
                bSSS = Buf("sss")
                OIN = sb(e1, [NTS, 4, 128], F32)
                bOIN = Buf("oin")
                MIXCP = sb(e1, [128, D], BF16)
                bMIXCP = Buf("mixcp")
                MIXCS = sb(e1, [NTS, D], BF16)
                bMIXCS = Buf("mixcs")
                XS = sb(e1, [NTS, D], F32)
                bXS = Buf("xs")
                MIXT = sb(e1, [128, 8, 128], BF16)
                bMIXT = Buf("mixt")
                CQN = sb(e1, [128, 384], BF16)
                bCQN = Buf("cqn")
                CQT = sb(e1, [128, 3, 128], BF16)
                bCQT = Buf("cqt")
                QNT = sb(e1, [128, 4, 128], BF16)
                bQNT = Buf("qnt")
                QRR = sb(e1, [128, 4, 64], BF16)
                bQRR = Buf("qrr")
                RT = sb(e1, [128, 4, 4, 32], F32)
                bRT = Buf("rt")
                QRT = sb(e1, [64, 4, 128], BF16)
                bQRT = Buf("qrt")
                QLT = sb(e1, [128, 2, 4, 128], BF16)
                bQLT = Buf("qlt")
                QLS = sb(e1, [128, NS, 2, NTS], BF16)
                QRS = sb(e1, [64, NS, NTS], BF16)
                bQS = Buf("qs")
                CKV = sb(e1, [128, 256], F32)
                bCKV = Buf("ckv")
                KRO = sb(e1, [128, 64], F32)
                KRB = sb(e1, [128, 64], BF16)
                bKRO = Buf("kro")
                COS = sb(e1, [128, 128], F32)
                SIN = sb(e1, [128, 128], F32)
                bCS = Buf("cs")
                CKVT = sb(e1, [128, 2, T + NTS], BF16)
                KRT = sb(e1, [64, T + NTS], BF16)
                VTOK = sb(e1, [128, NB + 1, 256], BF16)
                bKVS = [Buf("kvs%d" % i) for i in range(NB + 1)]
                def mk_fs(tag, rows, pf, pb):
                    return dict(FMS=sb(e1, [128, 4, 4], F32), FMT=sb(e1, [128, 4, 2, 8], F32),
                                bM=[Buf("fm_m%s%d" % (tag, i)) for i in range(4)], bL=[Buf("fm_l%s%d" % (tag, i)) for i in range(4)],
                                bFT=[[Buf("fm_t%s%d_%d" % (tag, i, j)) for j in range(2)] for i in range(4)], fpar=[0, 0, 0, 0],
                                OACC=sb(e1, [rows, 4, 256], F32), bOACC=[Buf("oacc%s%d" % (tag, i)) for i in range(4)], pf=pf, pb=pb,
                                PP=sb(e1, [rows, 512], BF16), bPP=Buf("pp" + tag), PT=sb(e1, [128, 4, rows], BF16), bPT=Buf("pt" + tag))
                FSP = mk_fs("p", 128, PF, PB)
                FSS = mk_fs("s", NTS, PFS, PBS)
                ONP = sb(e1, [128, 4, 256], BF16)
                bONP = Buf("onp")
                ONS = sb(e1, [NTS, 4, 256], BF16)
                bONS = Buf("ons")
                OT = sb(e1, [128, 2, 4, 128], BF16)
                bOT = Buf("ot")
                X1 = [sb(e1, [128, D], F32) for _ in range(2)]
                bX1 = [Buf("x1a"), Buf("x1b")]
                TMP = sb(e1, [128, 512], F32)
                bTMP = Buf("tmp")
                LATG = [sb(e1, [128, 8, 256], BF16) for _ in range(3)]
                KRG = [sb(e1, [128, 8, 64], BF16) for _ in range(3)]
                bG = [Buf("g%d" % i) for i in range(3)]
                KTS = [sb(e1, [128, 2, 512], BF16) for _ in range(2)]
                KRTS = [sb(e1, [64, 512], BF16) for _ in range(2)]
                bKTS = [Buf("kts0"), Buf("kts1")]
                PTB = sb(e1, [128, NS * NU], I32)
                IDX = sb(e1, [128, NS * NU], I32)
                LO = sb(e1, [128, 1], F32)
                bIDX = Buf("idx")

                op("pool", lambda e: e.memset(ONES[:], 1.0), writes=[bGATE])
                op("pool", lambda e: e.memset(HM[:], 0.0), writes=[bCONST])
                op("pool", lambda e: e.memset(HM[0:64, 0:1], 1.0), writes=[bCONST])
                op("pool", lambda e: e.memset(HM[64:128, 1:2], 1.0), writes=[bCONST])
                op("pool", lambda e: e.memset(SST[:], 0.0), writes=[bSST])
                op("pool", lambda e: e.memset(SSTB[:], 0.0), writes=[bSST])
                dma("sp", PTB[:], ptb[:], writes=[bIDX])
                dma("sp", LO[:], lo16[:], writes=[bIDX])
                op("dve", lambda e: e.tensor_scalar(out=IDX[:], in0=PTB[:], scalar1=16.0, scalar2=LO[:, 0:1],
                                                    op0=ALU.mult, op1=ALU.add), reads=[bIDX], writes=[bIDX])
                for s in range(NS):
                    for g in range(2):
                        dma("sp", SSS[:, g, s, :], sgla[s, g, :, :], writes=[bSSS])
                op("dve", lambda e: e.tensor_copy(out=SSSB[:].rearrange("p g s v -> p (g s v)"),
                                                  in_=SSS[:].rearrange("p g s v -> p (g s v)")), reads=[bSSS], writes=[bSSS])

                def flash_qk(n, h, qparts, kparts, nk, vblocks, mask, first, extra=(), FS=None):
                    sp_, bsp = FS["pf"].get()
                    rb = [b for _, b in qparts] + [b for _, b in kparts] + list(extra)

                    def qk(e):
                        r = None
                        np_ = len(qparts)
                        for i in range(np_):
                            r = e.matmul(sp_[0:n, 0:nk], lhsT=qparts[i][0], rhs=kparts[i][0], start=(i == 0),
                                         stop=(i == np_ - 1 and mask is None))
                        if mask is not None:
                            r = e.matmul(sp_[0:n, mask[2]:mask[2] + mask[1].shape[-1]], lhsT=mask[0], rhs=mask[1],
                                         start=False, stop=True)
                        return r
                    op("pe", qk, reads=rb + (list(mask[3]) if mask else []), writes=[bsp])
                    return dict(n=n, h=h, nk=nk, vblocks=vblocks, first=first, sp=sp_, bsp=bsp, FS=FS)

                def flash_rest(c):
                    n, h, nk, vblocks, first, sp_, bsp = c["n"], c["h"], c["nk"], c["vblocks"], c["first"], c["sp"], c["bsp"]
                    FS = c["FS"]
                    OACC, bOACC = FS["OACC"], FS["bOACC"]
                    par = FS["fpar"][h]
                    FS["fpar"][h] ^= 1
                    st = FS["FMS"][0:n, h, :]
                    ft = FS["FMT"][0:n, h, par, :]
                    bft = FS["bFT"][h][par]
                    bm, bl = FS["bM"][h], FS["bL"][h]
                    fto = FS["FMT"][0:n, h, par ^ 1, :]
                    bfto = FS["bFT"][h][par ^ 1]
                    op("dve", lambda e: e.reduce_max(out=ft[:, 0:1], in_=sp_[0:n, 0:nk], axis=AX.X), reads=[bsp], writes=[bft])
                    if first:
                        op("dve", lambda e: e.tensor_scalar(out=ft[:, 2:3], in0=ft[:, 0:1], scalar1=-SCALE, scalar2=None,
                                                            op0=ALU.mult), reads=[bft], writes=[bft])
                    else:
                        op("dve", lambda e: e.tensor_scalar(out=ft[:, 2:3], in0=ft[:, 0:1], scalar1=-SCALE, scalar2=fto[:, 2:3],
                                                            op0=ALU.mult, op1=ALU.min), reads=[bft, bfto], writes=[bft])
                        op("act", lambda e: e.activation(out=ft[:, 3:4], in_=fto[:, 2:3], func=AF.Exp, bias=ft[:, 2:3], scale=-1.0),
                           reads=[bft, bfto], writes=[bft])
                    pp, bpp = FS["PP"], FS["bPP"]
                    op("act", lambda e: e.activation(out=pp[0:n, 0:nk], in_=sp_[0:n, 0:nk], func=AF.Exp, bias=ft[:, 2:3],
                                                     scale=SCALE, accum_out=ft[:, 4:5]), reads=[bsp, bft], writes=[bpp, bft])
                    if first:
                        op("dve", lambda e: e.tensor_copy(out=st[:, 1:2], in_=ft[:, 4:5]), reads=[bft], writes=[bl])
                    else:
                        op("dve", lambda e: e.scalar_tensor_tensor(out=st[:, 1:2], in0=st[:, 1:2], scalar=ft[:, 3:4],
                                                                   in1=ft[:, 4:5], op0=ALU.mult, op1=ALU.add),
                           reads=[bft, bl], writes=[bl])
                    yield
                    tp, btp = FS["pb"].get()
                    nblk = len(vblocks)

                    def tps(e):
                        r = None
                        c0 = 0
                        for j in range(nblk):
                            kb = vblocks[j][1]
                            r = e.transpose(out=tp[0:kb, j * 128:j * 128 + n], in_=pp[0:n, c0:c0 + kb], identity=IDB[0:n, 0:n])
                            c0 += kb
                        return r
                    op("pe", tps, reads=[bpp, bID], writes=[btp])
                    pt, bpt = FS["PT"], FS["bPT"]
                    kb0 = vblocks[0][1]
                    op("act", lambda e: e.copy(out=pt[0:kb0, 0:nblk, 0:n],
                                               in_=tp[0:kb0, 0:nblk * 128].rearrange("p (j c) -> p j c", c=128)[:, :, 0:n]),
                       reads=[btp], writes=[bpt])
                    yield
                    oc, boc = FS["pf"].get()

                    def pv(e):
                        r = None
                        for j in range(nblk):
                            kb = vblocks[j][1]
                            r = e.matmul(oc[0:n, 0:256], lhsT=pt[0:kb, j, 0:n], rhs=vblocks[j][0], start=(j == 0), stop=(j == nblk - 1))
                        return r
                    op("pe", pv, reads=[bpt] + [v[2] for v in vblocks], writes=[boc])
                    if first:
                        op("act", lambda e: e.copy(out=OACC[0:n, h, :], in_=oc[0:n, 0:256]), reads=[boc], writes=[bOACC[h]])
                    else:
                        op("dve", lambda e: e.scalar_tensor_tensor(out=OACC[0:n, h, :], in0=OACC[0:n, h, :], scalar=ft[:, 3:4],
                                                                   in1=oc[0:n, 0:256], op0=ALU.mult, op1=ALU.add),
                           reads=[boc, bft, bOACC[h]], writes=[bOACC[h]])

                def flash_final(n, h, out_ap, bout, FS):
                    st = FS["FMS"][0:n, h, :]
                    bL_, OACC_, bOACC_ = FS["bL"], FS["OACC"], FS["bOACC"]
                    op("dve", lambda e: e.reciprocal(out=st[:, 2:3], in_=st[:, 1:2]), reads=[bL_[h]], writes=[bL_[h]])
                    op("dve", lambda e: e.tensor_scalar(out=out_ap, in0=OACC_[0:n, h, :], scalar1=st[:, 2:3], scalar2=None,
                                                        op0=ALU.mult), reads=[bL_[h], bOACC_[h]], writes=[bout])

                def run_pipelined(items, lookahead=True):
                    if not lookahead:
                        for it in items:
                            cur = it()
                            yield
                            yield from flash_rest(cur)
                            yield
                        return
                    nxt = items[0]() if items else None
                    yield
                    for k in range(len(items)):
                        cur = nxt
                        nxt = items[k + 1]() if k + 1 < len(items) else None
                        yield
                        yield from flash_rest(cur)
                        yield

                def load_x(i):
                    xt, bx = XT[i % 2], bXT[i % 2]
                    if i < NB:
                        dma("sp", xt[:], x_p[i * 128:(i + 1) * 128, :], writes=[bx])
                    else:
                        dma("sp", XS[0:NTS, :], x_s[:], writes=[bXS])

                def chk(name):
                    if stop == name:
                        tk.barrier()
                        raise _Stop(nc)

                def block(i, phase="full"):
                    isP = i < NB
                    n = 128 if isP else NTS
                    xt, bx = (XT[i % 2], bXT[i % 2]) if isP else (XS, bXS)
                    MIXC, bMIXC = (MIXCP, bMIXCP) if isP else (MIXCS, bMIXCS)
                    ON, bON = (ONP, bONP) if isP else (ONS, bONS)
                    if phase != "tail":
                        yield from block_A(i, isP, n, xt, bx, MIXC, bMIXC)
                    if phase == "A":
                        return
                    if phase == "full":
                        yield from block_attn(i, n, ON, bON)
                    yield from block_tail(i, isP, n, xt, bx, MIXC, bMIXC, ON, bON)

                def block_A(i, isP, n, xt, bx, MIXC, bMIXC):
                    MT, bMT = (MODP, bMODP) if isP else (MODS, bMODS)
                    col0 = i * 128
                    if isP:
                        dma("sp", COS[:], cos_p[i * 128:(i + 1) * 128, :], writes=[bCS])
                        dma("sp", SIN[:], sin_p[i * 128:(i + 1) * 128, :], writes=[bCS])
                    else:
                        dma("sp", COS[0:n, :], cos_s[:], writes=[bCS])
                        dma("sp", SIN[0:n, :], sin_s[:], writes=[bCS])
                    op("act", lambda e: e.activation(out=SQ[0:n, :], in_=xt[0:n, :], func=AF.Square, accum_out=ST[0:n, 0:1]),
                       reads=[bx], writes=[bSQ, bST])
                    rstd_from_ss(n, ST[0:n, 0:1], bST, ST[0:n, 1:2], bST, 1.0 / D)
                    op("dve", lambda e: e.tensor_scalar(out=HB[0:n, :], in0=xt[0:n, :], scalar1=ST[0:n, 1:2], scalar2=None, op0=ALU.mult),
                       reads=[bx, bST], writes=[bHB])
                    tp, btp = PB.get()

                    def tph(e):
                        for k in range(8):
                            r = e.transpose(out=tp[:, k * 128:k * 128 + n], in_=HB[0:n, k * 128:(k + 1) * 128], identity=IDB[0:n, 0:n])
                        return r
                    op("pe", tph, reads=[bHB, bID], writes=[btp])
                    for k in range(8):
                        if isP:
                            op("act", lambda e, k=k: e.activation(out=HT[:, k, 0:n], in_=tp[:, k * 128:k * 128 + n], func=AF.Identity,
                                                                  scale=ABT[:, 0, k, 0:1], bias=ABT[:, 1, k, 0:1]),
                               reads=[btp, bABT], writes=[bHT])
                        else:
                            for s_ in range(NS):
                                op("act", lambda e, k=k, s_=s_: e.activation(
                                    out=HT[:, k, s_ * TS:(s_ + 1) * TS], in_=tp[:, k * 128 + s_ * TS:k * 128 + (s_ + 1) * TS], func=AF.Identity,
                                    scale=ABT[:, 0, k, 1 + s_:2 + s_], bias=ABT[:, 1, k, 1 + s_:2 + s_]), reads=[btp, bABT], writes=[bHT])
                    chk("ht")
                    yield
                    fg, bfg = PF.get()

                    def mfg(e):
                        for k in range(8):
                            r = e.matmul(fg[0:16, 0:n], lhsT=WIN[:, k, C_G:C_G + 16], rhs=HT[:, k, 0:n], start=(k == 0), stop=(k == 7))
                        return r
                    op("pe", mfg, reads=[bW, bHT], writes=[bfg])
                    op("act", lambda e: e.copy(out=GRT[:, 0:n], in_=fg[0:16, 0:n]), reads=[bfg], writes=[bGRT])
                    pg, bpg = PF.get()

                    def mpg(e):
                        for c in range(2):
                            r = e.matmul(pg[:, c * 128:c * 128 + n], lhsT=WA2[:, c * 128:(c + 1) * 128], rhs=GRT[:, 0:n], start=True, stop=True)
                        return r
                    op("pe", mpg, reads=[bW, bGRT], writes=[bpg])
                    for c in range(2):
                        op("act", lambda e, c=c: e.activation(out=E1[:, c, 0:n], in_=pg[:, c * 128:c * 128 + n], func=AF.Exp,
                                                              bias=BAN[:, c:c + 1], scale=-1.0), reads=[bpg, bW], writes=[bGATE])
                    op("act", lambda e: e.activation(out=L1[:, :, 0:n], in_=E1[:, :, 0:n], func=AF.Ln, bias=ONEC[:, :], scale=1.0),
                       reads=[bGATE, bCONST], writes=[bGATE])
                    chk("gate")
                    yield
                    segs = [(0, n)] if isP else [(s * TS, TS) for s in range(NS)]
                    for c in range(2):
                        for (a, ln) in segs:
                            op("dve", lambda e, c=c, a=a, ln=ln: e.tensor_tensor_scan(
                                out=CL[:, c, a:a + ln], data0=ONES[:, 0:ln], data1=L1[:, c, a:a + ln], initial=0.0,
                                op0=ALU.mult, op1=ALU.add), reads=[bGATE], writes=[bGATE])
                    op("act", lambda e: e.activation(out=EB[:, :, 0:n], in_=CL[:, :, 0:n], func=AF.Exp, scale=-1.0 / 16.0),
                       reads=[bGATE], writes=[bEB])
                    op("act", lambda e: e.activation(out=ENB[:, :, 0:n], in_=CL[:, :, 0:n], func=AF.Exp, scale=1.0 / 16.0),
                       reads=[bGATE], writes=[bEB])
                    yield
                    fq, bfq = PF.get()

                    def mfq(e):
                        for gi, c0 in enumerate((C_Q, C_Q + 128, C_K, C_K + 128)):
                            for k in range(8):
                                r = e.matmul(fq[:, gi * 128:gi * 128 + n], lhsT=WIN[:, k, c0:c0 + 128], rhs=HT[:, k, 0:n],
                                             start=(k == 0), stop=(k == 7))
                        return r
                    op("pe", mfq, reads=[bW, bHT], writes=[bfq])
                    fq3 = fq[:, :].rearrange("p (g c) -> p g c", c=128)
                    op("dve", lambda e: e.scalar_tensor_tensor(out=QE[:, :, 0:n], in0=fq3[:, 0:2, 0:n], scalar=0.125,
                                                               in1=EB[:, :, 0:n], op0=ALU.mult, op1=ALU.mult),
                       reads=[bfq, bEB], writes=[bQK])
                    op("dve", lambda e: e.tensor_tensor(out=KE[:, :, 0:n], in0=fq3[:, 2:4, 0:n], in1=ENB[:, :, 0:n], op=ALU.mult),
                       reads=[bfq, bEB], writes=[bQK])
                    for hh in range(2):
                        op("dve", lambda e, hh=hh: e.tensor_scalar(out=QEZ[hh][:, :, 0:n], in0=QE[:, :, 0:n], scalar1=HM[:, hh:hh + 1], scalar2=None,
                                                                   op0=ALU.mult), reads=[bQK, bCONST], writes=[bQK])
                    chk("qke")
                    yield
                    def tokproj(c0, w):
                        pz, bpz = PF.get()

                        def mz(e):
                            for k in range(8):
                                r = e.matmul(pz[0:n, 0:w], lhsT=HT[:, k, 0:n], rhs=WIN[:, k, c0:c0 + w], start=(k == 0), stop=(k == 7))
                            return r
                        op("pe", mz, reads=[bW, bHT], writes=[bpz])
                        return pz, bpz
                    zv, bzv = tokproj(C_V, 512)
                    op("act", lambda e: e.copy(out=VA[0:n, :], in_=zv[0:n, :]), reads=[bzv], writes=[bVA])
                    zr, bzr = tokproj(C_R, 512)
                    op("act", lambda e: e.activation(out=GSR[0:n, :], in_=zr[0:n, :], func=AF.Silu), reads=[bzr], writes=[bGSR])
                    op("dve", lambda e: e.tensor_tensor(out=GSR[0:n, :], in0=GSR[0:n, :],
                                                         in1=GGLA[0:n, :, :].rearrange("p h v -> p (h v)"), op=ALU.mult),
                       reads=[bGSR, bW], writes=[bGSR])
                    yield
                    zq, bzq = tokproj(C_CQ, 384)
                    op("act", lambda e: e.activation(out=SQ[0:n, 0:384], in_=zq[0:n, 0:384], func=AF.Square, accum_out=ST[0:n, 2:3]),
                       reads=[bzq], writes=[bSQ, bST])
                    rstd_from_ss(n, ST[0:n, 2:3], bST, ST[0:n, 3:4], bST, 1.0 / 384)
                    op("dve", lambda e: e.scalar_tensor_tensor(out=CQN[0:n, :], in0=zq[0:n, 0:384], scalar=ST[0:n, 3:4],
                                                               in1=GQ[0:n, :], op0=ALU.mult, op1=ALU.mult),
                       reads=[bzq, bST, bW], writes=[bCQN])
                    yield
                    zk, bzk = tokproj(C_KV, 320)
                    op("act", lambda e: e.activation(out=SQ[0:n, 0:256], in_=zk[0:n, 0:256], func=AF.Square, accum_out=ST[0:n, 4:5]),
                       reads=[bzk], writes=[bSQ, bST])
                    rstd_from_ss(n, ST[0:n, 4:5], bST, ST[0:n, 5:6], bST, 1.0 / 256)
                    op("dve", lambda e: e.scalar_tensor_tensor(out=CKV[0:n, :], in0=zk[0:n, 0:256], scalar=ST[0:n, 5:6],
                                                               in1=GKV[0:n, :], op0=ALU.mult, op1=ALU.mult),
                       reads=[bzk, bST, bW], writes=[bCKV])
                    dma("sp", (lat_p[i * 128:(i + 1) * 128, :] if isP else lat_s[:]), CKV[0:n, :], reads=[bCKV])
                    bkv = bKVS[i]
                    op("act", lambda e: e.copy(out=VTOK[0:n, i, :], in_=CKV[0:n, :]), reads=[bCKV], writes=[bkv])
                    chk("tokproj")
                    x1 = zk[0:n, 256:288]
                    x2 = zk[0:n, 288:320]
                    op("dve", lambda e: e.tensor_tensor(out=RT[0:n, 0, 0, :], in0=x1, in1=COS[0:n, 0:32], op=ALU.mult), reads=[bzk, bCS], writes=[bRT])
                    op("dve", lambda e: e.tensor_tensor(out=RT[0:n, 1, 0, :], in0=x2, in1=SIN[0:n, 0:32], op=ALU.mult), reads=[bzk, bCS], writes=[bRT])
                    op("dve", lambda e: e.tensor_tensor(out=RT[0:n, 2, 0, :], in0=x1, in1=SIN[0:n, 0:32], op=ALU.mult), reads=[bzk, bCS], writes=[bRT])
                    op("dve", lambda e: e.tensor_tensor(out=RT[0:n, 3, 0, :], in0=x2, in1=COS[0:n, 0:32], op=ALU.mult), reads=[bzk, bCS], writes=[bRT])
                    op("dve", lambda e: e.tensor_tensor(out=KRO[0:n, 0:32], in0=RT[0:n, 0, 0, :], in1=RT[0:n, 1, 0, :], op=ALU.subtract),
                       reads=[bRT], writes=[bKRO])
                    op("dve", lambda e: e.tensor_tensor(out=KRO[0:n, 32:64], in0=RT[0:n, 2, 0, :], in1=RT[0:n, 3, 0, :], op=ALU.add),
                       reads=[bRT], writes=[bKRO])
                    dma("sp", (kr_p[i * 128:(i + 1) * 128, :] if isP else kr_s[:]), KRO[0:n, :], reads=[bKRO])
                    op("act", lambda e: e.copy(out=KRB[0:n, :], in_=KRO[0:n, :]), reads=[bKRO], writes=[bKRO])
                    tp, btp = PB.get()

                    def tpk(e):
                        for c in range(2):
                            r = e.transpose(out=tp[:, c * 128:c * 128 + n], in_=VTOK[0:n, i, c * 128:(c + 1) * 128], identity=IDB[0:n, 0:n])
                        r = e.transpose(out=tp[0:64, 256:256 + n], in_=KRB[0:n, :], identity=IDB[0:n, 0:n])
                        return r
                    op("pe", tpk, reads=[bkv, bKRO, bID], writes=[btp])
                    op("dve", lambda e: e.tensor_copy(out=CKVT[:, :, col0:col0 + n],
                                                      in_=tp[:, 0:256].rearrange("p (c k) -> p c k", k=128)[:, :, 0:n]),
                       reads=[btp], writes=[bkv])
                    op("act", lambda e: e.copy(out=KRT[:, col0:col0 + n], in_=tp[0:64, 256:256 + n]), reads=[btp], writes=[bkv])
                    chk("kt")
                    yield
                    gmask = GMSKP if isP else GMSKS
                    for g in range(2):
                        tpe, btpe = PB.get()
                        op("pe", lambda e, g=g: e.transpose(out=tpe[0:n, 0:128], in_=KE[:, g, 0:n], identity=IDB[:, :]),
                           reads=[bQK, bID], writes=[btpe])
                        op("act", lambda e, g=g: e.copy(out=KET[0:n, g, :], in_=tpe[0:n, 0:128]), reads=[btpe], writes=[bKET])
                    pa, bpa = PF.get()

                    def matt(e):
                        for g in range(2):
                            for hh in range(2):
                                h = 2 * g + hh
                                psl = slice(hh * 64, (hh + 1) * 64)
                                r = e.matmul(pa[0:n, h * 128:h * 128 + n], lhsT=KE[:, g, 0:n], rhs=QEZ[hh][:, g, 0:n], start=True, stop=True)
                        return r
                    op("pe", matt, reads=[bQK], writes=[bpa])
                    for h in range(4):
                        op("dve", lambda e, h=h: e.tensor_tensor(out=ATT[0:n, h, 0:n], in0=pa[0:n, h * 128:h * 128 + n],
                                                                 in1=gmask[0:n, 0:n], op=ALU.mult), reads=[bpa, bW], writes=[bATT])
                    po, bpo = PF.get()
                    if isP:
                        def mo(e):
                            for g in range(2):
                                for hh in range(2):
                                    h = 2 * g + hh
                                    psl = slice(hh * 64, (hh + 1) * 64)
                                    e.matmul(po[0:n, h * 128:(h + 1) * 128], lhsT=ATT[0:n, h, 0:n], rhs=VA[0:n, h * 128:(h + 1) * 128],
                                             start=True, stop=False)
                                    r = e.matmul(po[0:n, h * 128:(h + 1) * 128], lhsT=QEZ[hh][:, g, 0:n], rhs=SSTB[:, g, :],
                                                 start=False, stop=True)
                            return r
                        op("pe", mo, reads=[bATT, bVA, bQK, bSST], writes=[bpo])
                        oin = None
                    else:
                        def mo(e):
                            for h in range(4):
                                r = e.matmul(po[0:n, h * 128:(h + 1) * 128], lhsT=ATT[0:n, h, 0:n], rhs=VA[0:n, h * 128:(h + 1) * 128],
                                             start=True, stop=True)
                            return r
                        op("pe", mo, reads=[bATT, bVA], writes=[bpo])
                        op("act", lambda e: e.copy(out=OIN[0:n, :, :].rearrange("p h v -> p (h v)"), in_=po[0:n, :]), reads=[bpo], writes=[bOIN])
                        for g in range(2):
                            for hh in range(2):
                                h = 2 * g + hh
                                pi_, bpi = PF.get()
                                op("pe", lambda e, g=g, hh=hh, pi_=pi_: e.matmul(
                                    pi_[0:n, 0:512], lhsT=QEZ[hh][:, g, 0:n], rhs=SSSB[:, g, :, :].rearrange("p s v -> p (s v)"),
                                    start=True, stop=True), reads=[bQK, bSSS], writes=[bpi])
                                for s in range(NS):
                                    op("dve", lambda e, pi_=pi_, h=h, s=s: e.scalar_tensor_tensor(
                                        out=OIN[0:n, h, :], in0=pi_[0:n, s * 128:(s + 1) * 128], scalar=SEQSEL[0:n, s:s + 1],
                                        in1=OIN[0:n, h, :], op0=ALU.mult, op1=ALU.add), reads=[bpi, bW, bOIN], writes=[bOIN])
                        oin = OIN
                    if isP:
                        for g in range(2):
                            pkv, bpkv = PF.get()
                            op("pe", lambda e, g=g, pkv=pkv: e.matmul(pkv[:, 0:256], lhsT=KET[0:n, g, :], rhs=VA[0:n, g * 256:(g + 1) * 256],
                                                                     start=True, stop=True), reads=[bKET, bVA], writes=[bpkv])
                            for hh in range(2):
                                psl = slice(hh * 64, (hh + 1) * 64)
                                op("dve", lambda e, g=g, hh=hh, psl=psl, pkv=pkv: e.tensor_tensor(
                                    out=SST[psl, g, :], in0=pkv[psl, hh * 128:(hh + 1) * 128], in1=SST[psl, g, :], op=ALU.add),
                                   reads=[bpkv, bSST], writes=[bSST])
                            op("dve", lambda e, g=g: e.tensor_scalar(out=SST[:, g, :], in0=SST[:, g, :], scalar1=EB[:, g, n - 1:n],
                                                                     scalar2=None, op0=ALU.mult), reads=[bSST, bEB], writes=[bSST])
                            op("act", lambda e, g=g: e.copy(out=SSTB[:, g, :], in_=SST[:, g, :]), reads=[bSST], writes=[bSST])
                    else:
                        for s in range(NS):
                            for g in range(2):
                                op("dve", lambda e, s=s, g=g: e.tensor_scalar(out=KETM[0:n, :], in0=KET[0:n, g, :], scalar1=SEQSEL[0:n, s:s + 1],
                                                                              scalar2=None, op0=ALU.mult), reads=[bKET, bW], writes=[bKETM])
                                pkv, bpkv = PF.get()
                                op("pe", lambda e, g=g, pkv=pkv: e.matmul(pkv[:, 0:256], lhsT=KETM[0:n, :], rhs=VA[0:n, g * 256:(g + 1) * 256],
                                                                         start=True, stop=True), reads=[bKETM, bVA], writes=[bpkv])
                                for hh in range(2):
                                    psl = slice(hh * 64, (hh + 1) * 64)
                                    op("dve", lambda e, g=g, s=s, hh=hh, psl=psl, pkv=pkv: e.tensor_tensor(
                                        out=SSS[psl, g, s, :], in0=pkv[psl, hh * 128:(hh + 1) * 128], in1=SSS[psl, g, s, :], op=ALU.add),
                                       reads=[bpkv, bSSS], writes=[bSSS])
                                op("dve", lambda e, g=g, s=s: e.tensor_scalar(out=SSS[:, g, s, :], in0=SSS[:, g, s, :],
                                                                              scalar1=EB[:, g, s * TS + TS - 1:s * TS + TS], scalar2=None,
                                                                              op0=ALU.mult), reads=[bSSS, bEB], writes=[bSSS])
                    chk("glast")
                    if oin is not None:
                        osrc = lambda h: OIN[0:n, h, :]
                        bosrc = bOIN
                    else:
                        osrc = lambda h: po[0:n, h * 128:(h + 1) * 128]
                        bosrc = bpo
                    for h in range(4):
                        op("act", lambda e, h=h: e.activation(out=SQ[0:n, h * 128:(h + 1) * 128], in_=osrc(h), func=AF.Square,
                                                              accum_out=ST[0:n, 8 + h:9 + h]), reads=[bosrc], writes=[bSQ, bST])
                    rstd_from_ss(n, ST[0:n, 8:12], bST, ST[0:n, 12:16], bST, 1.0 / 128)
                    for h in range(4):
                        op("dve", lambda e, h=h: e.scalar_tensor_tensor(out=MIXC[0:n, h * 128:(h + 1) * 128], in0=osrc(h),
                                                                        scalar=ST[0:n, 12 + h:13 + h], in1=GSR[0:n, h * 128:(h + 1) * 128],
                                                                        op0=ALU.mult, op1=ALU.mult), reads=[bosrc, bST, bGSR], writes=[bMIXC])
                    chk("gla")
                    yield
                    tp, btp = PB.get()

                    def tpq(e):
                        for c in range(3):
                            r = e.transpose(out=tp[:, c * 128:c * 128 + n], in_=CQN[0:n, c * 128:(c + 1) * 128], identity=IDB[0:n, 0:n])
                        return r
                    op("pe", tpq, reads=[bCQN, bID], writes=[btp])
                    op("act", lambda e: e.copy(out=CQT[:, :, 0:n], in_=tp[:, 0:384].rearrange("p (c k) -> p c k", k=128)[:, :, 0:n]),
                       reads=[btp], writes=[bCQT])
                    pq, bpq = PF.get()

                    def mqn(e):
                        for h in range(4):
                            for c in range(3):
                                r = e.matmul(pq[:, h * 128:h * 128 + n], lhsT=WUQ[:, c, h, 0:128], rhs=CQT[:, c, 0:n], start=(c == 0), stop=(c == 2))
                        return r
                    op("pe", mqn, reads=[bW, bCQT], writes=[bpq])
                    op("act", lambda e: e.copy(out=QNT[:, :, 0:n], in_=pq[:, :].rearrange("p (h k) -> p h k", k=128)[:, :, 0:n]),
                       reads=[bpq], writes=[bQNT])
                    pr, bpr = PF.get()

                    def mqr(e):
                        for c in range(3):
                            r = e.matmul(pr[0:n, 0:256].rearrange("p (h r) -> p h r", r=64), lhsT=CQT[:, c, 0:n], rhs=WUQ[:, c, :, 128:192],
                                         start=(c == 0), stop=(c == 2))
                        return r
                    op("pe", mqr, reads=[bW, bCQT], writes=[bpr])
                    pr3 = pr[0:n, 0:256].rearrange("p (h r) -> p h r", r=64)
                    cos4 = COS[0:n, :].rearrange("p (h r) -> p h r", r=32)
                    sin4 = SIN[0:n, :].rearrange("p (h r) -> p h r", r=32)
                    op("dve", lambda e: e.tensor_tensor(out=RT[0:n, 0, :, :], in0=pr3[:, :, 0:32], in1=cos4, op=ALU.mult), reads=[bpr, bCS], writes=[bRT])
                    op("dve", lambda e: e.tensor_tensor(out=RT[0:n, 1, :, :], in0=pr3[:, :, 32:64], in1=sin4, op=ALU.mult), reads=[bpr, bCS], writes=[bRT])
                    op("dve", lambda e: e.tensor_tensor(out=RT[0:n, 2, :, :], in0=pr3[:, :, 0:32], in1=sin4, op=ALU.mult), reads=[bpr, bCS], writes=[bRT])
                    op("dve", lambda e: e.tensor_tensor(out=RT[0:n, 3, :, :], in0=pr3[:, :, 32:64], in1=cos4, op=ALU.mult), reads=[bpr, bCS], writes=[bRT])
                    op("dve", lambda e: e.tensor_tensor(out=QRR[0:n, :, 0:32], in0=RT[0:n, 0, :, :], in1=RT[0:n, 1, :, :], op=ALU.subtract),
                       reads=[bRT], writes=[bQRR])
                    op("dve", lambda e: e.tensor_tensor(out=QRR[0:n, :, 32:64], in0=RT[0:n, 2, :, :], in1=RT[0:n, 3, :, :], op=ALU.add),
                       reads=[bRT], writes=[bQRR])
                    tp, btp = PB.get()

                    def tpr(e):
                        for h in range(4):
                            r = e.transpose(out=tp[0:64, h * 128:h * 128 + n], in_=QRR[0:n, h, :], identity=IDB[0:n, 0:n])
                        return r
                    op("pe", tpr, reads=[bQRR, bID], writes=[btp])
                    op("act", lambda e: e.copy(out=QRT[:, :, 0:n], in_=tp[0:64, 0:512].rearrange("p (h k) -> p h k", k=128)[:, :, 0:n]),
                       reads=[btp], writes=[bQRT])
                    for c2 in range(2):
                        pl, bpl = PF.get()

                        def mql(e, c2=c2, pl=pl):
                            for h in range(4):
                                r = e.matmul(pl[:, h * 128:h * 128 + n], lhsT=WUKT[:, h, c2 * 128:(c2 + 1) * 128], rhs=QNT[:, h, 0:n],
                                             start=True, stop=True)
                            return r
                        op("pe", mql, reads=[bW, bQNT], writes=[bpl])
                        op("dve", lambda e, c2=c2, pl=pl: e.tensor_copy(out=QLT[:, c2, :, 0:n],
                                                                        in_=pl[:, :].rearrange("p (h k) -> p h k", k=128)[:, :, 0:n]),
                           reads=[bpl], writes=[bQLT])
                    chk("mlaq")
                    if not isP:
                        for s in range(NS):
                            for c2 in range(2):
                                op("pool", lambda e, s=s, c2=c2: e.tensor_copy(
                                    out=QLS[:, s, c2, :].rearrange("p (h t) -> p h t", t=TS), in_=QLT[:, c2, :, s * TS:(s + 1) * TS]),
                                   reads=[bQLT], writes=[bQS])
                            op("pool", lambda e, s=s: e.tensor_copy(out=QRS[:, s, :].rearrange("p (h t) -> p h t", t=TS),
                                                                   in_=QRT[:, :, s * TS:(s + 1) * TS]), reads=[bQRT], writes=[bQS])
                    yield

                def block_attn(i, n, ON, bON):
                    MT, bMT = MODP, bMODP
                    nkeys = (i + 1) * 128
                    items = []
                    k0 = 0
                    while k0 < nkeys:
                        nk = min(512, nkeys - k0)
                        for h in range(4):
                            def mk(h=h, k0=k0, nk=nk):
                                blks = list(range(k0 // 128, (k0 + nk) // 128))
                                kb = [bKVS[j] for j in blks]
                                qparts = [(QLT[:, 0, h, 0:n], bQLT), (QLT[:, 1, h, 0:n], bQLT), (QRT[:, h, 0:n], bQRT)]
                                kparts = [(CKVT[:, 0, k0:k0 + nk], kb[0]), (CKVT[:, 1, k0:k0 + nk], kb[-1]), (KRT[:, k0:k0 + nk], kb[-1])]
                                vbl = [(VTOK[:, j, :], 128, bKVS[j]) for j in blks]
                                mask = None
                                if k0 + nk == nkeys:
                                    mask = (IDB[:, :], AMP[:, :], nk - 128, [bID, bW] + kb)
                                return flash_qk(n, h, qparts, kparts, nk, vbl, mask, k0 == 0, extra=kb, FS=FSP)
                            items.append(mk)
                        k0 += nk
                    yield from run_pipelined(items)
                    for h in range(4):
                        flash_final(n, h, ON[0:n, h, :], bON, FSP)
                    yield

                def block_tail(i, isP, n, xt, bx, MIXC, bMIXC, ON, bON):
                    MT, bMT = (MODP, bMODP) if isP else (MODS, bMODS)
                    chk("attn")
                    for c2 in range(2):
                        tp, btp = PB.get()
                        if isP:
                            def tpo(e, c2=c2, tp=tp):
                                for h in range(4):
                                    r = e.transpose(out=tp[:, h * 128:h * 128 + n], in_=ON[0:n, h, c2 * 128:(c2 + 1) * 128], identity=IDB[0:n, 0:n])
                                return r
                            op("pe", tpo, reads=[bON, bID], writes=[btp])
                            op("act", lambda e, c2=c2, tp=tp: e.copy(out=OT[:, c2, :, 0:n], in_=tp[:, 0:512].rearrange("p (h k) -> p h k", k=128)[:, :, 0:n]),
                               reads=[btp], writes=[bOT])
                        else:
                            def tpo(e, c2=c2, tp=tp):
                                for s in range(NS):
                                    r = e.transpose(out=tp[:, s * 32:(s + 1) * 32], in_=ON[0:NTS, s, c2 * 128:(c2 + 1) * 128], identity=IDB[0:NTS, 0:NTS])
                                return r
                            op("pe", tpo, reads=[bON, bID], writes=[btp])
                            for s in range(NS):
                                op("act", lambda e, c2=c2, tp=tp, s=s: e.copy(out=OT[:, c2, :, s * TS:(s + 1) * TS],
                                                                             in_=tp[:, s * 32:(s + 1) * 32].rearrange("p (h t) -> p h t", t=TS)),
                                   reads=[btp], writes=[bOT])
                    pob, bpob = PF.get()

                    def mob(e):
                        for h in range(4):
                            for c2 in range(2):
                                r = e.matmul(pob[0:n, h * 128:(h + 1) * 128], lhsT=OT[:, c2, h, 0:n], rhs=WUV[:, c2, h * 128:(h + 1) * 128],
                                             start=(c2 == 0), stop=(c2 == 1))
                        return r
                    op("pe", mob, reads=[bOT, bW], writes=[bpob])
                    for h in range(4):
                        op("act", lambda e, h=h: e.activation(out=SQ[0:n, h * 128:(h + 1) * 128], in_=pob[0:n, h * 128:(h + 1) * 128],
                                                              func=AF.Square, accum_out=ST[0:n, 8 + h:9 + h]), reads=[bpob], writes=[bSQ, bST])
                    rstd_from_ss(n, ST[0:n, 8:12], bST, ST[0:n, 12:16], bST, 1.0 / 128)
                    for h in range(4):
                        op("dve", lambda e, h=h: e.scalar_tensor_tensor(out=MIXC[0:n, 512 + h * 128:512 + (h + 1) * 128],
                                                                        in0=pob[0:n, h * 128:(h + 1) * 128], scalar=ST[0:n, 12 + h:13 + h],
                                                                        in1=GMO[0:n, :], op0=ALU.mult, op1=ALU.mult),
                           reads=[bpob, bST, bW], writes=[bMIXC])
                    chk("oside")
                    yield
                    if dbg and not isP:
                        dma("sp", dbg_mixc[:, :], MIXC[0:n, :], reads=[bMIXC])
                    tp, btp = PB.get()

                    def tpm(e):
                        for k in range(8):
                            r = e.transpose(out=tp[:, k * 128:k * 128 + n], in_=MIXC[0:n, k * 128:(k + 1) * 128], identity=IDB[0:n, 0:n])
                        return r
                    op("pe", tpm, reads=[bMIXC, bID], writes=[btp])
                    op("act", lambda e: e.copy(out=MIXT[:, :, 0:n], in_=tp[:, :].rearrange("p (k c) -> p k c", c=128)[:, :, 0:n]),
                       reads=[btp], writes=[bMIXT])
                    pm2 = []
                    for hf in range(2):
                        pm, bpm = PF.get()

                        def mmx(e, hf=hf, pm=pm):
                            for k in range(8):
                                r = e.matmul(pm[0:n, :], lhsT=MIXT[:, k, 0:n], rhs=WO[:, k, hf * 512:(hf + 1) * 512], start=(k == 0), stop=(k == 7))
                            return r
                        op("pe", mmx, reads=[bMIXT, bW], writes=[bpm])
                        op("act", lambda e, hf=hf, pm=pm: e.activation(out=SQ[0:n, hf * 512:(hf + 1) * 512], in_=pm[0:n, :], func=AF.Square,
                                                                       accum_out=ST[0:n, 6 + hf:7 + hf]), reads=[bpm], writes=[bSQ, bST])
                        pm2.append((pm, bpm))
                    op("dve", lambda e: e.tensor_tensor(out=ST[0:n, 6:7], in0=ST[0:n, 6:7], in1=ST[0:n, 7:8], op=ALU.add), reads=[bST], writes=[bST])
                    rstd_from_ss(n, ST[0:n, 6:7], bST, ST[0:n, 7:8], bST, 1.0 / D)
                    x1t, bx1 = X1[i % 2], bX1[i % 2]
                    for hf in range(2):
                        pm, bpm = pm2[hf]
                        cs = slice(hf * 512, (hf + 1) * 512)
                        op("dve", lambda e, pm=pm, cs=cs: e.scalar_tensor_tensor(out=TMP[0:n, :], in0=pm[0:n, :], scalar=ST[0:n, 7:8],
                                                                                 in1=MT[0:n, 0, cs], op0=ALU.mult, op1=ALU.mult),
                           reads=[bpm, bST, bMT], writes=[bTMP])
                        op("dve", lambda e, cs=cs: e.tensor_tensor(out=x1t[0:n, cs], in0=TMP[0:n, :], in1=xt[0:n, cs], op=ALU.add),
                           reads=[bTMP, bx], writes=[bx1])
                    r0 = i * 128
                    dma("sp", x1s[r0:r0 + n, :], x1t[0:n, :], reads=[bx1])


                gstate = {"next": 0}

                def issue_gather(gidx):
                    s_, u = divmod(gidx, NU)
                    slot = gidx % 3
                    col = s_ * NU + u
                    dma("pool", LATG[slot][:].rearrange("p j f -> p (j f)"), lat_rows, reads=[bIDX], writes=[bG[slot]],
                        indirect=bass.IndirectOffsetOnAxis(ap=IDX[:, col:col + 1], axis=0))
                    dma("pool", KRG[slot][:].rearrange("p j f -> p (j f)"), kr_rows, reads=[bIDX], writes=[bG[slot]],
                        indirect=bass.IndirectOffsetOnAxis(ap=IDX[:, col:col + 1], axis=0))

                def prefetch_gathers(upto):
                    while gstate["next"] < min(upto, NS * NU):
                        issue_gather(gstate["next"])
                        gstate["next"] += 1

                def sample_attention():
                    n = NTS
                    kcount = [0]
                    for s in range(NS):
                        qparts = [(QLS[:, s, 0, :], bQS), (QLS[:, s, 1, :], bQS), (QRS[:, s, :], bQS)]
                        items = []
                        for u in range(NU):
                            for half in range(2):
                                def mk(s=s, u=u, half=half, qparts=qparts):
                                    gidx = s * NU + u
                                    slot = gidx % 3
                                    if half == 0:
                                        prefetch_gathers(gidx + 3)
                                    ki = kcount[0] % 2
                                    kcount[0] += 1
                                    kts, krts, bk = KTS[ki], KRTS[ki], bKTS[ki]
                                    tpa, btpa = PBS.get()

                                    def tl(e):
                                        for jj in range(4):
                                            j = half * 4 + jj
                                            for c2 in range(2):
                                                r = e.transpose(out=tpa[:, c2 * 512 + jj * 128:c2 * 512 + (jj + 1) * 128],
                                                                in_=LATG[slot][:, j, c2 * 128:(c2 + 1) * 128], identity=IDB[:, :])
                                        return r
                                    op("pe", tl, reads=[bG[slot], bID], writes=[btpa])
                                    op("dve", lambda e: e.tensor_copy(out=kts[:].rearrange("p c k -> p (c k)"), in_=tpa[:, :]),
                                       reads=[btpa], writes=[bk])
                                    tpb, btpb = PBS.get()

                                    def tr(e):
                                        for jj in range(4):
                                            j = half * 4 + jj
                                            r = e.transpose(out=tpb[0:64, jj * 128:(jj + 1) * 128], in_=KRG[slot][:, j, :], identity=IDB[:, :])
                                        return r
                                    op("pe", tr, reads=[bG[slot], bID], writes=[btpb])
                                    op("act", lambda e: e.copy(out=krts[:, :], in_=tpb[0:64, 0:512]), reads=[btpb], writes=[bk])
                                    kparts = [(kts[:, 0, :], bk), (kts[:, 1, :], bk), (krts[:, :], bk)]
                                    vbl = [(LATG[slot][:, half * 4 + jj, :], 128, bG[slot]) for jj in range(4)]
                                    return flash_qk(n, s, qparts, kparts, 512, vbl, None, u == 0 and half == 0, FS=FSS)
                                items.append(mk)

                        def mk_new(s=s, qparts=qparts):
                            kparts = [(CKVT[:, 0, T:T + NTS], bKVS[NB]), (CKVT[:, 1, T:T + NTS], bKVS[NB]), (KRT[:, T:T + NTS], bKVS[NB])]
                            vbl = [(VTOK[0:NTS, NB, :], NTS, bKVS[NB])]
                            mask = (IDB[0:NTS, 0:NTS], AMS[0:NTS, s * NTS:(s + 1) * NTS], 0, [bID, bW])
                            return flash_qk(n, s, qparts, kparts, NTS, vbl, mask, False, FS=FSS)
                        items.append(mk_new)
                        yield from run_pipelined(items, lookahead=False)
                        flash_final(n, s, ONS[0:NTS, s, :], bONS, FSS)
                        yield

                prefetch_gathers(3)
                load_x(NB)
                for _ in block(NB, "A"):
                    pass
                def chain_p():
                    load_x(0)
                    for i in range(NB):
                        if i + 1 < NB:
                            load_x(i + 1)
                        for _ in block(i):
                            pass

                def chain_s():
                    for _ in sample_attention():
                        pass
                tk.sched = Sched([5, 2])
                tk.sched.run([chain_p, chain_s])
                tk.sched = None
                for _ in block(NB, "tail"):
                    pass
                dma("sp", gla_p.rearrange("g p v -> p g v"), SST[:], reads=[bSST])
                for s_ in range(NS):
                    for g_ in range(2):
                        dma("sp", gla_s[s_, g_, :, :], SSS[:, g_, s_, :], reads=[bSSS])
                tk.barrier()
                if stop in ("p1", "p1sample"):
                    raise _Stop(nc)
        with ExitStack() as e2:
            WUP = sb(e2, [128, 8, 2 * DFF], BF16)
            WDN = sb(e2, [128, 22, D], BF16)
            WCV = sb(e2, [128, 44, 4], F32)
            bWUP = [MBuf("wup%d" % i) for i in range(4)]
            bWDN = [Buf("wdn%d" % i) for i in range(22)]
            bWCV = Buf("wcv")
            dma("sp", WCV[:], wconv[:], writes=[bWCV])
            for q4 in (0, 2, 1, 3):
                for k in range(8):
                    dma("pool", WUP[:, k, q4 * 1408:(q4 + 1) * 1408], w_up[k * 128:(k + 1) * 128, q4 * 1408:(q4 + 1) * 1408], writes=[bWUP[q4].new()])
                if q4 == 2:
                    for f in range(6):
                        dma("pool", WDN[:, f, :], w_down[f * 128:(f + 1) * 128, :], writes=[bWDN[f]])
            for f in range(6, 22):
                dma("pool", WDN[:, f, :], w_down[f * 128:(f + 1) * 128, :], writes=[bWDN[f]])
            PF2 = [ps(e2, [128, 512], F32) for _ in range(8)]
            PY = Pool_(PF2[0:4], "py")
            PU = Pool_(PF2[4:8], "pu")
            L = 256
            XG2 = [sb(e2, [128, 2, D], F32) for _ in range(2)]
            bXG2 = [[Buf("xg%d_%d" % (i, j)) for j in range(2)] for i in range(2)]
            ST2 = sb(e2, [128, 8], F32)
            bST2 = Buf("st2")
            TH = sb(e2, [128, D], F32)
            bTH = Buf("th")
            H2 = sb(e2, [128, D], BF16)
            bH2 = Buf("h2")
            H2T2 = [sb(e2, [128, 8, L], BF16) for _ in range(2)]
            bH2T2 = [Buf("h2t0"), Buf("h2t1")]
            NBUF = 4
            UU = [sb(e2, [128, 2, 2 + L], F32) for _ in range(NBUF)]
            bUU = [Buf("uu%d" % i) for i in range(NBUF)]
            bUUh = [Buf("uuh%d" % i) for i in range(NBUF)]
            CC = [sb(e2, [128, 2, L], F32) for _ in range(NBUF)]
            bCa = [Buf("ca%d" % i) for i in range(NBUF)]
            bCg = [Buf("cg%d" % i) for i in range(NBUF)]
            ACTT = [sb(e2, [128, L], BF16) for _ in range(NBUF)]
            bACTT = [Buf("at%d" % i) for i in range(NBUF)]
            HALO = sb(e2, [128, 22, 2, 2], F32)
            bHALO = [Buf("halo%d" % i) for i in range(22)]
            HALS = sb(e2, [128, 22, 2, 8], F32)
            bHALS = [Buf("hals%d" % i) for i in range(22)]
            CB = sb(e2, [8, 512], F32)
            bCB = Buf("cb")
            op("pool", lambda e: e.memset(HALO[:], 0.0), writes=bHALO)
            for q in range(11):
                dma("sp", CB[:, :], sconv[:, q * 512:(q + 1) * 512], writes=[bCB])
                pu, bpu = PU.get()

                def tph_(e, pu=pu):
                    for j in range(4):
                        r = e.transpose(out=pu[:, j * 8:(j + 1) * 8], in_=CB[0:8, j * 128:(j + 1) * 128], identity=IDF[0:8, 0:8])
                    return r
                op("pe", tph_, reads=[bCB, bID], writes=[bpu])
                for j in range(4):
                    fo = q * 4 + j
                    op("act", lambda e, j=j, fo=fo, pu=pu: e.copy(out=HALS[:, fo % 22, fo // 22, :], in_=pu[:, j * 8:(j + 1) * 8]),
                       reads=[bpu], writes=[bHALS[fo % 22]])

            def ffn_group(gi, part, pre=None, mid=None, late=None):
                isP = gi < 8
                XG, bXG, H2T, bH2T = XG2[gi % 2], bXG2[gi % 2], H2T2[gi % 2], bH2T2[gi % 2]
                nblk = 2 if isP else 1
                n = 128 if isP else NTS
                Lg = 256 if isP else NTS
                nseg = 1 if isP else NS
                Ls = Lg // nseg
                MT, bMT = (MODP, bMODP) if isP else (MODS, bMODS)
                r0 = gi * 256
                for b in (range(nblk) if part == "pro1" else ()):
                    dma("sp", XG[0:n, b, :], x1s[r0 + b * 128:r0 + b * 128 + n, :], writes=[bXG[b]])
                    op("act", lambda e, b=b: e.activation(out=TH[0:n, :], in_=XG[0:n, b, :], func=AF.Square, accum_out=ST2[0:n, 0:1]),
                       reads=[bXG[b]], writes=[bTH, bST2])
                    rstd_from_ss(n, ST2[0:n, 0:1], bST2, ST2[0:n, 4 + b:5 + b], bST2, 1.0 / D)
                for b in (range(nblk) if part == "pro2" else ()):
                    op("dve", lambda e, b=b: e.tensor_scalar(out=H2[0:n, :], in0=XG[0:n, b, :], scalar1=ST2[0:n, 4 + b:5 + b], scalar2=None, op0=ALU.mult),
                       reads=[bXG[b], bST2], writes=[bH2])
                    puf, bpu = PU.get()
                    pu = puf[:, :].bitcast(BF16)

                    def tp2(e, pu=pu):
                        for k in range(8):
                            r = e.transpose(out=pu[:, k * 128:k * 128 + n], in_=H2[0:n, k * 128:(k + 1) * 128], identity=IDB[0:n, 0:n])
                        return r
                    op("pe", tp2, reads=[bH2, bID], writes=[bpu])
                    for k in range(8):
                        if isP:
                            op("act", lambda e, k=k, b=b, pu=pu: e.activation(
                                out=H2T[:, k, b * 128:b * 128 + n], in_=pu[:, k * 128:k * 128 + n],
                                func=AF.Identity, scale=ABT[:, 2, k, 0:1], bias=ABT[:, 3, k, 0:1]),
                               reads=[bpu, bABT], writes=[bH2T])
                        else:
                            for s_ in range(NS):
                                op("act", lambda e, k=k, s_=s_, pu=pu: e.activation(
                                    out=H2T[:, k, s_ * TS:(s_ + 1) * TS], in_=pu[:, k * 128 + s_ * TS:k * 128 + (s_ + 1) * TS], func=AF.Identity,
                                    scale=ABT[:, 2, k, 1 + s_:2 + s_], bias=ABT[:, 3, k, 1 + s_:2 + s_]), reads=[bpu, bABT], writes=[bH2T])
                if part in ("pro1", "pro2"):
                    return
                ybanks = [PY.tiles[j] for j in range(nblk * 2)], [PY.bufs[j] for j in range(nblk * 2)]
                ybanks = list(zip(*ybanks))
                if part == "epi":
                    epilogue(n, nblk, ybanks, MT, bMT, XG, bXG, isP, r0)
                    return

                def up(f):
                    pu, bpu = PU.get()

                    def mup(e, f=f, pu=pu):
                        for ag in range(2):
                            fo = ag * 22 + f
                            for k in range(8):
                                r = e.matmul(pu[:, ag * 256:ag * 256 + Lg], lhsT=WUP[:, k, fo * 128:(fo + 1) * 128], rhs=H2T[:, k, 0:Lg],
                                             start=(k == 0), stop=(k == 7))
                        return r
                    op("pe", mup, reads=[bWUP[f * 128 // 1408], bWUP[(22 + f) * 128 // 1408], bH2T], writes=[bpu])
                    return pu, bpu

                def ew1(f, bank):
                    pu, bpu = bank
                    bi = f % NBUF
                    U, Cc = UU[bi], CC[bi]
                    bUh, bUm = bUUh[bi], bUU[bi]
                    U4 = U[:, :, 0:nseg * (2 + Ls)].rearrange("p a (s c) -> p a s c", c=2 + Ls)
                    pu4 = pu[:, :].rearrange("p (a c) -> p a c", a=2)[:, :, 0:Lg].rearrange("p a (s c) -> p a s c", c=Ls)
                    halo = (HALO[:, f, :, :].rearrange("p a (s c) -> p a s c", c=2) if isP
                            else HALS[:, f, :, :].rearrange("p a (s c) -> p a s c", c=2))
                    bh = bHALO[f] if isP else bHALS[f]
                    op("act", lambda e: e.copy(out=U4[:, :, :, 0:2], in_=halo), reads=[bh], writes=[bUh])
                    op("act", lambda e: e.copy(out=U4[:, :, :, 2:2 + Ls], in_=pu4), reads=[bpu], writes=[bUm])
                    op("act", lambda e: e.copy(out=halo, in_=U4[:, :, :, Ls:Ls + 2]), reads=[bUm], writes=[bh])
                    for ag, bC in ((1, bCg[bi]), (0, bCa[bi])):
                        fo = ag * 22 + f
                        op("act", lambda e, ag=ag, fo=fo: e.activation(out=Cc[:, ag, 0:Lg], in_=pu[:, ag * 256:ag * 256 + Lg], func=AF.Identity,
                                                                       scale=WCV[:, fo, 2:3], bias=WCV[:, fo, 3:4]),
                           reads=[bpu, bWCV], writes=[bC])
                    for ag, bC in ((1, bCg[bi]), (0, bCa[bi])):
                        fo = ag * 22 + f
                        C3 = Cc[:, ag, 0:Lg].rearrange("p (s c) -> p s c", c=Ls)
                        for tap in (1, 0):
                            op("dve", lambda e, ag=ag, fo=fo, C3=C3, tap=tap: e.scalar_tensor_tensor(
                                out=C3, in0=U4[:, ag, :, tap:tap + Ls], scalar=WCV[:, fo, tap:tap + 1], in1=C3, op0=ALU.mult, op1=ALU.add),
                               reads=[bUh, bUm, bWCV, bC], writes=[bC])

                def ew2(f):
                    bi = f % NBUF
                    Cc = CC[bi]
                    op("act", lambda e: e.activation(out=Cc[:, 1, 0:Lg], in_=Cc[:, 1, 0:Lg], func=AF.Gelu_apprx_tanh),
                       reads=[bCg[bi]], writes=[bCg[bi]])
                    op("pool", lambda e: e.tensor_tensor(out=ACTT[bi][:, 0:Lg], in0=Cc[:, 0, 0:Lg], in1=Cc[:, 1, 0:Lg], op=ALU.mult),
                       reads=[bCa[bi], bCg[bi]], writes=[bACTT[bi]])

                def down(f):
                    bi = f % NBUF
                    for b in range(nblk):
                        for hf in range(2):
                            py, bpy = ybanks[b * 2 + hf]
                            op("pe", lambda e, py=py, b=b, hf=hf, bi=bi, f=f: e.matmul(
                                py[0:n, :], lhsT=ACTT[bi][:, b * 128:b * 128 + n], rhs=WDN[:, f, hf * 512:(hf + 1) * 512],
                                start=(f == 0), stop=(f == 21)), reads=[bACTT[bi], bWDN[f]], writes=[bpy])

                LOOK = 3
                pend = [up(f) for f in range(min(LOOK, 22))]
                if pre is not None:
                    pre()
                for f in range(22):
                    if f == 8 and mid is not None:
                        mid()
                    cur = pend.pop(0)
                    if f + LOOK < 22:
                        pend.append(up(f + LOOK))
                    if f >= 1:
                        ew2(f - 1)
                    ew1(f, cur)
                    if f == 19 and late is not None:
                        late()
                    if f >= 2:
                        down(f - 2)
                ew2(21)
                down(20)
                down(21)

            def epilogue(n, nblk, ybanks, MT, bMT, XG, bXG, isP, r0):
                for b in range(nblk):
                    for hf in range(2):
                        py, bpy = ybanks[b * 2 + hf]
                        op("act", lambda e, py=py, hf=hf: e.activation(out=H2[0:n, hf * 512:(hf + 1) * 512], in_=py[0:n, :], func=AF.Square,
                                                                       accum_out=ST2[0:n, 2 + hf:3 + hf]), reads=[bpy], writes=[bH2, bST2])
                    op("dve", lambda e: e.tensor_tensor(out=ST2[0:n, 2:3], in0=ST2[0:n, 2:3], in1=ST2[0:n, 3:4], op=ALU.add), reads=[bST2], writes=[bST2])
                    rstd_from_ss(n, ST2[0:n, 2:3], bST2, ST2[0:n, 3:4], bST2, 1.0 / D)
                    for hf in range(2):
                        py, bpy = ybanks[b * 2 + hf]
                        cs = slice(hf * 512, (hf + 1) * 512)
                        op("dve", lambda e, py=py, cs=cs: e.scalar_tensor_tensor(out=TH[0:n, cs], in0=py[0:n, :], scalar=ST2[0:n, 3:4],
                                                                                 in1=MT[0:n, 1, cs], op0=ALU.mult, op1=ALU.mult),
                           reads=[bpy, bST2, bMT], writes=[bTH])
                    op("pool", lambda e, b=b: e.tensor_tensor(out=XG[0:n, b, :], in0=TH[0:n, :], in1=XG[0:n, b, :], op=ALU.add),
                       reads=[bTH, bXG[b]], writes=[bXG[b]])
                    if isP:
                        dma("sp", y_p[r0 + b * 128:r0 + (b + 1) * 128, :], XG[:, b, :], reads=[bXG[b]])
                    else:
                        dma("sp", y_s[:, :], XG[0:n, b, :], reads=[bXG[b]])

            ffn_group(0, "pro1")
            ffn_group(0, "pro2")
            for gi in range(9):
                ffn_group(gi, "loop",
                          pre=(lambda gi=gi: ffn_group(gi - 1, "epi")) if gi > 0 else None,
                          mid=(lambda gi=gi: ffn_group(gi + 1, "pro1")) if gi + 1 < 9 else None,
                          late=(lambda gi=gi: ffn_group(gi + 1, "pro2")) if gi + 1 < 9 else None)
            ffn_group(8, "epi")
            for (HL, bHLs, ncol, dst) in ((HALO, bHALO, 2, conv_p), (HALS, bHALS, 8, conv_s)):
                for q in range(11):
                    pu, bpu = PU.get()

                    def tpc(e, q=q, pu=pu, HL=HL, ncol=ncol):
                        for j in range(4):
                            fo = q * 4 + j
                            r = e.transpose(out=pu[0:ncol, j * 128:(j + 1) * 128], in_=HL[:, fo % 22, fo // 22, :], identity=IDF[:, :])
                        return r
                    op("pe", tpc, reads=[bHLs[(q * 4 + j) % 22] for j in range(4)] + [bID], writes=[bpu])
                    op("act", lambda e, pu=pu, ncol=ncol: e.copy(out=CB[0:ncol, :], in_=pu[0:ncol, :]), reads=[bpu], writes=[bCB])
                    dma("sp", dst[:, q * 512:(q + 1) * 512], CB[0:ncol, :], reads=[bCB])
            tk.barrier()
    return nc


_NC_CACHE = {}


def _get_nc():
    if "nc" not in _NC_CACHE:
        _NC_CACHE["nc"] = build_nc()
    return _NC_CACHE["nc"]


def _consts():
    inv = 10000.0 ** (-np.arange(0, 64, 2, dtype=np.float32) / np.float32(64))
    inv = inv.astype(np.float32)

    def tabs(pos):
        ang = pos.astype(np.float32)[:, None] * inv[None, :]
        c = np.cos(ang).astype(np.float32)
        s = np.sin(ang).astype(np.float32)
        return np.tile(c, (1, 4)), np.tile(s, (1, 4))
    cos_p, sin_p = tabs(np.arange(T))
    cs, ss = tabs(PAST + np.arange(TS))
    cos_s = np.tile(cs, (NS, 1))
    sin_s = np.tile(ss, (NS, 1))
    ii = np.arange(128)
    gmask_p = (ii[:, None] <= ii[None, :]).astype(np.float32)
    amask_p = np.where(ii[None, :] > ii[:, None], NEG, 0.0).astype(np.float32)
    seq = np.arange(NTS) // TS
    tt = np.arange(NTS) % TS
    gmask_s = ((seq[:, None] == seq[None, :]) & (tt[:, None] <= tt[None, :])).astype(np.float32)
    qt = np.arange(NTS) % TS
    am = np.full((NTS, NS, NTS), NEG, np.float32)
    for s in range(NS):
        ok = (seq[None, :] == s) & (tt[None, :] <= qt[:, None])
        am[:, s, :] = np.where(ok, 0.0, NEG)
    sel_p = np.zeros((5, 128), np.float32)
    sel_p[0, :] = 1.0
    sel_s = np.zeros((5, NTS), np.float32)
    for s in range(NS):
        sel_s[1 + s, s * TS:(s + 1) * TS] = 1.0
    seqsel = (seq[:, None] == np.arange(NS)[None, :]).astype(np.float32)
    lo16 = (np.arange(128) % 16).astype(np.float32).reshape(128, 1)
    return dict(ident_f=np.eye(128, dtype=np.float32), cos_p=cos_p, sin_p=sin_p, cos_s=cos_s, sin_s=sin_s,
                gmask_p=gmask_p, gmask_s=gmask_s, amask_p=amask_p, amask_s=am.reshape(NTS, NS * NTS),
                sel_p=sel_p, sel_s=sel_s, seqsel=seqsel, lo16=lo16)


def _prep(x_prompt, x_sample, c_prompt, c_sample, cache_latent, cache_krope, state_gla, state_conv,
          page_table, w_ada, b_ada, g_pre_mix, g_post_mix, g_pre_ffn, g_post_ffn, w_in, w_gla_a2,
          b_gla_a, g_gla_out, g_mla_q, g_mla_kv, w_mla_uq, w_mla_uk, w_mla_uv, g_mla_out, w_o,
          w_ffn_up, w_ffn_conv, b_ffn_conv, w_ffn_down):
    f = lambda a: np.ascontiguousarray(np.asarray(a, dtype=np.float32))
    cst = _consts()
    lat_rows = f(cache_latent).reshape(NPHYS * 16, 8 * 256)
    kr_rows = f(cache_krope).reshape(NPHYS * 16, 8 * 64)
    wconv = np.concatenate([f(w_ffn_conv)[0], f(b_ffn_conv)], axis=0)
    wconv = np.ascontiguousarray(wconv.reshape(4, 44, 128).transpose(2, 1, 0))
    shared = dict(
        lat_rows=lat_rows, kr_rows=kr_rows,
        w_ada=f(w_ada)[0], b_ada=f(b_ada), g_post_mix=f(g_post_mix),
        g_post_ffn=f(g_post_ffn), w_in=f(w_in)[0], w_a2=f(w_gla_a2)[0],
        b_a=np.ascontiguousarray(f(b_gla_a)[0].reshape(2, 128).T), g_gla_out=f(g_gla_out), g_mla_q=f(g_mla_q),
        g_mla_kv=f(g_mla_kv), w_uq=f(w_mla_uq)[0], w_uk=f(w_mla_uk)[0].reshape(256, 512),
        w_uv=f(w_mla_uv)[0].reshape(256, 512), g_mla_out=f(g_mla_out), w_o=f(w_o)[0], w_up=f(w_ffn_up)[0],
        wconv=wconv, w_down=f(w_ffn_down)[0],
        gvt=np.ascontiguousarray(np.stack([f(g)[0].reshape(8, 128).T for g in (g_pre_mix, g_post_mix, g_pre_ffn, g_post_ffn)], axis=1)),
        **cst)
    pt = np.asarray(page_table, dtype=np.int32)
    xp, xs = f(x_prompt), f(x_sample)
    cp, csm = f(c_prompt), f(c_sample)
    sg, sc = f(state_gla)[0], f(state_conv)[0]
    in_maps = []
    for c in range(8):
        sl = slice(4 * c, 4 * c + 4)
        pts = pt[sl].reshape(NS, NU, 8)
        ptb = np.repeat(pts.transpose(2, 0, 1), 16, axis=0)
        m = dict(shared)
        m.update(
            x_p=xp[c], x_s=np.ascontiguousarray(xs[sl].reshape(NTS, D)),
            c_all=np.ascontiguousarray(np.concatenate([cp[c:c + 1], csm[sl]], axis=0)),
            ptb=np.ascontiguousarray(ptb.reshape(128, NS * NU)).astype(np.int32),
            sgla=np.ascontiguousarray(sg[sl].reshape(NS, 2, 128, 128)),
            sconv=np.ascontiguousarray(sc[sl].reshape(8, 2 * DFF)))
        in_maps.append(m)
    return in_maps


def _post(R):
    n = len(R)
    y_p = np.stack([R[c]["y_p"] for c in range(n)])
    y_s = np.concatenate([R[c]["y_s"].reshape(NS, TS, D) for c in range(n)])
    lat_p = np.stack([R[c]["lat_p"] for c in range(n)])[None]
    kr_p = np.stack([R[c]["kr_p"] for c in range(n)])[None]
    gla_p = np.stack([R[c]["gla_p"].reshape(4, 64, 128) for c in range(n)])[None]
    conv_p = np.stack([R[c]["conv_p"] for c in range(n)])[None]
    lat_s = np.concatenate([R[c]["lat_s"].reshape(NS, TS, 256) for c in range(n)])[None]
    kr_s = np.concatenate([R[c]["kr_s"].reshape(NS, TS, 64) for c in range(n)])[None]
    gla_s = np.concatenate([R[c]["gla_s"].reshape(NS, 4, 64, 128) for c in range(n)])[None]
    conv_s = np.concatenate([R[c]["conv_s"].reshape(NS, 2, 2 * DFF) for c in range(n)])[None]
    outs = (y_p, y_s, lat_p, kr_p, gla_p, conv_p, lat_s, kr_s, gla_s, conv_s)
    return tuple(np.ascontiguousarray(o, dtype=np.float32) for o in outs)


def kernel(**inputs):
    in_maps = _prep(**inputs)
    nc = _get_nc()
    res = run_bass_kernel_spmd(nc, in_maps, core_ids=list(range(8)))
    return _post(res.results)
```

```python
import numpy as np
from contextlib import ExitStack
import concourse.bass as bass
import concourse.mybir as mybir
from concourse.bass_utils import run_bass_kernel_spmd

F32 = mybir.dt.float32
BF16 = mybir.dt.bfloat16
I32 = mybir.dt.int32
AF = mybir.ActivationFunctionType
ALU = mybir.AluOpType
AX = mybir.AxisListType

D = 1024
T = 2048
NB = 16
NS = 4
TS = 8
NTS = 32
DFF = 2816
INC = 2256
NPHYS = 5120
PAST = 16384
NU = 16
SCALE = 192.0 ** -0.5
EPS = 1e-6
C_Q, C_K, C_V, C_R, C_G, C_CQ, C_KV, C_KR = 0, 256, 512, 1024, 1536, 1552, 1936, 2192
NEG = -30000.0


class Buf:
    __slots__ = ("name", "w", "r")

    def __init__(self, name):
        self.name = name
        self.w = None
        self.r = {}


class MBuf:
    def __init__(self, name):
        self.name = name
        self.parts = []

    def new(self):
        b = Buf("%s_%d" % (self.name, len(self.parts)))
        self.parts.append(b)
        return b


def _expand(bufs):
    out = []
    for b in bufs:
        if isinstance(b, MBuf):
            out.extend(b.parts)
        else:
            out.append(b)
    return out


class Trk:
    def __init__(self, nc, es, n_dsem=56):
        self.nc = nc
        self.eng = dict(pe=nc.tensor, act=nc.scalar, dve=nc.vector, pool=nc.gpsimd, sp=nc.sync)
        self.sem = {k: es.enter_context(nc.semaphore("sem_" + k)) for k in self.eng}
        self.cnt = {k: 0 for k in self.eng}
        self.seen = {k: {} for k in self.eng}
        self.dsem = [es.enter_context(nc.semaphore("dsem%d" % i)) for i in range(n_dsem)]
        self.dval = [0] * n_dsem
        self.snaps = {k: [] for k in self.eng}
        self.dsnaps = {}
        self.sched = None
        self.dnext = {"sp": 0, "pool": n_dsem // 2}
        self.drange = {"sp": (0, n_dsem // 2), "pool": (n_dsem // 2, n_dsem)}

    def _semobj(self, k):
        return self.sem[k] if isinstance(k, str) else self.dsem[k]

    def _waits(self, e, reads, writes):
        need = {}

        def add(kv):
            if kv is None:
                return
            k, v = kv
            if need.get(k, 0) < v:
                need[k] = v

        for b in reads:
            add(b.w)
        for b in writes:
            add(b.w)
            for k, v in b.r.items():
                add((k, v))
        for k, v in sorted(need.items(), key=lambda kv: -kv[1] if isinstance(kv[0], str) else 0):
            if self.seen[e].get(k, 0) >= v:
                continue
            self.eng[e].wait_ge(self._semobj(k), v)
            self.seen[e][k] = v
            snap = self.snaps[k][v - 1] if isinstance(k, str) else self.dsnaps.get((k, v))
            if snap:
                se = self.seen[e]
                for k2, v2 in snap.items():
                    if se.get(k2, 0) < v2:
                        se[k2] = v2

    def op(self, e, fn, reads=(), writes=()):
        reads = _expand(reads)
        self._waits(e, reads, writes)
        ins = fn(self.eng[e])
        self.cnt[e] += 1
        ins.then_inc(self.sem[e], 1)
        n = self.cnt[e]
        self.snaps[e].append(dict(self.seen[e]))
        for b in reads:
            b.r[e] = n
        for b in writes:
            b.w = (e, n)
            b.r = {}
        if self.sched is not None:
            self.sched.switch()
        return ins

    def dma(self, q, out, in_, reads=(), writes=(), indirect=None):
        reads = _expand(reads)
        i = self.dnext[q]
        lo_, hi_ = self.drange[q]
        self.dnext[q] = lo_ + (i + 1 - lo_) % (hi_ - lo_)
        self._waits(q, reads, writes)
        if self.dval[i] > 0 and self.seen[q].get(i, 0) < self.dval[i]:
            self.eng[q].wait_ge(self.dsem[i], self.dval[i])
            self.seen[q][i] = self.dval[i]
        if indirect is None:
            ins = self.eng[q].dma_start(out=out, in_=in_)
        else:
            ins = self.eng[q].indirect_dma_start(out=out, out_offset=None, in_=in_, in_offset=indirect)
        self.dval[i] += 16
        ins.then_inc(self.dsem[i], 16)
        v = self.dval[i]
        self.dsnaps[(i, v)] = dict(self.seen[q])
        for b in reads:
            b.r[i] = v
        for b in writes:
            b.w = (i, v)
            b.r = {}
        if self.sched is not None:
            self.sched.switch()
        return ins

    def barrier(self):
        for e in self.eng:
            for k in self.eng:
                if k != e and self.cnt[k] > self.seen[e].get(k, 0):
                    self.eng[e].wait_ge(self.sem[k], self.cnt[k])
                    self.seen[e][k] = self.cnt[k]
            for i, v in enumerate(self.dval):
                if v > self.seen[e].get(i, 0):
                    self.eng[e].wait_ge(self.dsem[i], v)
                    self.seen[e][i] = v


class Sched:
    def __init__(self, weights):
        import threading
        self.th = threading
        self.cv = threading.Condition()
        self.weights = list(weights)
        self.cur = 0
        self.left = self.weights[0]
        self.live = []
        self.exc = None
        self.tls = threading.local()

    def run(self, fns):
        self.live = list(range(len(fns)))
        ts = [self.th.Thread(target=self._body, args=(i, fn)) for i, fn in enumerate(fns)]
        for t in ts:
            t.start()
        for t in ts:
            t.join()
        if self.exc is not None:
            raise self.exc

    def _next(self, me):
        k = self.live.index(me) if me in self.live else -1
        return self.live[(k + 1) % len(self.live)]

    def _body(self, idx, fn):
        self.tls.idx = idx
        with self.cv:
            while self.cur != idx and self.exc is None:
                self.cv.wait()
        try:
            if self.exc is None:
                fn()
        except BaseException as e:
            if self.exc is None:
                self.exc = e
        finally:
            with self.cv:
                nxt = None
                if idx in self.live:
                    if len(self.live) > 1:
                        nxt = self._next(idx)
                    self.live.remove(idx)
                if nxt is not None:
                    self.cur = nxt
                    self.left = self.weights[nxt]
                self.cv.notify_all()

    def switch(self):
        me = self.tls.idx
        with self.cv:
            if self.exc is not None:
                raise RuntimeError("aborted")
            self.left -= 1
            if self.left > 0 or len(self.live) <= 1:
                if self.left <= 0:
                    self.left = self.weights[me]
                return
            nxt = self._next(me)
            self.cur = nxt
            self.left = self.weights[nxt]
            self.cv.notify_all()
            while self.cur != me and self.exc is None:
                self.cv.wait()
            if self.exc is not None:
                raise RuntimeError("aborted")


class Pool_:
    def __init__(self, tiles, name):
        self.tiles = tiles
        self.bufs = [Buf("%s%d" % (name, i)) for i in range(len(tiles))]
        self.i = 0

    def get(self):
        t, b = self.tiles[self.i], self.bufs[self.i]
        self.i = (self.i + 1) % len(self.tiles)
        return t, b


class _Stop(Exception):
    pass


def build_nc(nphys=NPHYS, dbg=False, stop=None):
    try:
        return _build_nc(nphys, dbg, stop)
    except _Stop as e:
        return e.args[0]


def _build_nc(nphys, dbg, stop):
    nc = bass.Bass("TRN2", target_bir_lowering=False)

    def din(name, shape, dt=F32):
        return nc.dram_tensor(name, list(shape), dt, kind="ExternalInput").ap()

    def dout(name, shape, dt=F32):
        return nc.dram_tensor(name, list(shape), dt, kind="ExternalOutput").ap()

    x_p = din("x_p", [T, D])
    x_s = din("x_s", [NTS, D])
    c_all = din("c_all", [5, D])
    lat_rows = din("lat_rows", [nphys * 16, 8 * 256])
    kr_rows = din("kr_rows", [nphys * 16, 8 * 64])
    ptb = din("ptb", [128, NS * NU], I32)
    lo16 = din("lo16", [128, 1])
    sgla = din("sgla", [NS, 2, 128, 128])
    sconv = din("sconv", [8, 2 * DFF])
    w_ada = din("w_ada", [D, 6 * D])
    b_ada = din("b_ada", [1, 6 * D])
    g_post_mix = din("g_post_mix", [1, D])
    g_post_ffn = din("g_post_ffn", [1, D])
    w_in = din("w_in", [D, INC])
    w_a2 = din("w_a2", [16, 256])
    b_a = din("b_a", [128, 2])
    g_gla_out = din("g_gla_out", [1, 128])
    g_mla_q = din("g_mla_q", [1, 384])
    g_mla_kv = din("g_mla_kv", [1, 256])
    w_uq = din("w_uq", [384, 768])
    w_uk = din("w_uk", [256, 512])
    w_uv = din("w_uv", [256, 512])
    g_mla_out = din("g_mla_out", [1, 128])
    w_o = din("w_o", [D, D])
    w_up = din("w_up", [D, 2 * DFF])
    wconv = din("wconv", [128, 44, 4])
    w_down = din("w_down", [DFF, D])
    ident_f = din("ident_f", [128, 128])
    cos_p = din("cos_p", [T, 128])
    sin_p = din("sin_p", [T, 128])
    cos_s = din("cos_s", [NTS, 128])
    sin_s = din("sin_s", [NTS, 128])
    gmask_p = din("gmask_p", [128, 128])
    gmask_s = din("gmask_s", [NTS, NTS])
    amask_p = din("amask_p", [128, 128])
    amask_s = din("amask_s", [NTS, NS * NTS])
    sel_p = din("sel_p", [5, 128])
    sel_s = din("sel_s", [5, NTS])
    seqsel = din("seqsel", [NTS, NS])
    gvt = din("gvt", [128, 4, 8])

    y_p = dout("y_p", [T, D])
    y_s = dout("y_s", [NTS, D])
    lat_p = dout("lat_p", [T, 256])
    kr_p = dout("kr_p", [T, 64])
    gla_p = dout("gla_p", [2, 128, 128])
    conv_p = dout("conv_p", [2, 2 * DFF])
    lat_s = dout("lat_s", [NTS, 256])
    kr_s = dout("kr_s", [NTS, 64])
    gla_s = dout("gla_s", [NS, 2, 128, 128])
    conv_s = dout("conv_s", [8, 2 * DFF])
    x1s = nc.dram_tensor("x1s", [T + NTS, D], F32, kind="Internal").ap()
    dbg_mixc = dout("dbg_mixc", [NTS, D], BF16) if dbg else None

    with ExitStack() as es:
        tk = Trk(nc, es)
        cnt = [0]

        def sb(stack, shape, dt=F32, name=None):
            cnt[0] += 1
            return stack.enter_context(nc.sbuf_tensor(name or ("t%d" % cnt[0]), list(shape), dt))

        def ps(stack, shape, dt=F32, name=None):
            cnt[0] += 1
            return stack.enter_context(nc.psum_tensor(name or ("p%d" % cnt[0]), list(shape), dt))

        op = tk.op
        dma = tk.dma

        MODP = sb(es, [128, 2, D], BF16)
        MODS = sb(es, [NTS, 2, D], BF16)
        ABT = sb(es, [128, 4, 8, 5], F32)
        bMODP, bMODS = Buf("modp"), Buf("mods")
        bABT = Buf("abt")
        IDF = sb(es, [128, 128], F32)
        IDB = sb(es, [128, 128], BF16)
        bID = Buf("id")
        EPSC = sb(es, [128, 1], F32)
        ONEC = sb(es, [128, 1], F32)
        bCONST = Buf("const")
        dma("sp", IDF[:], ident_f[:], writes=[bID])
        op("dve", lambda e: e.tensor_copy(out=IDB[:], in_=IDF[:]), reads=[bID], writes=[bID])
        op("pool", lambda e: e.memset(EPSC[:], EPS), writes=[bCONST])
        op("pool", lambda e: e.memset(ONEC[:], 1.0), writes=[bCONST])

        def rstd_from_ss(n, ss, bss, out, bout, inv_n):
            op("act", lambda e: e.activation(out=out, in_=ss, func=AF.Ln, bias=EPSC[0:n, :], scale=inv_n),
               reads=[bss, bCONST], writes=[bout])
            op("act", lambda e: e.activation(out=out, in_=out, func=AF.Exp, scale=-0.5),
               reads=[bout], writes=[bout])

        with ExitStack() as e1:
            WIN = sb(e1, [128, 8, INC], BF16)
            WUQ = sb(e1, [128, 3, 4, 192], BF16)
            WUKT = sb(e1, [128, 4, 256], BF16)
            WUV = sb(e1, [128, 2, 512], BF16)
            WO = sb(e1, [128, 8, D], BF16)
            WA2 = sb(e1, [16, 256], BF16)
            BAN = sb(e1, [128, 2], F32)
            GGLA = sb(e1, [128, 4, 128], F32)
            GQ = sb(e1, [128, 384], F32)
            GKV = sb(e1, [128, 256], F32)
            GMO = sb(e1, [128, 128], F32)
            GMSKP = sb(e1, [128, 128], F32)
            GMSKS = sb(e1, [NTS, NTS], F32)
            AMP = sb(e1, [128, 128], BF16)
            AMS = sb(e1, [NTS, NS * NTS], BF16)
            SEQSEL = sb(e1, [NTS, NS], F32)
            bW = MBuf("w1")
            for k in range(8):
                for h2 in range(2):
                    dma("pool", WIN[:, k, h2 * 1128:(h2 + 1) * 1128],
                        w_in[k * 128:(k + 1) * 128, h2 * 1128:(h2 + 1) * 1128], writes=[bW.new()])
            for c in range(3):
                dma("pool", WUQ[:, c, :, :].rearrange("p h n -> p (h n)"), w_uq[c * 128:(c + 1) * 128, :], writes=[bW.new()])
            for c in range(2):
                dma("pool", WUV[:, c, :], w_uv[c * 128:(c + 1) * 128, :], writes=[bW.new()])
            for k in range(8):
                dma("pool", WO[:, k, :], w_o[k * 128:(k + 1) * 128, :], writes=[bW.new()])
            dma("pool", WA2[:], w_a2[:], writes=[bW.new()])
            dma("pool", AMP[:], amask_p[:], writes=[bW.new()])
            dma("pool", AMS[:], amask_s[:], writes=[bW.new()])
            dma("sp", BAN[:], b_a[:], writes=[bW.new()])
            for h in range(4):
                dma("sp", GGLA[:, h, :], g_gla_out.partition_broadcast(128).rearrange("p a f -> p (a f)"), writes=[bW.new()])
            dma("sp", GQ[:], g_mla_q.partition_broadcast(128).rearrange("p a f -> p (a f)"), writes=[bW.new()])
            dma("sp", GKV[:], g_mla_kv.partition_broadcast(128).rearrange("p a f -> p (a f)"), writes=[bW.new()])
            dma("sp", GMO[:], g_mla_out.partition_broadcast(128).rearrange("p a f -> p (a f)"), writes=[bW.new()])
            dma("sp", GMSKP[:], gmask_p[:], writes=[bW.new()])
            dma("sp", GMSKS[:], gmask_s[:], writes=[bW.new()])
            dma("sp", SEQSEL[:], seqsel[:], writes=[bW.new()])
            op("dve", lambda e: e.tensor_scalar(out=BAN[:], in0=BAN[:], scalar1=-1.0, scalar2=None, op0=ALU.mult),
               reads=[bW], writes=[bW.new()])

            with ExitStack() as ep:
                PF = Pool_([ps(ep, [128, 512], F32) for _ in range(3)], "pf")
                PB = Pool_([ps(ep, [128, 1024], BF16) for _ in range(2)], "pb")
                PFS = Pool_([ps(ep, [128, 512], F32) for _ in range(2)], "pfs")
                PBS = Pool_([ps(ep, [128, 1024], BF16) for _ in range(1)], "pbs")

                with ExitStack() as e0:
                    WKF = sb(e0, [128, 2, 512], F32)
                    CALL = sb(e0, [5, D], F32)
                    CT = sb(e0, [128, 8, 5], BF16)
                    BADA = sb(e0, [5, 6 * D], F32)
                    MOD = sb(e0, [5, 6 * D], F32)
                    GT = sb(e0, [128, 2, D], F32)
                    GTT = sb(e0, [128, 4, 8], F32)
                    MODT = sb(e0, [128, 48, 5], F32)
                    SELP = sb(e0, [5, 128], F32)
                    SELS = sb(e0, [5, NTS], F32)
                    WA = [sb(e0, [128, 8, 512], BF16) for _ in range(4)]
                    bWA = [Buf("wa%d" % i) for i in range(4)]
                    bS = MBuf("setup")
                    bMOD = Buf("mod")
                    for c in range(2):
                        dma("sp", WKF[:, c, :], w_uk[c * 128:(c + 1) * 128, :], writes=[bS.new()])
                    for h in range(4):
                        pt_, bpt = PF.get()
                        for c in range(2):
                            op("pe", lambda e, c=c, h=h: e.transpose(out=pt_[:, c * 128:(c + 1) * 128],
                                                                    in_=WKF[:, c, h * 128:(h + 1) * 128], identity=IDF[:]),
                               reads=[bS, bID], writes=[bpt])
                        op("act", lambda e, h=h: e.copy(out=WUKT[:, h, :], in_=pt_[:, 0:256]), reads=[bpt], writes=[bW.new()])
                    dma("sp", CALL[:], c_all[:], writes=[bS.new()])
                    dma("sp", BADA[:], b_ada.partition_broadcast(5).rearrange("p a f -> p (a f)"), writes=[bS.new()])
                    dma("sp", SELP[:], sel_p[:], writes=[bS.new()])
                    dma("sp", SELS[:], sel_s[:], writes=[bS.new()])
                    for i, g in enumerate([g_post_mix, g_post_ffn]):
                        dma("sp", GT[:, i, :], g.partition_broadcast(128).rearrange("p a f -> p (a f)"), writes=[bS.new()])
                    dma("sp", GTT[:], gvt[:], writes=[bS.new()])
                    op("act", lambda e: e.activation(out=CALL[:], in_=CALL[:], func=AF.Silu), reads=[bS], writes=[bS.new()])
                    pt_, bpt = PF.get()
                    for k in range(8):
                        op("pe", lambda e, k=k: e.transpose(out=pt_[:, k * 5:(k + 1) * 5], in_=CALL[:, k * 128:(k + 1) * 128],
                                                            identity=IDF[0:5, 0:5]), reads=[bS, bID], writes=[bpt])
                    op("dve", lambda e: e.tensor_copy(out=CT[:].rearrange("p k c -> p (k c)"), in_=pt_[:, 0:40]),
                       reads=[bpt], writes=[bS.new()])
                    for cc in range(12):
                        wa, bwa = WA[cc % 4], bWA[cc % 4]
                        dma("pool", wa[:], w_ada[:, cc * 512:(cc + 1) * 512].rearrange("(k p) c -> p k c", p=128),
                            writes=[bwa])
                        pm, bpm = PF.get()

                        def mm(e, wa=wa, pm=pm):
                            for k in range(8):
                                r = e.matmul(pm[0:5, :], lhsT=CT[:, k, :], rhs=wa[:, k, :], start=(k == 0), stop=(k == 7))
                            return r
                        op("pe", mm, reads=[bS, bwa], writes=[bpm])
                        op("dve", lambda e, pm=pm, cc=cc: e.tensor_tensor(out=MOD[:, cc * 512:(cc + 1) * 512], in0=pm[0:5, :],
                                                                          in1=BADA[:, cc * 512:(cc + 1) * 512], op=ALU.add),
                           reads=[bpm, bS], writes=[bMOD])
                    for (SEL, n, MT, bMT) in ((SELP, 128, MODP, bMODP), (SELS, NTS, MODS, bMODS)):
                        for mi, part in enumerate((2, 5)):
                            for hf in range(2):
                                pm, bpm = PF.get()
                                op("pe", lambda e, pm=pm, SEL=SEL, n=n, part=part, hf=hf: e.matmul(
                                    pm[0:n, :], lhsT=SEL[:, 0:n], rhs=MOD[:, part * D + hf * 512: part * D + (hf + 1) * 512],
                                    start=True, stop=True), reads=[bS, bMOD], writes=[bpm])
                                cs = slice(hf * 512, (hf + 1) * 512)
                                op("dve", lambda e, pm=pm, n=n, MT=MT, mi=mi, cs=cs: e.tensor_tensor(
                                    out=MT[0:n, mi, cs], in0=pm[0:n, :], in1=GT[0:n, mi, cs], op=ALU.mult),
                                   reads=[bpm, bS], writes=[bMT])
                    pm, bpm = PF.get()

                    def tmod(e, pm=pm):
                        for j in range(48):
                            r = e.transpose(out=pm[:, j * 5:(j + 1) * 5], in_=MOD[0:5, j * 128:(j + 1) * 128], identity=IDF[0:5, 0:5])
                        return r
                    op("pe", tmod, reads=[bMOD, bID], writes=[bpm])
                    op("act", lambda e, pm=pm: e.copy(out=MODT[:].rearrange("p j c -> p (j c)"), in_=pm[:, 0:240]), reads=[bpm], writes=[bS.new()])
                    for (ai, scp, shp, gi) in ((0, 1, 0, 0), (2, 4, 3, 2)):
                        for k in range(8):
                            op("dve", lambda e, ai=ai, scp=scp, gi=gi, k=k: e.tensor_scalar(
                                out=ABT[:, ai, k, :], in0=MODT[:, scp * 8 + k, :], scalar1=1.0, scalar2=GTT[:, gi, k:k + 1],
                                op0=ALU.add, op1=ALU.mult), reads=[bS], writes=[bABT])
                        op("dve", lambda e, ai=ai, shp=shp: e.tensor_copy(out=ABT[:, ai + 1, :, :], in_=MODT[:, shp * 8:(shp + 1) * 8, :]),
                           reads=[bS], writes=[bABT])
                    tk.barrier()
                    if stop == "setup":
                        raise _Stop(nc)
                XT = [sb(e1, [128, D], F32) for _ in range(2)]
                bXT = [Buf("x0"), Buf("x1")]
                SQ = sb(e1, [128, D], BF16)
                bSQ = Buf("sq")
                HB = sb(e1, [128, D], BF16)
                bHB = Buf("hb")
                HT = sb(e1, [128, 8, 128], BF16)
                bHT = Buf("ht")
                ST = sb(e1, [128, 16], F32)
                bST = Buf("st")
                GRT = sb(e1, [16, 128], BF16)
                bGRT = Buf("grt")
                E1 = sb(e1, [128, 2, 128], F32)
                L1 = sb(e1, [128, 2, 128], F32)
                CL = sb(e1, [128, 2, 128], F32)
                EB = sb(e1, [128, 2, 128], F32)
                ENB = sb(e1, [128, 2, 128], F32)
                ONES = sb(e1, [128, 128], F32)
                bGATE = Buf("gate")
                bEB = Buf("eb")
                QE = sb(e1, [128, 2, 128], BF16)
                KE = sb(e1, [128, 2, 128], BF16)
                bQK = Buf("qk")
                QEZ = [sb(e1, [128, 2, 128], BF16) for _ in range(2)]
                HM = sb(e1, [128, 2], F32)
                KET = sb(e1, [128, 2, 128], BF16)
                bKET = Buf("ket")
                KETM = sb(e1, [NTS, 128], BF16)
                bKETM = Buf("ketm")
                ATT = sb(e1, [128, 4, 128], BF16)
                bATT = Buf("att")
                VA = sb(e1, [128, 512], BF16)
                bVA = Buf("va")
                GSR = sb(e1, [128, 512], F32)
                bGSR = Buf("gsr")
                SST = sb(e1, [128, 2, 128], F32)
                SSTB = sb(e1, [128, 2, 128], BF16)
                bSST = Buf("sst")
                SSS = sb(e1, [128, 2, NS, 128], F32)
                SSSB = sb(e1, [128, 2, NS, 128], BF16)
                bSSS = Buf("sss")
                OIN = sb(e1, [NTS, 4, 128], F32)
                bOIN = Buf("oin")
                MIXCP = sb(e1, [128, D], BF16)
                bMIXCP = Buf("mixcp")
                MIXCS = sb(e1, [NTS, D], BF16)
                bMIXCS = Buf("mixcs")
                XS = sb(e1, [NTS, D], F32)
                bXS = Buf("xs")
                MIXT = sb(e1, [128, 8, 128], BF16)
                bMIXT = Buf("mixt")
                CQN = sb(e1, [128, 384], BF16)
                bCQN = Buf("cqn")
                CQT = sb(e1, [128, 3, 128], BF16)
                bCQT = Buf("cqt")
                QNT = sb(e1, [128, 4, 128], BF16)
                bQNT = Buf("qnt")
                QRR = sb(e1, [128, 4, 64], BF16)
                bQRR = Buf("qrr")
                RT = sb(e1, [128, 4, 4, 32], F32)
                bRT = Buf("rt")
                QRT = sb(e1, [64, 4, 128], BF16)
                bQRT = Buf("qrt")
                QLT = sb(e1, [128, 2, 4, 128], BF16)
                bQLT = Buf("qlt")
                QLS = sb(e1, [128, NS, 2, NTS], BF16)
                QRS = sb(e1, [64, NS, NTS], BF16)
                bQS = Buf("qs")
                CKV = sb(e1, [128, 256], F32)
                bCKV = Buf("ckv")
                KRO = sb(e1, [128, 64], F32)
                KRB = sb(e1, [128, 64], BF16)
                bKRO = Buf("kro")
                COS = sb(e1, [128, 128], F32)
                SIN = sb(e1, [128, 128], F32)
                bCS = Buf("cs")
                CKVT = sb(e1, [128, 2, T + NTS], BF16)
                KRT = sb(e1, [64, T + NTS], BF16)
                VTOK = sb(e1, [128, NB + 1, 256], BF16)
                bKVS = [Buf("kvs%d" % i) for i in range(NB + 1)]
                def mk_fs(tag, rows, pf, pb):
                    return dict(FMS=sb(e1, [128, 4, 4], F32), FMT=sb(e1, [128, 4, 2, 8], F32),
                                bM=[Buf("fm_m%s%d" % (tag, i)) for i in range(4)], bL=[Buf("fm_l%s%d" % (tag, i)) for i in range(4)],
                                bFT=[[Buf("fm_t%s%d_%d" % (tag, i, j)) for j in range(2)] for i in range(4)], fpar=[0, 0, 0, 0],
                                OACC=sb(e1, [rows, 4, 256], F32), bOACC=[Buf("oacc%s%d" % (tag, i)) for i in range(4)], pf=pf, pb=pb,
                                PP=sb(e1, [rows, 512], BF16), bPP=Buf("pp" + tag), PT=sb(e1, [128, 4, rows], BF16), bPT=Buf("pt" + tag))
                FSP = mk_fs("p", 128, PF, PB)
                FSS = mk_fs("s", NTS, PFS, PBS)
                ONP = sb(e1, [128, 4, 256], BF16)
                bONP = Buf("onp")
                ONS = sb(e1, [NTS, 4, 256], BF16)
                bONS = Buf("ons")
                OT = sb(e1, [128, 2, 4, 128], BF16)
                bOT = Buf("ot")
                X1 = [sb(e1, [128, D], F32) for _ in range(2)]
                bX1 = [Buf("x1a"), Buf("x1b")]
                TMP = sb(e1, [128, 512], F32)
                bTMP = Buf("tmp")
                LATG = [sb(e1, [128, 8, 256], BF16) for _ in range(3)]
                KRG = [sb(e1, [128, 8, 64], BF16) for _ in range(3)]
                bG = [Buf("g%d" % i) for i in range(3)]
                KTS = [sb(e1, [128, 2, 512], BF16) for _ in range(2)]
                KRTS = [sb(e1, [64, 512], BF16) for _ in range(2)]
                bKTS = [Buf("kts0"), Buf("kts1")]
                PTB = sb(e1, [128, NS * NU], I32)
                IDX = sb(e1, [128, NS * NU], I32)
                LO = sb(e1, [128, 1], F32)
                bIDX = Buf("idx")

                op("pool", lambda e: e.memset(ONES[:], 1.0), writes=[bGATE])
                op("pool", lambda e: e.memset(HM[:], 0.0), writes=[bCONST])
                op("pool", lambda e: e.memset(HM[0:64, 0:1], 1.0), writes=[bCONST])
                op("pool", lambda e: e.memset(HM[64:128, 1:2], 1.0), writes=[bCONST])
                op("pool", lambda e: e.memset(SST[:], 0.0), writes=[bSST])
                op("pool", lambda e: e.memset(SSTB[:], 0.0), writes=[bSST])
                dma("sp", PTB[:], ptb[:], writes=[bIDX])
                dma("sp", LO[:], lo16[:], writes=[bIDX])
                op("dve", lambda e: e.tensor_scalar(out=IDX[:], in0=PTB[:], scalar1=16.0, scalar2=LO[:, 0:1],
                                                    op0=ALU.mult, op1=ALU.add), reads=[bIDX], writes=[bIDX])
                for s in range(NS):
                    for g in range(2):
                        dma("sp", SSS[:, g, s, :], sgla[s, g, :, :], writes=[bSSS])
                op("dve", lambda e: e.tensor_copy(out=SSSB[:].rearrange("p g s v -> p (g s v)"),
                                                  in_=SSS[:].rearrange("p g s v -> p (g s v)")), reads=[bSSS], writes=[bSSS])

                def flash_qk(n, h, qparts, kparts, nk, vblocks, mask, first, extra=(), FS=None):
                    sp_, bsp = FS["pf"].get()
                    rb = [b for _, b in qparts] + [b for _, b in kparts] + list(extra)

                    def qk(e):
                        r = None
                        np_ = len(qparts)
                        for i in range(np_):
                            r = e.matmul(sp_[0:n, 0:nk], lhsT=qparts[i][0], rhs=kparts[i][0], start=(i == 0),
                                         stop=(i == np_ - 1 and mask is None))
                        if mask is not None:
                            r = e.matmul(sp_[0:n, mask[2]:mask[2] + mask[1].shape[-1]], lhsT=mask[0], rhs=mask[1],
                                         start=False, stop=True)
                        return r
                    op("pe", qk, reads=rb + (list(mask[3]) if mask else []), writes=[bsp])
                    return dict(n=n, h=h, nk=nk, vblocks=vblocks, first=first, sp=sp_, bsp=bsp, FS=FS)

                def flash_rest(c):
                    n, h, nk, vblocks, first, sp_, bsp = c["n"], c["h"], c["nk"], c["vblocks"], c["first"], c["sp"], c["bsp"]
                    FS = c["FS"]
                    OACC, bOACC = FS["OACC"], FS["bOACC"]
                    par = FS["fpar"][h]
                    FS["fpar"][h] ^= 1
                    st = FS["FMS"][0:n, h, :]
                    ft = FS["FMT"][0:n, h, par, :]
                    bft = FS["bFT"][h][par]
                    bm, bl = FS["bM"][h], FS["bL"][h]
                    fto = FS["FMT"][0:n, h, par ^ 1, :]
                    bfto = FS["bFT"][h][par ^ 1]
                    op("dve", lambda e: e.reduce_max(out=ft[:, 0:1], in_=sp_[0:n, 0:nk], axis=AX.X), reads=[bsp], writes=[bft])
                    if first:
                        op("dve", lambda e: e.tensor_scalar(out=ft[:, 2:3], in0=ft[:, 0:1], scalar1=-SCALE, scalar2=None,
                                                            op0=ALU.mult), reads=[bft], writes=[bft])
                    else:
                        op("dve", lambda e: e.tensor_scalar(out=ft[:, 2:3], in0=ft[:, 0:1], scalar1=-SCALE, scalar2=fto[:, 2:3],
                                                            op0=ALU.mult, op1=ALU.min), reads=[bft, bfto], writes=[bft])
                        op("act", lambda e: e.activation(out=ft[:, 3:4], in_=fto[:, 2:3], func=AF.Exp, bias=ft[:, 2:3], scale=-1.0),
                           reads=[bft, bfto], writes=[bft])
                    pp, bpp = FS["PP"], FS["bPP"]
                    op("act", lambda e: e.activation(out=pp[0:n, 0:nk], in_=sp_[0:n, 0:nk], func=AF.Exp, bias=ft[:, 2:3],
                                                     scale=SCALE, accum_out=ft[:, 4:5]), reads=[bsp, bft], writes=[bpp, bft])
                    if first:
                        op("dve", lambda e: e.tensor_copy(out=st[:, 1:2], in_=ft[:, 4:5]), reads=[bft], writes=[bl])
                    else:
                        op("dve", lambda e: e.scalar_tensor_tensor(out=st[:, 1:2], in0=st[:, 1:2], scalar=ft[:, 3:4],
                                                                   in1=ft[:, 4:5], op0=ALU.mult, op1=ALU.add),
                           reads=[bft, bl], writes=[bl])
                    yield
                    tp, btp = FS["pb"].get()
                    nblk = len(vblocks)

                    def tps(e):
                        r = None
                        c0 = 0
                        for j in range(nblk):
                            kb = vblocks[j][1]
                            r = e.transpose(out=tp[0:kb, j * 128:j * 128 + n], in_=pp[0:n, c0:c0 + kb], identity=IDB[0:n, 0:n])
                            c0 += kb
                        return r
                    op("pe", tps, reads=[bpp, bID], writes=[btp])
                    pt, bpt = FS["PT"], FS["bPT"]
                    kb0 = vblocks[0][1]
                    op("act", lambda e: e.copy(out=pt[0:kb0, 0:nblk, 0:n],
                                               in_=tp[0:kb0, 0:nblk * 128].rearrange("p (j c) -> p j c", c=128)[:, :, 0:n]),
                       reads=[btp], writes=[bpt])
                    yield
                    oc, boc = FS["pf"].get()

                    def pv(e):
                        r = None
                        for j in range(nblk):
                            kb = vblocks[j][1]
                            r = e.matmul(oc[0:n, 0:256], lhsT=pt[0:kb, j, 0:n], rhs=vblocks[j][0], start=(j == 0), stop=(j == nblk - 1))
                        return r
                    op("pe", pv, reads=[bpt] + [v[2] for v in vblocks], writes=[boc])
                    if first:
                        op("act", lambda e: e.copy(out=OACC[0:n, h, :], in_=oc[0:n, 0:256]), reads=[boc], writes=[bOACC[h]])
                    else:
                        op("dve", lambda e: e.scalar_tensor_tensor(out=OACC[0:n, h, :], in0=OACC[0:n, h, :], scalar=ft[:, 3:4],
                                                                   in1=oc[0:n, 0:256], op0=ALU.mult, op1=ALU.add),
                           reads=[boc, bft, bOACC[h]], writes=[bOACC[h]])

                def flash_final(n, h, out_ap, bout, FS):
                    st = FS["FMS"][0:n, h, :]
                    bL_, OACC_, bOACC_ = FS["bL"], FS["OACC"], FS["bOACC"]
                    op("dve", lambda e: e.reciprocal(out=st[:, 2:3], in_=st[:, 1:2]), reads=[bL_[h]], writes=[bL_[h]])
                    op("dve", lambda e: e.tensor_scalar(out=out_ap, in0=OACC_[0:n, h, :], scalar1=st[:, 2:3], scalar2=None,
                                                        op0=ALU.mult), reads=[bL_[h], bOACC_[h]], writes=[bout])

                def run_pipelined(items, lookahead=True):
                    if not lookahead:
                        for it in items:
                            cur = it()
                            yield
                            yield from flash_rest(cur)
                            yield
                        return
                    nxt = items[0]() if items else None
                    yield
                    for k in range(len(items)):
                        cur = nxt
                        nxt = items[k + 1]() if k + 1 < len(items) else None
                        yield
                        yield from flash_rest(cur)
                        yield

                def load_x(i):
                    xt, bx = XT[i % 2], bXT[i % 2]
                    if i < NB:
                        dma("sp", xt[:], x_p[i * 128:(i + 1) * 128, :], writes=[bx])
                    else:
                        dma("sp", XS[0:NTS, :], x_s[:], writes=[bXS])

                def chk(name):
                    if stop == name:
                        tk.barrier()
                        raise _Stop(nc)

                def block(i, phase="full"):
                    isP = i < NB
                    n = 128 if isP else NTS
                    xt, bx = (XT[i % 2], bXT[i % 2]) if isP else (XS, bXS)
                    MIXC, bMIXC = (MIXCP, bMIXCP) if isP else (MIXCS, bMIXCS)
                    ON, bON = (ONP, bONP) if isP else (ONS, bONS)
                    if phase != "tail":
                        yield from block_A(i, isP, n, xt, bx, MIXC, bMIXC)
                    if phase == "A":
                        return
                    if phase == "full":
                        yield from block_attn(i, n, ON, bON)
                    yield from block_tail(i, isP, n, xt, bx, MIXC, bMIXC, ON, bON)

                def block_A(i, isP, n, xt, bx, MIXC, bMIXC):
                    MT, bMT = (MODP, bMODP) if isP else (MODS, bMODS)
                    col0 = i * 128
                    if isP:
                        dma("sp", COS[:], cos_p[i * 128:(i + 1) * 128, :], writes=[bCS])
                        dma("sp", SIN[:], sin_p[i * 128:(i + 1) * 128, :], writes=[bCS])
                    else:
                        dma("sp", COS[0:n, :], cos_s[:], writes=[bCS])
                        dma("sp", SIN[0:n, :], sin_s[:], writes=[bCS])
                    op("act", lambda e: e.activation(out=SQ[0:n, :], in_=xt[0:n, :], func=AF.Square, accum_out=ST[0:n, 0:1]),
                       reads=[bx], writes=[bSQ, bST])
                    rstd_from_ss(n, ST[0:n, 0:1], bST, ST[0:n, 1:2], bST, 1.0 / D)
                    op("dve", lambda e: e.tensor_scalar(out=HB[0:n, :], in0=xt[0:n, :], scalar1=ST[0:n, 1:2], scalar2=None, op0=ALU.mult),
                       reads=[bx, bST], writes=[bHB])
                    tp, btp = PB.get()

                    def tph(e):
                        for k in range(8):
                            r = e.transpose(out=tp[:, k * 128:k * 128 + n], in_=HB[0:n, k * 128:(k + 1) * 128], identity=IDB[0:n, 0:n])
                        return r
                    op("pe", tph, reads=[bHB, bID], writes=[btp])
                    for k in range(8):
                        if isP:
                            op("act", lambda e, k=k: e.activation(out=HT[:, k, 0:n], in_=tp[:, k * 128:k * 128 + n], func=AF.Identity,
                                                                  scale=ABT[:, 0, k, 0:1], bias=ABT[:, 1, k, 0:1]),
                               reads=[btp, bABT], writes=[bHT])
                        else:
                            for s_ in range(NS):
                                op("act", lambda e, k=k, s_=s_: e.activation(
                                    out=HT[:, k, s_ * TS:(s_ + 1) * TS], in_=tp[:, k * 128 + s_ * TS:k * 128 + (s_ + 1) * TS], func=AF.Identity,
                                    scale=ABT[:, 0, k, 1 + s_:2 + s_], bias=ABT[:, 1, k, 1 + s_:2 + s_]), reads=[btp, bABT], writes=[bHT])
                    chk("ht")
                    yield
                    fg, bfg = PF.get()

                    def mfg(e):
                        for k in range(8):
                            r = e.matmul(fg[0:16, 0:n], lhsT=WIN[:, k, C_G:C_G + 16], rhs=HT[:, k, 0:n], start=(k == 0), stop=(k == 7))
                        return r
                    op("pe", mfg, reads=[bW, bHT], writes=[bfg])
                    op("act", lambda e: e.copy(out=GRT[:, 0:n], in_=fg[0:16, 0:n]), reads=[bfg], writes=[bGRT])
                    pg, bpg = PF.get()

                    def mpg(e):
                        for c in range(2):
                            r = e.matmul(pg[:, c * 128:c * 128 + n], lhsT=WA2[:, c * 128:(c + 1) * 128], rhs=GRT[:, 0:n], start=True, stop=True)
                        return r
                    op("pe", mpg, reads=[bW, bGRT], writes=[bpg])
                    for c in range(2):
                        op("act", lambda e, c=c: e.activation(out=E1[:, c, 0:n], in_=pg[:, c * 128:c * 128 + n], func=AF.Exp,
                                                              bias=BAN[:, c:c + 1], scale=-1.0), reads=[bpg, bW], writes=[bGATE])
                    op("act", lambda e: e.activation(out=L1[:, :, 0:n], in_=E1[:, :, 0:n], func=AF.Ln, bias=ONEC[:, :], scale=1.0),
                       reads=[bGATE, bCONST], writes=[bGATE])
                    chk("gate")
                    yield
                    segs = [(0, n)] if isP else [(s * TS, TS) for s in range(NS)]
                    for c in range(2):
                        for (a, ln) in segs:
                            op("dve", lambda e, c=c, a=a, ln=ln: e.tensor_tensor_scan(
                                out=CL[:, c, a:a + ln], data0=ONES[:, 0:ln], data1=L1[:, c, a:a + ln], initial=0.0,
                                op0=ALU.mult, op1=ALU.add), reads=[bGATE], writes=[bGATE])
                    op("act", lambda e: e.activation(out=EB[:, :, 0:n], in_=CL[:, :, 0:n], func=AF.Exp, scale=-1.0 / 16.0),
                       reads=[bGATE], writes=[bEB])
                    op("act", lambda e: e.activation(out=ENB[:, :, 0:n], in_=CL[:, :, 0:n], func=AF.Exp, scale=1.0 / 16.0),
                       reads=[bGATE], writes=[bEB])
                    yield
                    fq, bfq = PF.get()

                    def mfq(e):
                        for gi, c0 in enumerate((C_Q, C_Q + 128, C_K, C_K + 128)):
                            for k in range(8):
                                r = e.matmul(fq[:, gi * 128:gi * 128 + n], lhsT=WIN[:, k, c0:c0 + 128], rhs=HT[:, k, 0:n],
                                             start=(k == 0), stop=(k == 7))
                        return r
                    op("pe", mfq, reads=[bW, bHT], writes=[bfq])
                    fq3 = fq[:, :].rearrange("p (g c) -> p g c", c=128)
                    op("dve", lambda e: e.scalar_tensor_tensor(out=QE[:, :, 0:n], in0=fq3[:, 0:2, 0:n], scalar=0.125,
                                                               in1=EB[:, :, 0:n], op0=ALU.mult, op1=ALU.mult),
                       reads=[bfq, bEB], writes=[bQK])
                    op("dve", lambda e: e.tensor_tensor(out=KE[:, :, 0:n], in0=fq3[:, 2:4, 0:n], in1=ENB[:, :, 0:n], op=ALU.mult),
                       reads=[bfq, bEB], writes=[bQK])
                    for hh in range(2):
                        op("dve", lambda e, hh=hh: e.tensor_scalar(out=QEZ[hh][:, :, 0:n], in0=QE[:, :, 0:n], scalar1=HM[:, hh:hh + 1], scalar2=None,
                                                                   op0=ALU.mult), reads=[bQK, bCONST], writes=[bQK])
                    chk("qke")
                    yield
                    def tokproj(c0, w):
                        pz, bpz = PF.get()

                        def mz(e):
                            for k in range(8):
                                r = e.matmul(pz[0:n, 0:w], lhsT=HT[:, k, 0:n], rhs=WIN[:, k, c0:c0 + w], start=(k == 0), stop=(k == 7))
                            return r
                        op("pe", mz, reads=[bW, bHT], writes=[bpz])
                        return pz, bpz
                    zv, bzv = tokproj(C_V, 512)
                    op("act", lambda e: e.copy(out=VA[0:n, :], in_=zv[0:n, :]), reads=[bzv], writes=[bVA])
                    zr, bzr = tokproj(C_R, 512)
                    op("act", lambda e: e.activation(out=GSR[0:n, :], in_=zr[0:n, :], func=AF.Silu), reads=[bzr], writes=[bGSR])
                    op("dve", lambda e: e.tensor_tensor(out=GSR[0:n, :], in0=GSR[0:n, :],
                                                         in1=GGLA[0:n, :, :].rearrange("p h v -> p (h v)"), op=ALU.mult),
                       reads=[bGSR, bW], writes=[bGSR])
                    yield
                    zq, bzq = tokproj(C_CQ, 384)
                    op("act", lambda e: e.activation(out=SQ[0:n, 0:384], in_=zq[0:n, 0:384], func=AF.Square, accum_out=ST[0:n, 2:3]),
                       reads=[bzq], writes=[bSQ, bST])
                    rstd_from_ss(n, ST[0:n, 2:3], bST, ST[0:n, 3:4], bST, 1.0 / 384)
                    op("dve", lambda e: e.scalar_tensor_tensor(out=CQN[0:n, :], in0=zq[0:n, 0:384], scalar=ST[0:n, 3:4],
                                                               in1=GQ[0:n, :], op0=ALU.mult, op1=ALU.mult),
                       reads=[bzq, bST, bW], writes=[bCQN])
                    yield
                    zk, bzk = tokproj(C_KV, 320)
                    op("act", lambda e: e.activation(out=SQ[0:n, 0:256], in_=zk[0:n, 0:256], func=AF.Square, accum_out=ST[0:n, 4:5]),
                       reads=[bzk], writes=[bSQ, bST])
                    rstd_from_ss(n, ST[0:n, 4:5], bST, ST[0:n, 5:6], bST, 1.0 / 256)
                    op("dve", lambda e: e.scalar_tensor_tensor(out=CKV[0:n, :], in0=zk[0:n, 0:256], scalar=ST[0:n, 5:6],
                                                               in1=GKV[0:n, :], op0=ALU.mult, op1=ALU.mult),
                       reads=[bzk, bST, bW], writes=[bCKV])
                    dma("sp", (lat_p[i * 128:(i + 1) * 128, :] if isP else lat_s[:]), CKV[0:n, :], reads=[bCKV])
                    bkv = bKVS[i]
                    op("act", lambda e: e.copy(out=VTOK[0:n, i, :], in_=CKV[0:n, :]), reads=[bCKV], writes=[bkv])
                    chk("tokproj")
                    x1 = zk[0:n, 256:288]
                    x2 = zk[0:n, 288:320]
                    op("dve", lambda e: e.tensor_tensor(out=RT[0:n, 0, 0, :], in0=x1, in1=COS[0:n, 0:32], op=ALU.mult), reads=[bzk, bCS], writes=[bRT])
                    op("dve", lambda e: e.tensor_tensor(out=RT[0:n, 1, 0, :], in0=x2, in1=SIN[0:n, 0:32], op=ALU.mult), reads=[bzk, bCS], writes=[bRT])
                    op("dve", lambda e: e.tensor_tensor(out=RT[0:n, 2, 0, :], in0=x1, in1=SIN[0:n, 0:32], op=ALU.mult), reads=[bzk, bCS], writes=[bRT])
                    op("dve", lambda e: e.tensor_tensor(out=RT[0:n, 3, 0, :], in0=x2, in1=COS[0:n, 0:32], op=ALU.mult), reads=[bzk, bCS], writes=[bRT])
                    op("dve", lambda e: e.tensor_tensor(out=KRO[0:n, 0:32], in0=RT[0:n, 0, 0, :], in1=RT[0:n, 1, 0, :], op=ALU.subtract),
                       reads=[bRT], writes=[bKRO])
                    op("dve", lambda e: e.tensor_tensor(out=KRO[0:n, 32:64], in0=RT[0:n, 2, 0, :], in1=RT[0:n, 3, 0, :], op=ALU.add),
                       reads=[bRT], writes=[bKRO])
                    dma("sp", (kr_p[i * 128:(i + 1) * 128, :] if isP else kr_s[:]), KRO[0:n, :], reads=[bKRO])
                    op("act", lambda e: e.copy(out=KRB[0:n, :], in_=KRO[0:n, :]), reads=[bKRO], writes=[bKRO])
                    tp, btp = PB.get()

                    def tpk(e):
                        for c in range(2):
                            r = e.transpose(out=tp[:, c * 128:c * 128 + n], in_=VTOK[0:n, i, c * 128:(c + 1) * 128], identity=IDB[0:n, 0:n])
                        r = e.transpose(out=tp[0:64, 256:256 + n], in_=KRB[0:n, :], identity=IDB[0:n, 0:n])
                        return r
                    op("pe", tpk, reads=[bkv, bKRO, bID], writes=[btp])
                    op("dve", lambda e: e.tensor_copy(out=CKVT[:, :, col0:col0 + n],
                                                      in_=tp[:, 0:256].rearrange("p (c k) -> p c k", k=128)[:, :, 0:n]),
                       reads=[btp], writes=[bkv])
                    op("act", lambda e: e.copy(out=KRT[:, col0:col0 + n], in_=tp[0:64, 256:256 + n]), reads=[btp], writes=[bkv])
                    chk("kt")
                    yield
                    gmask = GMSKP if isP else GMSKS
                    for g in range(2):
                        tpe, btpe = PB.get()
                        op("pe", lambda e, g=g: e.transpose(out=tpe[0:n, 0:128], in_=KE[:, g, 0:n], identity=IDB[:, :]),
                           reads=[bQK, bID], writes=[btpe])
                        op("act", lambda e, g=g: e.copy(out=KET[0:n, g, :], in_=tpe[0:n, 0:128]), reads=[btpe], writes=[bKET])
                    pa, bpa = PF.get()

                    def matt(e):
                        for g in range(2):
                            for hh in range(2):
                                h = 2 * g + hh
                                psl = slice(hh * 64, (hh + 1) * 64)
                                r = e.matmul(pa[0:n, h * 128:h * 128 + n], lhsT=KE[:, g, 0:n], rhs=QEZ[hh][:, g, 0:n], start=True, stop=True)
                        return r
                    op("pe", matt, reads=[bQK], writes=[bpa])
                    for h in range(4):
                        op("dve", lambda e, h=h: e.tensor_tensor(out=ATT[0:n, h, 0:n], in0=pa[0:n, h * 128:h * 128 + n],
                                                                 in1=gmask[0:n, 0:n], op=ALU.mult), reads=[bpa, bW], writes=[bATT])
                    po, bpo = PF.get()
                    if isP:
                        def mo(e):
                            for g in range(2):
                                for hh in range(2):
                                    h = 2 * g + hh
                                    psl = slice(hh * 64, (hh + 1) * 64)
                                    e.matmul(po[0:n, h * 128:(h + 1) * 128], lhsT=ATT[0:n, h, 0:n], rhs=VA[0:n, h * 128:(h + 1) * 128],
                                             start=True, stop=False)
                                    r = e.matmul(po[0:n, h * 128:(h + 1) * 128], lhsT=QEZ[hh][:, g, 0:n], rhs=SSTB[:, g, :],
                                                 start=False, stop=True)
                            return r
                        op("pe", mo, reads=[bATT, bVA, bQK, bSST], writes=[bpo])
                        oin = None
                    else:
                        def mo(e):
                            for h in range(4):
                                r = e.matmul(po[0:n, h * 128:(h + 1) * 128], lhsT=ATT[0:n, h, 0:n], rhs=VA[0:n, h * 128:(h + 1) * 128],
                                             start=True, stop=True)
                            return r
                        op("pe", mo, reads=[bATT, bVA], writes=[bpo])
                        op("act", lambda e: e.copy(out=OIN[0:n, :, :].rearrange("p h v -> p (h v)"), in_=po[0:n, :]), reads=[bpo], writes=[bOIN])
                        for g in range(2):
                            for hh in range(2):
                                h = 2 * g + hh
                                pi_, bpi = PF.get()
                                op("pe", lambda e, g=g, hh=hh, pi_=pi_: e.matmul(
                                    pi_[0:n, 0:512], lhsT=QEZ[hh][:, g, 0:n], rhs=SSSB[:, g, :, :].rearrange("p s v -> p (s v)"),
                                    start=True, stop=True), reads=[bQK, bSSS], writes=[bpi])
                                for s in range(NS):
                                    op("dve", lambda e, pi_=pi_, h=h, s=s: e.scalar_tensor_tensor(
                                        out=OIN[0:n, h, :], in0=pi_[0:n, s * 128:(s + 1) * 128], scalar=SEQSEL[0:n, s:s + 1],
                                        in1=OIN[0:n, h, :], op0=ALU.mult, op1=ALU.add), reads=[bpi, bW, bOIN], writes=[bOIN])
                        oin = OIN
                    if isP:
                        for g in range(2):
                            pkv, bpkv = PF.get()
                            op("pe", lambda e, g=g, pkv=pkv: e.matmul(pkv[:, 0:256], lhsT=KET[0:n, g, :], rhs=VA[0:n, g * 256:(g + 1) * 256],
                                                                     start=True, stop=True), reads=[bKET, bVA], writes=[bpkv])
                            for hh in range(2):
                                psl = slice(hh * 64, (hh + 1) * 64)
                                op("dve", lambda e, g=g, hh=hh, psl=psl, pkv=pkv: e.tensor_tensor(
                                    out=SST[psl, g, :], in0=pkv[psl, hh * 128:(hh + 1) * 128], in1=SST[psl, g, :], op=ALU.add),
                                   reads=[bpkv, bSST], writes=[bSST])
                            op("dve", lambda e, g=g: e.tensor_scalar(out=SST[:, g, :], in0=SST[:, g, :], scalar1=EB[:, g, n - 1:n],
                                                                     scalar2=None, op0=ALU.mult), reads=[bSST, bEB], writes=[bSST])
                            op("act", lambda e, g=g: e.copy(out=SSTB[:, g, :], in_=SST[:, g, :]), reads=[bSST], writes=[bSST])
                    else:
                        for s in range(NS):
                            for g in range(2):
                                op("dve", lambda e, s=s, g=g: e.tensor_scalar(out=KETM[0:n, :], in0=KET[0:n, g, :], scalar1=SEQSEL[0:n, s:s + 1],
                                                                              scalar2=None, op0=ALU.mult), reads=[bKET, bW], writes=[bKETM])
                                pkv, bpkv = PF.get()
                                op("pe", lambda e, g=g, pkv=pkv: e.matmul(pkv[:, 0:256], lhsT=KETM[0:n, :], rhs=VA[0:n, g * 256:(g + 1) * 256],
                                                                         start=True, stop=True), reads=[bKETM, bVA], writes=[bpkv])
                                for hh in range(2):
                                    psl = slice(hh * 64, (hh + 1) * 64)
                                    op("dve", lambda e, g=g, s=s, hh=hh, psl=psl, pkv=pkv: e.tensor_tensor(
                                        out=SSS[psl, g, s, :], in0=pkv[psl, hh * 128:(hh + 1) * 128], in1=SSS[psl, g, s, :], op=ALU.add),
                                       reads=[bpkv, bSSS], writes=[bSSS])
                                op("dve", lambda e, g=g, s=s: e.tensor_scalar(out=SSS[:, g, s, :], in0=SSS[:, g, s, :],
                                                                              scalar1=EB[:, g, s * TS + TS - 1:s * TS + TS], scalar2=None,
                                                                              op0=ALU.mult), reads=[bSSS, bEB], writes=[bSSS])
                    chk("glast")
                    if oin is not None:
                        osrc = lambda h: OIN[0:n, h, :]
                        bosrc = bOIN
                    else:
                        osrc = lambda h: po[0:n, h * 128:(h + 1) * 128]
                        bosrc = bpo
                    for h in range(4):
                        op("act", lambda e, h=h: e.activation(out=SQ[0:n, h * 128:(h + 1) * 128], in_=osrc(h), func=AF.Square,
                                                              accum_out=ST[0:n, 8 + h:9 + h]), reads=[bosrc], writes=[bSQ, bST])
                    rstd_from_ss(n, ST[0:n, 8:12], bST, ST[0:n, 12:16], bST, 1.0 / 128)
                    for h in range(4):
                        op("dve", lambda e, h=h: e.scalar_tensor_tensor(out=MIXC[0:n, h * 128:(h + 1) * 128], in0=osrc(h),
                                                                        scalar=ST[0:n, 12 + h:13 + h], in1=GSR[0:n, h * 128:(h + 1) * 128],
                                                                        op0=ALU.mult, op1=ALU.mult), reads=[bosrc, bST, bGSR], writes=[bMIXC])
                    chk("gla")
                    yield
                    tp, btp = PB.get()

                    def tpq(e):
                        for c in range(3):
                            r = e.transpose(out=tp[:, c * 128:c * 128 + n], in_=CQN[0:n, c * 128:(c + 1) * 128], identity=IDB[0:n, 0:n])
                        return r
                    op("pe", tpq, reads=[bCQN, bID], writes=[btp])
                    op("act", lambda e: e.copy(out=CQT[:, :, 0:n], in_=tp[:, 0:384].rearrange("p (c k) -> p c k", k=128)[:, :, 0:n]),
                       reads=[btp], writes=[bCQT])
                    pq, bpq = PF.get()

                    def mqn(e):
                        for h in range(4):
                            for c in range(3):
                                r = e.matmul(pq[:, h * 128:h * 128 + n], lhsT=WUQ[:, c, h, 0:128], rhs=CQT[:, c, 0:n], start=(c == 0), stop=(c == 2))
                        return r
                    op("pe", mqn, reads=[bW, bCQT], writes=[bpq])
                    op("act", lambda e: e.copy(out=QNT[:, :, 0:n], in_=pq[:, :].rearrange("p (h k) -> p h k", k=128)[:, :, 0:n]),
                       reads=[bpq], writes=[bQNT])
                    pr, bpr = PF.get()

                    def mqr(e):
                        for c in range(3):
                            r = e.matmul(pr[0:n, 0:256].rearrange("p (h r) -> p h r", r=64), lhsT=CQT[:, c, 0:n], rhs=WUQ[:, c, :, 128:192],
                                         start=(c == 0), stop=(c == 2))
                        return r
                    op("pe", mqr, reads=[bW, bCQT], writes=[bpr])
                    pr3 = pr[0:n, 0:256].rearrange("p (h r) -> p h r", r=64)
                    cos4 = COS[0:n, :].rearrange("p (h r) -> p h r", r=32)
                    sin4 = SIN[0:n, :].rearrange("p (h r) -> p h r", r=32)
                    op("dve", lambda e: e.tensor_tensor(out=RT[0:n, 0, :, :], in0=pr3[:, :, 0:32], in1=cos4, op=ALU.mult), reads=[bpr, bCS], writes=[bRT])
                    op("dve", lambda e: e.tensor_tensor(out=RT[0:n, 1, :, :], in0=pr3[:, :, 32:64], in1=sin4, op=ALU.mult), reads=[bpr, bCS], writes=[bRT])
                    op("dve", lambda e: e.tensor_tensor(out=RT[0:n, 2, :, :], in0=pr3[:, :, 0:32], in1=sin4, op=ALU.mult), reads=[bpr, bCS], writes=[bRT])
                    op("dve", lambda e: e.tensor_tensor(out=RT[0:n, 3, :, :], in0=pr3[:, :, 32:64], in1=cos4, op=ALU.mult), reads=[bpr, bCS], writes=[bRT])
                    op("dve", lambda e: e.tensor_tensor(out=QRR[0:n, :, 0:32], in0=RT[0:n, 0, :, :], in1=RT[0:n, 1, :, :], op=ALU.subtract),
                       reads=[bRT], writes=[bQRR])
                    op("dve", lambda e: e.tensor_tensor(out=QRR[0:n, :, 32:64], in0=RT[0:n, 2, :, :], in1=RT[0:n, 3, :, :], op=ALU.add),
                       reads=[bRT], writes=[bQRR])
                    tp, btp = PB.get()

                    def tpr(e):
                        for h in range(4):
                            r = e.transpose(out=tp[0:64, h * 128:h * 128 + n], in_=QRR[0:n, h, :], identity=IDB[0:n, 0:n])
                        return r
                    op("pe", tpr, reads=[bQRR, bID], writes=[btp])
                    op("act", lambda e: e.copy(out=QRT[:, :, 0:n], in_=tp[0:64, 0:512].rearrange("p (h k) -> p h k", k=128)[:, :, 0:n]),
                       reads=[btp], writes=[bQRT])
                    for c2 in range(2):
                        pl, bpl = PF.get()

                        def mql(e, c2=c2, pl=pl):
                            for h in range(4):
                                r = e.matmul(pl[:, h * 128:h * 128 + n], lhsT=WUKT[:, h, c2 * 128:(c2 + 1) * 128], rhs=QNT[:, h, 0:n],
                                             start=True, stop=True)
                            return r
                        op("pe", mql, reads=[bW, bQNT], writes=[bpl])
                        op("dve", lambda e, c2=c2, pl=pl: e.tensor_copy(out=QLT[:, c2, :, 0:n],
                                                                        in_=pl[:, :].rearrange("p (h k) -> p h k", k=128)[:, :, 0:n]),
                           reads=[bpl], writes=[bQLT])
                    chk("mlaq")
                    if not isP:
                        for s in range(NS):
                            for c2 in range(2):
                                op("pool", lambda e, s=s, c2=c2: e.tensor_copy(
                                    out=QLS[:, s, c2, :].rearrange("p (h t) -> p h t", t=TS), in_=QLT[:, c2, :, s * TS:(s + 1) * TS]),
                                   reads=[bQLT], writes=[bQS])
                            op("pool", lambda e, s=s: e.tensor_copy(out=QRS[:, s, :].rearrange("p (h t) -> p h t", t=TS),
                                                                   in_=QRT[:, :, s * TS:(s + 1) * TS]), reads=[bQRT], writes=[bQS])
                    yield

                def block_attn(i, n, ON, bON):
                    MT, bMT = MODP, bMODP
                    nkeys = (i + 1) * 128
                    items = []
                    k0 = 0
                    while k0 < nkeys:
                        nk = min(512, nkeys - k0)
                        for h in range(4):
                            def mk(h=h, k0=k0, nk=nk):
                                blks = list(range(k0 // 128, (k0 + nk) // 128))
                                kb = [bKVS[j] for j in blks]
                                qparts = [(QLT[:, 0, h, 0:n], bQLT), (QLT[:, 1, h, 0:n], bQLT), (QRT[:, h, 0:n], bQRT)]
                                kparts = [(CKVT[:, 0, k0:k0 + nk], kb[0]), (CKVT[:, 1, k0:k0 + nk], kb[-1]), (KRT[:, k0:k0 + nk], kb[-1])]
                                vbl = [(VTOK[:, j, :], 128, bKVS[j]) for j in blks]
                                mask = None
                                if k0 + nk == nkeys:
                                    mask = (IDB[:, :], AMP[:, :], nk - 128, [bID, bW] + kb)
                                return flash_qk(n, h, qparts, kparts, nk, vbl, mask, k0 == 0, extra=kb, FS=FSP)
                            items.append(mk)
                        k0 += nk
                    yield from run_pipelined(items)
                    for h in range(4):
                        flash_final(n, h, ON[0:n, h, :], bON, FSP)
                    yield

                def block_tail(i, isP, n, xt, bx, MIXC, bMIXC, ON, bON):
                    MT, bMT = (MODP, bMODP) if isP else (MODS, bMODS)
                    chk("attn")
                    for c2 in range(2):
                        tp, btp = PB.get()
                        if isP:
                            def tpo(e, c2=c2, tp=tp):
                                for h in range(4):
                                    r = e.transpose(out=tp[:, h * 128:h * 128 + n], in_=ON[0:n, h, c2 * 128:(c2 + 1) * 128], identity=IDB[0:n, 0:n])
                                return r
                            op("pe", tpo, reads=[bON, bID], writes=[btp])
                            op("act", lambda e, c2=c2, tp=tp: e.copy(out=OT[:, c2, :, 0:n], in_=tp[:, 0:512].rearrange("p (h k) -> p h k", k=128)[:, :, 0:n]),
                               reads=[btp], writes=[bOT])
                        else:
                            def tpo(e, c2=c2, tp=tp):
                                for s in range(NS):
                                    r = e.transpose(out=tp[:, s * 32:(s + 1) * 32], in_=ON[0:NTS, s, c2 * 128:(c2 + 1) * 128], identity=IDB[0:NTS, 0:NTS])
                                return r
                            op("pe", tpo, reads=[bON, bID], writes=[btp])
                            for s in range(NS):
                                op("act", lambda e, c2=c2, tp=tp, s=s: e.copy(out=OT[:, c2, :, s * TS:(s + 1) * TS],
                                                                             in_=tp[:, s * 32:(s + 1) * 32].rearrange("p (h t) -> p h t", t=TS)),
                                   reads=[btp], writes=[bOT])
                    pob, bpob = PF.get()

                    def mob(e):
                        for h in range(4):
                            for c2 in range(2):
                                r = e.matmul(pob[0:n, h * 128:(h + 1) * 128], lhsT=OT[:, c2, h, 0:n], rhs=WUV[:, c2, h * 128:(h + 1) * 128],
                                             start=(c2 == 0), stop=(c2 == 1))
                        return r
                    op("pe", mob, reads=[bOT, bW], writes=[bpob])
                    for h in range(4):
                        op("act", lambda e, h=h: e.activation(out=SQ[0:n, h * 128:(h + 1) * 128], in_=pob[0:n, h * 128:(h + 1) * 128],
                                                              func=AF.Square, accum_out=ST[0:n, 8 + h:9 + h]), reads=[bpob], writes=[bSQ, bST])
                    rstd_from_ss(n, ST[0:n, 8:12], bST, ST[0:n, 12:16], bST, 1.0 / 128)
                    for h in range(4):
                        op("dve", lambda e, h=h: e.scalar_tensor_tensor(out=MIXC[0:n, 512 + h * 128:512 + (h + 1) * 128],
                                                                        in0=pob[0:n, h * 128:(h + 1) * 128], scalar=ST[0:n, 12 + h:13 + h],
                                                                        in1=GMO[0:n, :], op0=ALU.mult, op1=ALU.mult),
                           reads=[bpob, bST, bW], writes=[bMIXC])
                    chk("oside")
                    yield
                    if dbg and not isP:
                        dma("sp", dbg_mixc[:, :], MIXC[0:n, :], reads=[bMIXC])
                    tp, btp = PB.get()

                    def tpm(e):
                        for k in range(8):
                            r = e.transpose(out=tp[:, k * 128:k * 128 + n], in_=MIXC[0:n, k * 128:(k + 1) * 128], identity=IDB[0:n, 0:n])
                        return r
                    op("pe", tpm, reads=[bMIXC, bID], writes=[btp])
                    op("act", lambda e: e.copy(out=MIXT[:, :, 0:n], in_=tp[:, :].rearrange("p (k c) -> p k c", c=128)[:, :, 0:n]),
                       reads=[btp], writes=[bMIXT])
                    pm2 = []
                    for hf in range(2):
                        pm, bpm = PF.get()

                        def mmx(e, hf=hf, pm=pm):
                            for k in range(8):
                                r = e.matmul(pm[0:n, :], lhsT=MIXT[:, k, 0:n], rhs=WO[:, k, hf * 512:(hf + 1) * 512], start=(k == 0), stop=(k == 7))
                            return r
                        op("pe", mmx, reads=[bMIXT, bW], writes=[bpm])
                        op("act", lambda e, hf=hf, pm=pm: e.activation(out=SQ[0:n, hf * 512:(hf + 1) * 512], in_=pm[0:n, :], func=AF.Square,
                                                                       accum_out=ST[0:n, 6 + hf:7 + hf]), reads=[bpm], writes=[bSQ, bST])
                        pm2.append((pm, bpm))
                    op("dve", lambda e: e.tensor_tensor(out=ST[0:n, 6:7], in0=ST[0:n, 6:7], in1=ST[0:n, 7:8], op=ALU.add), reads=[bST], writes=[bST])
                    rstd_from_ss(n, ST[0:n, 6:7], bST, ST[0:n, 7:8], bST, 1.0 / D)
                    x1t, bx1 = X1[i % 2], bX1[i % 2]
                    for hf in range(2):
                        pm, bpm = pm2[hf]
                        cs = slice(hf * 512, (hf + 1) * 512)
                        op("dve", lambda e, pm=pm, cs=cs: e.scalar_tensor_tensor(out=TMP[0:n, :], in0=pm[0:n, :], scalar=ST[0:n, 7:8],
                                                                                 in1=MT[0:n, 0, cs], op0=ALU.mult, op1=ALU.mult),
                           reads=[bpm, bST, bMT], writes=[bTMP])
                        op("dve", lambda e, cs=cs: e.tensor_tensor(out=x1t[0:n, cs], in0=TMP[0:n, :], in1=xt[0:n, cs], op=ALU.add),
                           reads=[bTMP, bx], writes=[bx1])
                    r0 = i * 128
                    dma("sp", x1s[r0:r0 + n, :], x1t[0:n, :], reads=[bx1])


                gstate = {"next": 0}

                def issue_gather(gidx):
                    s_, u = divmod(gidx, NU)
                    slot = gidx % 3
                    col = s_ * NU + u
                    dma("pool", LATG[slot][:].rearrange("p j f -> p (j f)"), lat_rows, reads=[bIDX], writes=[bG[slot]],
                        indirect=bass.IndirectOffsetOnAxis(ap=IDX[:, col:col + 1], axis=0))
                    dma("pool", KRG[slot][:].rearrange("p j f -> p (j f)"), kr_rows, reads=[bIDX], writes=[bG[slot]],
                        indirect=bass.IndirectOffsetOnAxis(ap=IDX[:, col:col + 1], axis=0))

                def prefetch_gathers(upto):
                    while gstate["next"] < min(upto, NS * NU):
                        issue_gather(gstate["next"])
                        gstate["next"] += 1

                def sample_attention():
                    n = NTS
                    kcount = [0]
                    for s in range(NS):
                        qparts = [(QLS[:, s, 0, :], bQS), (QLS[:, s, 1, :], bQS), (QRS[:, s, :], bQS)]
                        items = []
                        for u in range(NU):
                            for half in range(2):
                                def mk(s=s, u=u, half=half, qparts=qparts):
                                    gidx = s * NU + u
                                    slot = gidx % 3
                                    if half == 0:
                                        prefetch_gathers(gidx + 3)
                                    ki = kcount[0] % 2
                                    kcount[0] += 1
                                    kts, krts, bk = KTS[ki], KRTS[ki], bKTS[ki]
                                    tpa, btpa = PBS.get()

                                    def tl(e):
                                        for jj in range(4):
                                            j = half * 4 + jj
                                            for c2 in range(2):
                                                r = e.transpose(out=tpa[:, c2 * 512 + jj * 128:c2 * 512 + (jj + 1) * 128],
                                                                in_=LATG[slot][:, j, c2 * 128:(c2 + 1) * 128], identity=IDB[:, :])
                                        return r
                                    op("pe", tl, reads=[bG[slot], bID], writes=[btpa])
                                    op("dve", lambda e: e.tensor_copy(out=kts[:].rearrange("p c k -> p (c k)"), in_=tpa[:, :]),
                                       reads=[btpa], writes=[bk])
                                    tpb, btpb = PBS.get()

                                    def tr(e):
                                        for jj in range(4):
                                            j = half * 4 + jj
                                            r = e.transpose(out=tpb[0:64, jj * 128:(jj + 1) * 128], in_=KRG[slot][:, j, :], identity=IDB[:, :])
                                        return r
                                    op("pe", tr, reads=[bG[slot], bID], writes=[btpb])
                                    op("act", lambda e: e.copy(out=krts[:, :], in_=tpb[0:64, 0:512]), reads=[btpb], writes=[bk])
                                    kparts = [(kts[:, 0, :], bk), (kts[:, 1, :], bk), (krts[:, :], bk)]
                                    vbl = [(LATG[slot][:, half * 4 + jj, :], 128, bG[slot]) for jj in range(4)]
                                    return flash_qk(n, s, qparts, kparts, 512, vbl, None, u == 0 and half == 0, FS=FSS)
                                items.append(mk)

                        def mk_new(s=s, qparts=qparts):
                            kparts = [(CKVT[:, 0, T:T + NTS], bKVS[NB]), (CKVT[:, 1, T:T + NTS], bKVS[NB]), (KRT[:, T:T + NTS], bKVS[NB])]
                            vbl = [(VTOK[0:NTS, NB, :], NTS, bKVS[NB])]
                            mask = (IDB[0:NTS, 0:NTS], AMS[0:NTS, s * NTS:(s + 1) * NTS], 0, [bID, bW])
                            return flash_qk(n, s, qparts, kparts, NTS, vbl, mask, False, FS=FSS)
                        items.append(mk_new)
                        yield from run_pipelined(items, lookahead=False)
                        flash_final(n, s, ONS[0:NTS, s, :], bONS, FSS)
                        yield

                prefetch_gathers(3)
                load_x(NB)
                for _ in block(NB, "A"):
                    pass
                def chain_p():
                    load_x(0)
                    for i in range(NB):
                        if i + 1 < NB:
                            load_x(i + 1)
                        for _ in block(i):
                            pass

                def chain_s():
                    for _ in sample_attention():
                        pass
                tk.sched = Sched([2, 1])
                tk.sched.run([chain_p, chain_s])
                tk.sched = None
                for _ in block(NB, "tail"):
                    pass
                dma("sp", gla_p.rearrange("g p v -> p g v"), SST[:], reads=[bSST])
                for s_ in range(NS):
                    for g_ in range(2):
                        dma("sp", gla_s[s_, g_, :, :], SSS[:, g_, s_, :], reads=[bSSS])
                tk.barrier()
                if stop in ("p1", "p1sample"):
                    raise _Stop(nc)
        with ExitStack() as e2:
            WUP = sb(e2, [128, 8, 2 * DFF], BF16)
            WDN = sb(e2, [128, 22, D], BF16)
            WCV = sb(e2, [128, 44, 4], F32)
            bWUP = [MBuf("wup%d" % i) for i in range(4)]
            bWDN = [Buf("wdn%d" % i) for i in range(22)]
            bWCV = Buf("wcv")
            dma("sp", WCV[:], wconv[:], writes=[bWCV])
            for q4 in (0, 2, 1, 3):
                for k in range(8):
                    dma("pool", WUP[:, k, q4 * 1408:(q4 + 1) * 1408], w_up[k * 128:(k + 1) * 128, q4 * 1408:(q4 + 1) * 1408], writes=[bWUP[q4].new()])
                if q4 == 2:
                    for f in range(6):
                        dma("pool", WDN[:, f, :], w_down[f * 128:(f + 1) * 128, :], writes=[bWDN[f]])
            for f in range(6, 22):
                dma("pool", WDN[:, f, :], w_down[f * 128:(f + 1) * 128, :], writes=[bWDN[f]])
            PF2 = [ps(e2, [128, 512], F32) for _ in range(8)]
            PY = Pool_(PF2[0:4], "py")
            PU = Pool_(PF2[4:8], "pu")
            L = 256
            XG2 = [sb(e2, [128, 2, D], F32) for _ in range(2)]
            bXG2 = [[Buf("xg%d_%d" % (i, j)) for j in range(2)] for i in range(2)]
            ST2 = sb(e2, [128, 8], F32)
            bST2 = Buf("st2")
            TH = sb(e2, [128, D], F32)
            bTH = Buf("th")
            H2 = sb(e2, [128, D], BF16)
            bH2 = Buf("h2")
            H2T2 = [sb(e2, [128, 8, L], BF16) for _ in range(2)]
            bH2T2 = [Buf("h2t0"), Buf("h2t1")]
            NBUF = 4
            UU = [sb(e2, [128, 2, 2 + L], F32) for _ in range(NBUF)]
            bUU = [Buf("uu%d" % i) for i in range(NBUF)]
            bUUh = [Buf("uuh%d" % i) for i in range(NBUF)]
            CC = [sb(e2, [128, 2, L], F32) for _ in range(NBUF)]
            bCa = [Buf("ca%d" % i) for i in range(NBUF)]
            bCg = [Buf("cg%d" % i) for i in range(NBUF)]
            ACTT = [sb(e2, [128, L], BF16) for _ in range(NBUF)]
            bACTT = [Buf("at%d" % i) for i in range(NBUF)]
            HALO = sb(e2, [128, 22, 2, 2], F32)
            bHALO = [Buf("halo%d" % i) for i in range(22)]
            HALS = sb(e2, [128, 22, 2, 8], F32)
            bHALS = [Buf("hals%d" % i) for i in range(22)]
            CB = sb(e2, [8, 512], F32)
            bCB = Buf("cb")
            op("pool", lambda e: e.memset(HALO[:], 0.0), writes=bHALO)
            for q in range(11):
                dma("sp", CB[:, :], sconv[:, q * 512:(q + 1) * 512], writes=[bCB])
                pu, bpu = PU.get()

                def tph_(e, pu=pu):
                    for j in range(4):
                        r = e.transpose(out=pu[:, j * 8:(j + 1) * 8], in_=CB[0:8, j * 128:(j + 1) * 128], identity=IDF[0:8, 0:8])
                    return r
                op("pe", tph_, reads=[bCB, bID], writes=[bpu])
                for j in range(4):
                    fo = q * 4 + j
                    op("act", lambda e, j=j, fo=fo, pu=pu: e.copy(out=HALS[:, fo % 22, fo // 22, :], in_=pu[:, j * 8:(j + 1) * 8]),
                       reads=[bpu], writes=[bHALS[fo % 22]])

            def ffn_group(gi, part, pre=None, mid=None, late=None):
                isP = gi < 8
                XG, bXG, H2T, bH2T = XG2[gi % 2], bXG2[gi % 2], H2T2[gi % 2], bH2T2[gi % 2]
                nblk = 2 if isP else 1
                n = 128 if isP else NTS
                Lg = 256 if isP else NTS
                nseg = 1 if isP else NS
                Ls = Lg // nseg
                MT, bMT = (MODP, bMODP) if isP else (MODS, bMODS)
                r0 = gi * 256
                for b in (range(nblk) if part == "pro1" else ()):
                    dma("sp", XG[0:n, b, :], x1s[r0 + b * 128:r0 + b * 128 + n, :], writes=[bXG[b]])
                    op("act", lambda e, b=b: e.activation(out=TH[0:n, :], in_=XG[0:n, b, :], func=AF.Square, accum_out=ST2[0:n, 0:1]),
                       reads=[bXG[b]], writes=[bTH, bST2])
                    rstd_from_ss(n, ST2[0:n, 0:1], bST2, ST2[0:n, 4 + b:5 + b], bST2, 1.0 / D)
                for b in (range(nblk) if part == "pro2" else ()):
                    op("dve", lambda e, b=b: e.tensor_scalar(out=H2[0:n, :], in0=XG[0:n, b, :], scalar1=ST2[0:n, 4 + b:5 + b], scalar2=None, op0=ALU.mult),
                       reads=[bXG[b], bST2], writes=[bH2])
                    puf, bpu = PU.get()
                    pu = puf[:, :].bitcast(BF16)

                    def tp2(e, pu=pu):
                        for k in range(8):
                            r = e.transpose(out=pu[:, k * 128:k * 128 + n], in_=H2[0:n, k * 128:(k + 1) * 128], identity=IDB[0:n, 0:n])
                        return r
                    op("pe", tp2, reads=[bH2, bID], writes=[bpu])
                    for k in range(8):
                        if isP:
                            op("act", lambda e, k=k, b=b, pu=pu: e.activation(
                                out=H2T[:, k, b * 128:b * 128 + n], in_=pu[:, k * 128:k * 128 + n],
                                func=AF.Identity, scale=ABT[:, 2, k, 0:1], bias=ABT[:, 3, k, 0:1]),
                               reads=[bpu, bABT], writes=[bH2T])
                        else:
                            for s_ in range(NS):
                                op("act", lambda e, k=k, s_=s_, pu=pu: e.activation(
                                    out=H2T[:, k, s_ * TS:(s_ + 1) * TS], in_=pu[:, k * 128 + s_ * TS:k * 128 + (s_ + 1) * TS], func=AF.Identity,
                                    scale=ABT[:, 2, k, 1 + s_:2 + s_], bias=ABT[:, 3, k, 1 + s_:2 + s_]), reads=[bpu, bABT], writes=[bH2T])
                if part in ("pro1", "pro2"):
                    return
                ybanks = [PY.tiles[j] for j in range(nblk * 2)], [PY.bufs[j] for j in range(nblk * 2)]
                ybanks = list(zip(*ybanks))
                if part == "epi":
                    epilogue(n, nblk, ybanks, MT, bMT, XG, bXG, isP, r0)
                    return

                def up(f):
                    pu, bpu = PU.get()

                    def mup(e, f=f, pu=pu):
                        for ag in range(2):
                            fo = ag * 22 + f
                            for k in range(8):
                                r = e.matmul(pu[:, ag * 256:ag * 256 + Lg], lhsT=WUP[:, k, fo * 128:(fo + 1) * 128], rhs=H2T[:, k, 0:Lg],
                                             start=(k == 0), stop=(k == 7))
                        return r
                    op("pe", mup, reads=[bWUP[f * 128 // 1408], bWUP[(22 + f) * 128 // 1408], bH2T], writes=[bpu])
                    return pu, bpu

                def ew1(f, bank):
                    pu, bpu = bank
                    bi = f % NBUF
                    U, Cc = UU[bi], CC[bi]
                    bUh, bUm = bUUh[bi], bUU[bi]
                    U4 = U[:, :, 0:nseg * (2 + Ls)].rearrange("p a (s c) -> p a s c", c=2 + Ls)
                    pu4 = pu[:, :].rearrange("p (a c) -> p a c", a=2)[:, :, 0:Lg].rearrange("p a (s c) -> p a s c", c=Ls)
                    halo = (HALO[:, f, :, :].rearrange("p a (s c) -> p a s c", c=2) if isP
                            else HALS[:, f, :, :].rearrange("p a (s c) -> p a s c", c=2))
                    bh = bHALO[f] if isP else bHALS[f]
                    op("dve", lambda e: e.tensor_copy(out=U4[:, :, :, 0:2], in_=halo), reads=[bh], writes=[bUh])
                    op("act", lambda e: e.copy(out=U4[:, :, :, 2:2 + Ls], in_=pu4), reads=[bpu], writes=[bUm])
                    op("dve", lambda e: e.tensor_copy(out=halo, in_=U4[:, :, :, Ls:Ls + 2]), reads=[bUm], writes=[bh])
                    for ag, bC in ((1, bCg[bi]), (0, bCa[bi])):
                        fo = ag * 22 + f
                        op("act", lambda e, ag=ag, fo=fo: e.activation(out=Cc[:, ag, 0:Lg], in_=pu[:, ag * 256:ag * 256 + Lg], func=AF.Identity,
                                                                       scale=WCV[:, fo, 2:3], bias=WCV[:, fo, 3:4]),
                           reads=[bpu, bWCV], writes=[bC])
                    for ag, bC in ((1, bCg[bi]), (0, bCa[bi])):
                        fo = ag * 22 + f
                        C3 = Cc[:, ag, 0:Lg].rearrange("p (s c) -> p s c", c=Ls)
                        for tap in (1, 0):
                            op("dve", lambda e, ag=ag, fo=fo, C3=C3, tap=tap: e.scalar_tensor_tensor(
                                out=C3, in0=U4[:, ag, :, tap:tap + Ls], scalar=WCV[:, fo, tap:tap + 1], in1=C3, op0=ALU.mult, op1=ALU.add),
                               reads=[bUh, bUm, bWCV, bC], writes=[bC])

                def ew2(f):
                    bi = f % NBUF
                    Cc = CC[bi]
                    op("act", lambda e: e.activation(out=Cc[:, 1, 0:Lg], in_=Cc[:, 1, 0:Lg], func=AF.Gelu_apprx_tanh),
                       reads=[bCg[bi]], writes=[bCg[bi]])
                    op("pool", lambda e: e.tensor_tensor(out=ACTT[bi][:, 0:Lg], in0=Cc[:, 0, 0:Lg], in1=Cc[:, 1, 0:Lg], op=ALU.mult),
                       reads=[bCa[bi], bCg[bi]], writes=[bACTT[bi]])

                def down(f):
                    bi = f % NBUF
                    for b in range(nblk):
                        for hf in range(2):
                            py, bpy = ybanks[b * 2 + hf]
                            op("pe", lambda e, py=py, b=b, hf=hf, bi=bi, f=f: e.matmul(
                                py[0:n, :], lhsT=ACTT[bi][:, b * 128:b * 128 + n], rhs=WDN[:, f, hf * 512:(hf + 1) * 512],
                                start=(f == 0), stop=(f == 21)), reads=[bACTT[bi], bWDN[f]], writes=[bpy])

                LOOK = 3
                pend = [up(f) for f in range(min(LOOK, 22))]
                if pre is not None:
                    pre()
                for f in range(22):
                    if f == 8 and mid is not None:
                        mid()
                    cur = pend.pop(0)
                    if f + LOOK < 22:
                        pend.append(up(f + LOOK))
                    if f >= 1:
                        ew2(f - 1)
                    ew1(f, cur)
                    if f == 19 and late is not None:
                        late()
                    if f >= 2:
                        down(f - 2)
                ew2(21)
                down(20)
                down(21)

            def epilogue(n, nblk, ybanks, MT, bMT, XG, bXG, isP, r0):
                for b in range(nblk):
                    for hf in range(2):
                        py, bpy = ybanks[b * 2 + hf]
                        op("act", lambda e, py=py, hf=hf: e.activation(out=H2[0:n, hf * 512:(hf + 1) * 512], in_=py[0:n, :], func=AF.Square,
                                                                       accum_out=ST2[0:n, 2 + hf:3 + hf]), reads=[bpy], writes=[bH2, bST2])
                    op("dve", lambda e: e.tensor_tensor(out=ST2[0:n, 2:3], in0=ST2[0:n, 2:3], in1=ST2[0:n, 3:4], op=ALU.add), reads=[bST2], writes=[bST2])
                    rstd_from_ss(n, ST2[0:n, 2:3], bST2, ST2[0:n, 3:4], bST2, 1.0 / D)
                    for hf in range(2):
                        py, bpy = ybanks[b * 2 + hf]
                        cs = slice(hf * 512, (hf + 1) * 512)
                        op("dve", lambda e, py=py, cs=cs: e.scalar_tensor_tensor(out=TH[0:n, cs], in0=py[0:n, :], scalar=ST2[0:n, 3:4],
                                                                                 in1=MT[0:n, 1, cs], op0=ALU.mult, op1=ALU.mult),
                           reads=[bpy, bST2, bMT], writes=[bTH])
                    op("pool", lambda e, b=b: e.tensor_tensor(out=XG[0:n, b, :], in0=TH[0:n, :], in1=XG[0:n, b, :], op=ALU.add),
                       reads=[bTH, bXG[b]], writes=[bXG[b]])
                    if isP:
                        dma("sp", y_p[r0 + b * 128:r0 + (b + 1) * 128, :], XG[:, b, :], reads=[bXG[b]])
                    else:
                        dma("sp", y_s[:, :], XG[0:n, b, :], reads=[bXG[b]])

            ffn_group(0, "pro1")
            ffn_group(0, "pro2")
            for gi in range(9):
                ffn_group(gi, "loop",
                          pre=(lambda gi=gi: ffn_group(gi - 1, "epi")) if gi > 0 else None,
                          mid=(lambda gi=gi: ffn_group(gi + 1, "pro1")) if gi + 1 < 9 else None,
                          late=(lambda gi=gi: ffn_group(gi + 1, "pro2")) if gi + 1 < 9 else None)
            ffn_group(8, "epi")
            for (HL, bHLs, ncol, dst) in ((HALO, bHALO, 2, conv_p), (HALS, bHALS, 8, conv_s)):
                for q in range(11):
                    pu, bpu = PU.get()

                    def tpc(e, q=q, pu=pu, HL=HL, ncol=ncol):
                        for j in range(4):
                            fo = q * 4 + j
                            r = e.transpose(out=pu[0:ncol, j * 128:(j + 1) * 128], in_=HL[:, fo % 22, fo // 22, :], identity=IDF[:, :])
                        return r
                    op("pe", tpc, reads=[bHLs[(q * 4 + j) % 22] for j in range(4)] + [bID], writes=[bpu])
                    op("act", lambda e, pu=pu, ncol=ncol: e.copy(out=CB[0:ncol, :], in_=pu[0:ncol, :]), reads=[bpu], writes=[bCB])
                    dma("sp", dst[:, q * 512:(q + 1) * 512], CB[0:ncol, :], reads=[bCB])
            tk.barrier()
    return nc


_NC_CACHE = {}


def _get_nc():
    if "nc" not in _NC_CACHE:
        _NC_CACHE["nc"] = build_nc()
    return _NC_CACHE["nc"]


def _consts():
    inv = 10000.0 ** (-np.arange(0, 64, 2, dtype=np.float32) / np.float32(64))
    inv = inv.astype(np.float32)

    def tabs(pos):
        ang = pos.astype(np.float32)[:, None] * inv[None, :]
        c = np.cos(ang).astype(np.float32)
        s = np.sin(ang).astype(np.float32)
        return np.tile(c, (1, 4)), np.tile(s, (1, 4))
    cos_p, sin_p = tabs(np.arange(T))
    cs, ss = tabs(PAST + np.arange(TS))
    cos_s = np.tile(cs, (NS, 1))
    sin_s = np.tile(ss, (NS, 1))
    ii = np.arange(128)
    gmask_p = (ii[:, None] <= ii[None, :]).astype(np.float32)
    amask_p = np.where(ii[None, :] > ii[:, None], NEG, 0.0).astype(np.float32)
    seq = np.arange(NTS) // TS
    tt = np.arange(NTS) % TS
    gmask_s = ((seq[:, None] == seq[None, :]) & (tt[:, None] <= tt[None, :])).astype(np.float32)
    qt = np.arange(NTS) % TS
    am = np.full((NTS, NS, NTS), NEG, np.float32)
    for s in range(NS):
        ok = (seq[None, :] == s) & (tt[None, :] <= qt[:, None])
        am[:, s, :] = np.where(ok, 0.0, NEG)
    sel_p = np.zeros((5, 128), np.float32)
    sel_p[0, :] = 1.0
    sel_s = np.zeros((5, NTS), np.float32)
    for s in range(NS):
        sel_s[1 + s, s * TS:(s + 1) * TS] = 1.0
    seqsel = (seq[:, None] == np.arange(NS)[None, :]).astype(np.float32)
    lo16 = (np.arange(128) % 16).astype(np.float32).reshape(128, 1)
    return dict(ident_f=np.eye(128, dtype=np.float32), cos_p=cos_p, sin_p=sin_p, cos_s=cos_s, sin_s=sin_s,
                gmask_p=gmask_p, gmask_s=gmask_s, amask_p=amask_p, amask_s=am.reshape(NTS, NS * NTS),
                sel_p=sel_p, sel_s=sel_s, seqsel=seqsel, lo16=lo16)


def _prep(x_prompt, x_sample, c_prompt, c_sample, cache_latent, cache_krope, state_gla, state_conv,
          page_table, w_ada, b_ada, g_pre_mix, g_post_mix, g_pre_ffn, g_post_ffn, w_in, w_gla_a2,
          b_gla_a, g_gla_out, g_mla_q, g_mla_kv, w_mla_uq, w_mla_uk, w_mla_uv, g_mla_out, w_o,
          w_ffn_up, w_ffn_conv, b_ffn_conv, w_ffn_down):
    f = lambda a: np.ascontiguousarray(np.asarray(a, dtype=np.float32))
    cst = _consts()
    lat_rows = f(cache_latent).reshape(NPHYS * 16, 8 * 256)
    kr_rows = f(cache_krope).reshape(NPHYS * 16, 8 * 64)
    wconv = np.concatenate([f(w_ffn_conv)[0], f(b_ffn_conv)], axis=0)
    wconv = np.ascontiguousarray(wconv.reshape(4, 44, 128).transpose(2, 1, 0))
    shared = dict(
        lat_rows=lat_rows, kr_rows=kr_rows,
        w_ada=f(w_ada)[0], b_ada=f(b_ada), g_post_mix=f(g_post_mix),
        g_post_ffn=f(g_post_ffn), w_in=f(w_in)[0], w_a2=f(w_gla_a2)[0],
        b_a=np.ascontiguousarray(f(b_gla_a)[0].reshape(2, 128).T), g_gla_out=f(g_gla_out), g_mla_q=f(g_mla_q),
        g_mla_kv=f(g_mla_kv), w_uq=f(w_mla_uq)[0], w_uk=f(w_mla_uk)[0].reshape(256, 512),
        w_uv=f(w_mla_uv)[0].reshape(256, 512), g_mla_out=f(g_mla_out), w_o=f(w_o)[0], w_up=f(w_ffn_up)[0],
        wconv=wconv, w_down=f(w_ffn_down)[0],
        gvt=np.ascontiguousarray(np.stack([f(g)[0].reshape(8, 128).T for g in (g_pre_mix, g_post_mix, g_pre_ffn, g_post_ffn)], axis=1)),
        **cst)
    pt = np.asarray(page_table, dtype=np.int32)
    xp, xs = f(x_prompt), f(x_sample)
    cp, csm = f(c_prompt), f(c_sample)
    sg, sc = f(state_gla)[0], f(state_conv)[0]
    in_maps = []
    for c in range(8):
        sl = slice(4 * c, 4 * c + 4)
        pts = pt[sl].reshape(NS, NU, 8)
        ptb = np.repeat(pts.transpose(2, 0, 1), 16, axis=0)
        m = dict(shared)
        m.update(
            x_p=xp[c], x_s=np.ascontiguousarray(xs[sl].reshape(NTS, D)),
            c_all=np.ascontiguousarray(np.concatenate([cp[c:c + 1], csm[sl]], axis=0)),
            ptb=np.ascontiguousarray(ptb.reshape(128, NS * NU)).astype(np.int32),
            sgla=np.ascontiguousarray(sg[sl].reshape(NS, 2, 128, 128)),
            sconv=np.ascontiguousarray(sc[sl].reshape(8, 2 * DFF)))
        in_maps.append(m)
    return in_maps


def _post(R):
    n = len(R)
    y_p = np.stack([R[c]["y_p"] for c in range(n)])
    y_s = np.concatenate([R[c]["y_s"].reshape(NS, TS, D) for c in range(n)])
    lat_p = np.stack([R[c]["lat_p"] for c in range(n)])[None]
    kr_p = np.stack([R[c]["kr_p"] for c in range(n)])[None]
    gla_p = np.stack([R[c]["gla_p"].reshape(4, 64, 128) for c in range(n)])[None]
    conv_p = np.stack([R[c]["conv_p"] for c in range(n)])[None]
    lat_s = np.concatenate([R[c]["lat_s"].reshape(NS, TS, 256) for c in range(n)])[None]
    kr_s = np.concatenate([R[c]["kr_s"].reshape(NS, TS, 64) for c in range(n)])[None]
    gla_s = np.concatenate([R[c]["gla_s"].reshape(NS, 4, 64, 128) for c in range(n)])[None]
    conv_s = np.concatenate([R[c]["conv_s"].reshape(NS, 2, 2 * DFF) for c in range(n)])[None]
    outs = (y_p, y_s, lat_p, kr_p, gla_p, conv_p, lat_s, kr_s, gla_s, conv_s)
    return tuple(np.ascontiguousarray(o, dtype=np.float32) for o in outs)


def kernel(**inputs):
    in_maps = _prep(**inputs)
    nc = _get_nc()
    res = run_bass_kernel_spmd(nc, in_maps, core_ids=list(range(8)))
    return _post(res.results)
```
